# Optimizing a Trainium2 kernel written in Bass

```python
import math
import jax
import jax.numpy as jnp
from jax import lax
import numpy as np

D_MODEL = 1024
BATCH = 32
SEQ = 2048
DEPTH = 4

N_MIXERS = 3
N_SUB = 3
NORM_EPS = 1e-6
D_FF = 2816
MACARON_W = 0.5
N_A = (DEPTH + 2) // 3
N_B = (DEPTH + 1) // 3
N_C = DEPTH // 3
N_A_VRES = max(N_A - 1, 0)

RW_HEAD = 64
RW_HEADS = D_MODEL // RW_HEAD
RW_DECAY_LORA = max(32, int(round(1.8 * D_MODEL ** 0.5 / 32)) * 32)
RW_AAA_LORA = max(32, int(round(1.8 * D_MODEL ** 0.5 / 32)) * 32)
RW_MV_LORA = max(32, int(round(1.3 * D_MODEL ** 0.5 / 32)) * 32)
RW_GATE_LORA = max(32, int(round(0.6 * D_MODEL ** 0.8 / 32)) * 32)
RW_GN_EPS = 64e-5

MB_D_INNER = 2 * D_MODEL
MB_HEADDIM = 64
MB_HEADS = MB_D_INNER // MB_HEADDIM
MB_GROUPS = 8
MB_STATE = 128
MB_CONV = 4
MB_CHUNK = 128
MB_CONV_DIM = MB_D_INNER + 2 * MB_GROUPS * MB_STATE
MB_IN_DIM = MB_D_INNER + MB_CONV_DIM + MB_HEADS

SW_HEAD_DIM = 64
SW_Q_HEADS = D_MODEL // SW_HEAD_DIM
SW_KV_HEADS = 4
SW_WINDOW = 128
SW_BLOCK = 128
NEG_INF = -1e30

kernel_name = "hybrid_rwkv7_mamba2_swa_macaron_adaln"


def rms_norm(x, g, eps=NORM_EPS):
    xf = x.astype(jnp.float32)
    y = xf * lax.rsqrt(jnp.mean(xf * xf, axis=-1, keepdims=True) + eps)
    return (y * g.astype(jnp.float32)).astype(x.dtype)


def adaln(x, g, shift, scale):
    return rms_norm(x, g) * (1 + scale[:, None, :]) + shift[:, None, :]


def swiglu(h, w_in, w_out):
    a, b = jnp.split(h @ w_in, 2, axis=-1)
    return (jax.nn.silu(a) * b) @ w_out


def token_shift(h):
    return jnp.pad(h, ((0, 0), (1, 0), (0, 0)))[:, :-1]


def rwkv7_time_mix(h, mu, w_rkv, w_o, w0, w1, w2, a0, a1, a2, g1, g2,
                   k_k, k_a, r_k, ln_w, ln_b, v_first, vres):
    f32 = jnp.float32
    bsz, seq, d = h.shape
    H, N = RW_HEADS, RW_HEAD
    dx = token_shift(h) - h
    xr, xw, xk, xv, xa, xg = [h + dx * mu[j] for j in range(6)]
    r = xr @ w_rkv[0]
    k = xk @ w_rkv[1]
    v = xv @ w_rkv[2]
    w = -jax.nn.softplus(-(w0 + jnp.tanh(xw @ w1) @ w2)) - 0.5
    decay = jnp.exp(-jnp.exp(w.astype(f32)))
    a = jax.nn.sigmoid(a0 + (xa @ a1) @ a2)
    if vres is None:
        v_first = v
    else:
        v0, v1, v2 = vres
        v = v + (v_first - v) * jax.nn.sigmoid(v0 + (xv @ v1) @ v2)
    g = jax.nn.sigmoid(xg @ g1) @ g2

    def heads(t):
        return t.reshape(bsz, seq, H, N).astype(f32)

    kk = heads(k * k_k)
    kk = kk / jnp.maximum(jnp.sqrt(jnp.sum(kk * kk, axis=-1, keepdims=True)), 1e-12)
    k = k * (1 + (a - 1) * k_a)
    rh, kh, vh, wh, ah = heads(r), heads(k), heads(v), heads(decay), heads(a)

    def step(state, inp):
        r_t, w_t, k_t, v_t, kk_t, a_t = inp
        sa = jnp.einsum('bhvk,bhk->bhv', state, -kk_t)
        state = (state * w_t[:, :, None, :]
                 + sa[..., None] * (kk_t * a_t)[:, :, None, :]
                 + v_t[..., None] * k_t[:, :, None, :])
        return state, jnp.einsum('bhvk,bhk->bhv', state, r_t)

    def tm(t):
        return jnp.swapaxes(t, 0, 1)

    state0 = jnp.zeros((bsz, H, N, N), f32)
    _, y = lax.scan(step, state0, (tm(rh), tm(wh), tm(kh), tm(vh), tm(kk), tm(ah)))
    y = tm(y)
    mean = jnp.mean(y, axis=-1, keepdims=True)
    var = jnp.mean(jnp.square(y - mean), axis=-1, keepdims=True)
    y = ((y - mean) * lax.rsqrt(var + RW_GN_EPS)).reshape(bsz, seq, d) * ln_w + ln_b
    bonus = jnp.sum(rh * kh * r_k, axis=-1, keepdims=True) * vh
    out = ((y + bonus.reshape(bsz, seq, d)) * g).astype(h.dtype) @ w_o
    return out, v_first


def causal_dwconv(x, w, b):
    k_w, ch = w.shape
    out = lax.conv_general_dilated(x, w[:, None, :].astype(x.dtype), window_strides=(1,),
                                   padding=[(k_w - 1, 0)],
                                   dimension_numbers=('NWC', 'WIO', 'NWC'),
                                   feature_group_count=ch)
    return out + b


def ssd_chunked(x, dt, A, Bm, Cm):
    f32 = jnp.float32
    bsz, seq, H, P = x.shape
    G, N = Bm.shape[2], Bm.shape[3]
    hg = H // G
    L = MB_CHUNK
    nc = seq // L
    X = (x.astype(f32) * dt[..., None]).reshape(bsz, nc, L, G, hg, P)
    dA = (dt * A).reshape(bsz, nc, L, G, hg)
    Bc = Bm.astype(f32).reshape(bsz, nc, L, G, N)
    Cc = Cm.astype(f32).reshape(bsz, nc, L, G, N)
    X, dA, Bc, Cc = [jnp.moveaxis(t, 1, 0) for t in (X, dA, Bc, Cc)]
    mask = jnp.tril(jnp.ones((L, L), dtype=bool))[None, :, :, None, None]

    def step(state, inp):
        Xc, dAc, Bk, Ck = inp
        acum = jnp.cumsum(dAc, axis=1)
        seg = acum[:, :, None] - acum[:, None, :]
        lmat = jnp.exp(jnp.where(mask, seg, -jnp.inf))
        cb = jnp.einsum('blgn,bsgn->blsg', Ck, Bk)
        y_diag = jnp.einsum('blsgh,bsghp->blghp', cb[..., None] * lmat, Xc)
        y_off = jnp.einsum('blgn,bghpn->blghp', Ck, state) * jnp.exp(acum)[..., None]
        decay_end = jnp.exp(acum[:, -1:] - acum)
        new_state = (state * jnp.exp(acum[:, -1])[..., None, None]
                     + jnp.einsum('blgn,blghp->bghpn', Bk, Xc * decay_end[..., None]))
        return new_state, y_diag + y_off

    state0 = jnp.zeros((bsz, G, hg, P, N), f32)
    _, y = lax.scan(step, state0, (X, dA, Bc, Cc))
    return jnp.moveaxis(y, 0, 1).reshape(bsz, seq, H, P)


def mamba2_ssd_mix(h, w_in, conv_w, conv_b, dt_bias, A_log, D_skip, norm_g, w_out):
    f32 = jnp.float32
    bsz, seq, _ = h.shape
    zxbcdt = h @ w_in
    z, xbc, dt = jnp.split(zxbcdt, [MB_D_INNER, MB_D_INNER + MB_CONV_DIM], axis=-1)
    xbc = jax.nn.silu(causal_dwconv(xbc, conv_w, conv_b))
    xs, Bm, Cm = jnp.split(xbc, [MB_D_INNER, MB_D_INNER + MB_GROUPS * MB_STATE], axis=-1)
    dt = jax.nn.softplus((dt + dt_bias).astype(f32))
    A = -jnp.exp(A_log.astype(f32))
    xs = xs.reshape(bsz, seq, MB_HEADS, MB_HEADDIM)
    y = ssd_chunked(xs, dt, A,
                    Bm.reshape(bsz, seq, MB_GROUPS, MB_STATE),
                    Cm.reshape(bsz, seq, MB_GROUPS, MB_STATE))
    y = y + xs.astype(f32) * D_skip.astype(f32)[:, None]
    y = y.reshape(bsz, seq, MB_D_INNER) * jax.nn.silu(z.astype(f32))
    y = rms_norm(y.reshape(bsz, seq, MB_GROUPS, -1), norm_g.reshape(MB_GROUPS, -1))
    return y.reshape(bsz, seq, MB_D_INNER).astype(h.dtype) @ w_out


def swa_sink_attention(h, w_qkv, q_norm, k_norm, sinks, w_o):
    f32 = jnp.float32
    bsz, seq, _ = h.shape
    Hq, Hk, Dh, T = SW_Q_HEADS, SW_KV_HEADS, SW_HEAD_DIM, SW_BLOCK
    G = Hq // Hk
    nb = seq // T
    q, k, v = jnp.split(h @ w_qkv, [Hq * Dh, (Hq + Hk) * Dh], axis=-1)
    q = rms_norm(q.reshape(bsz, seq, Hq, Dh), q_norm).reshape(bsz, nb, T, Hk, G, Dh)
    k = rms_norm(k.reshape(bsz, seq, Hk, Dh), k_norm).reshape(bsz, nb, T, Hk, Dh)
    v = v.reshape(bsz, nb, T, Hk, Dh)

    def with_prev(t):
        prev = jnp.pad(t, ((0, 0), (1, 0), (0, 0), (0, 0), (0, 0)))[:, :-1]
        return jnp.concatenate([prev, t], axis=2)

    kb, vb = with_prev(k), with_prev(v)
    qi = jnp.arange(T)[:, None]
    si = jnp.arange(2 * T)[None, :]
    rel = qi + T - si
    band = (rel >= 0) & (rel < SW_WINDOW)
    mask = band[None] & ((jnp.arange(nb)[:, None, None] > 0) | (si >= T)[None])
    scale = Dh ** -0.5
    sink = sinks.astype(f32).reshape(Hk, G)[None, :, :, None, None]

    def block(args):
        qb, kbb, vbb, mb = args
        s = jnp.einsum('bqkgd,bskd->bkgqs', qb, kbb).astype(f32) * scale
        s = jnp.where(mb, s, NEG_INF)
        m = jnp.maximum(jnp.max(s, axis=-1, keepdims=True), sink)
        p = jnp.exp(s - m)
        denom = jnp.sum(p, axis=-1, keepdims=True) + jnp.exp(sink - m)
        return jnp.einsum('bkgqs,bskd->bqkgd', (p / denom).astype(vbb.dtype), vbb)

    o = lax.map(block, (jnp.moveaxis(q, 1, 0), jnp.moveaxis(kb, 1, 0),
                        jnp.moveaxis(vb, 1, 0), mask))
    o = jnp.moveaxis(o, 0, 1).reshape(bsz, seq, Hq * Dh)
    return o @ w_o


def setup_inputs(seed: int = 0) -> dict:
    key = jax.random.key(seed)
    ks = iter(jax.random.split(key, 64))

    def nrm(shape, scale=1.0):
        return scale * jax.random.normal(next(ks), shape, jnp.float32)

    def unif(shape, lo, hi):
        return jax.random.uniform(next(ks), shape, jnp.float32, lo, hi)

    D = D_MODEL
    dt0 = jnp.exp(unif((N_B, MB_HEADS), math.log(1e-3), math.log(1e-1)))
    return {
        'x': nrm((BATCH, SEQ, D)),
        'c': nrm((BATCH, D)),
        'ada_w': nrm((DEPTH, D, N_SUB * 3 * D), 0.5 * D ** -0.5),
        'ada_b': nrm((DEPTH, N_SUB * 3 * D), 0.02),
        'norm_g': 1.0 + nrm((DEPTH, N_SUB, D), 0.05),
        'ffn_w_in': nrm((DEPTH, 2, D, 2 * D_FF), D ** -0.5),
        'ffn_w_out': nrm((DEPTH, 2, D_FF, D), D_FF ** -0.5),
        'rw_mu': unif((N_A, 6, D), 0.0, 1.0),
        'rw_w_rkv': nrm((N_A, 3, D, D), D ** -0.5),
        'rw_w_o': nrm((N_A, D, D), D ** -0.5),
        'rw_w0': unif((N_A, D), -6.5, -1.5),
        'rw_w1': nrm((N_A, D, RW_DECAY_LORA), D ** -0.5),
        'rw_w2': nrm((N_A, RW_DECAY_LORA, D), 0.1 * RW_DECAY_LORA ** -0.5),
        'rw_a0': nrm((N_A, D), 0.1),
        'rw_a1': nrm((N_A, D, RW_AAA_LORA), D ** -0.5),
        'rw_a2': nrm((N_A, RW_AAA_LORA, D), 0.5 * RW_AAA_LORA ** -0.5),
        'rw_g1': nrm((N_A, D, RW_GATE_LORA), D ** -0.5),
        'rw_g2': nrm((N_A, RW_GATE_LORA, D), RW_GATE_LORA ** -0.5),
        'rw_k_k': 0.85 + nrm((N_A, D), 0.05),
        'rw_k_a': 1.0 + nrm((N_A, D), 0.05),
        'rw_r_k': nrm((N_A, RW_HEADS, RW_HEAD), 0.1),
        'rw_ln_w': 1.0 + nrm((N_A, D), 0.05),
        'rw_ln_b': nrm((N_A, D), 0.02),
        'rw_v0': 1.0 + nrm((N_A_VRES, D), 0.1),
        'rw_v1': nrm((N_A_VRES, D, RW_MV_LORA), D ** -0.5),
        'rw_v2': nrm((N_A_VRES, RW_MV_LORA, D), 0.5 * RW_MV_LORA ** -0.5),
        'mb_w_in': nrm((N_B, D, MB_IN_DIM), D ** -0.5),
        'mb_conv_w': nrm((N_B, MB_CONV, MB_CONV_DIM), 0.5),
        'mb_conv_b': nrm((N_B, MB_CONV_DIM), 0.02),
        'mb_dt_bias': dt0 + jnp.log(-jnp.expm1(-dt0)),
        'mb_A_log': jnp.log(unif((N_B, MB_HEADS), 1.0, 16.0)),
        'mb_D': 1.0 + nrm((N_B, MB_HEADS), 0.1),
        'mb_norm_g': 1.0 + nrm((N_B, MB_D_INNER), 0.05),
        'mb_w_out': nrm((N_B, MB_D_INNER, D), MB_D_INNER ** -0.5),
        'sw_w_qkv': nrm((N_C, D, (SW_Q_HEADS + 2 * SW_KV_HEADS) * SW_HEAD_DIM), D ** -0.5),
        'sw_q_norm': 1.0 + nrm((N_C, SW_HEAD_DIM), 0.05),
        'sw_k_norm': 1.0 + nrm((N_C, SW_HEAD_DIM), 0.05),
        'sw_sinks': nrm((N_C, SW_Q_HEADS), 0.5),
        'sw_w_o': nrm((N_C, SW_Q_HEADS * SW_HEAD_DIM, D), (SW_Q_HEADS * SW_HEAD_DIM) ** -0.5),
    }


def reference(x, c, ada_w, ada_b, norm_g, ffn_w_in, ffn_w_out,
              rw_mu, rw_w_rkv, rw_w_o, rw_w0, rw_w1, rw_w2, rw_a0, rw_a1, rw_a2,
              rw_g1, rw_g2, rw_k_k, rw_k_a, rw_r_k, rw_ln_w, rw_ln_b,
              rw_v0, rw_v1, rw_v2,
              mb_w_in, mb_conv_w, mb_conv_b, mb_dt_bias, mb_A_log, mb_D, mb_norm_g, mb_w_out,
              sw_w_qkv, sw_q_norm, sw_k_norm, sw_sinks, sw_w_o):
    bsz, _, d = x.shape
    c_act = jax.nn.silu(c)
    v_first = None
    for i in range(DEPTH):
        mod = (c_act @ ada_w[i] + ada_b[i]).reshape(bsz, N_SUB, 3, d)
        shift, scale, gate = mod[:, :, 0], mod[:, :, 1], mod[:, :, 2]

        h = adaln(x, norm_g[i, 0], shift[:, 0], scale[:, 0])
        x = x + MACARON_W * gate[:, 0, None] * swiglu(h, ffn_w_in[i, 0], ffn_w_out[i, 0])

        h = adaln(x, norm_g[i, 1], shift[:, 1], scale[:, 1])
        kind, j = i % N_MIXERS, i // N_MIXERS
        if kind == 0:
            vres = None if v_first is None else (rw_v0[j - 1], rw_v1[j - 1], rw_v2[j - 1])
            y, v_first = rwkv7_time_mix(h, rw_mu[j], rw_w_rkv[j], rw_w_o[j], rw_w0[j], rw_w1[j],
                                        rw_w2[j], rw_a0[j], rw_a1[j], rw_a2[j], rw_g1[j],
                                        rw_g2[j], rw_k_k[j], rw_k_a[j], rw_r_k[j],
                                        rw_ln_w[j], rw_ln_b[j], v_first, vres)
        elif kind == 1:
            y = mamba2_ssd_mix(h, mb_w_in[j], mb_conv_w[j], mb_conv_b[j], mb_dt_bias[j],
                               mb_A_log[j], mb_D[j], mb_norm_g[j], mb_w_out[j])
        else:
            y = swa_sink_attention(h, sw_w_qkv[j], sw_q_norm[j], sw_k_norm[j],
                                   sw_sinks[j], sw_w_o[j])
        x = x + gate[:, 1, None] * y

        h = adaln(x, norm_g[i, 2], shift[:, 2], scale[:, 2])
        x = x + MACARON_W * gate[:, 2, None] * swiglu(h, ffn_w_in[i, 1], ffn_w_out[i, 1])
    return x
```

```python
import math
from contextlib import ExitStack
import numpy as np
import concourse.bass as bass
import concourse.mybir as mybir
from concourse.bass_utils import run_bass_kernel_spmd

F32 = mybir.dt.float32
BF16 = mybir.dt.bfloat16
AF = mybir.ActivationFunctionType
ALU = mybir.AluOpType
AX = mybir.AxisListType

D = 1024
S = 2048
KC = 8
DFF = 2816
NJ = 22
DEPTH = 4
NCORES = 8
NSEQ_CORE = 4
EPS = 1e-6

SEM_BLOCK = 30000
ENGS = ("pe", "dve", "act", "pool", "sp")


class Buf:
    __slots__ = ("name", "last_w", "readers", "excl")

    def __init__(self, name="", excl=False):
        self.name = name
        self.last_w = None
        self.readers = []
        self.excl = excl


class Prog:
    def __init__(self, nc, stack, n_dma_sems=32):
        self.nc = nc
        self.stack = stack
        self.eng = {"pe": nc.tensor, "dve": nc.vector, "act": nc.scalar,
                    "pool": nc.gpsimd, "sp": nc.sync}
        self.count = {e: 0 for e in ENGS}
        self.sems = {e: [] for e in ENGS}
        self.waited = {e: {} for e in ENGS}
        self.waited_dma = {e: {} for e in ENGS}
        self.dma_sems = [stack.enter_context(nc.semaphore(f"dsem{i}")) for i in range(n_dma_sems)]
        self.dma_val = [0] * n_dma_sems
        self.dma_k = 0
        self.n_waits = 0
        self.n_dma = 0

    def _sem_for(self, e, idx):
        b = idx // SEM_BLOCK
        while len(self.sems[e]) <= b:
            self.sems[e].append(self.stack.enter_context(
                self.nc.semaphore(f"s_{e}_{len(self.sems[e])}")))
        return self.sems[e][b], (idx % SEM_BLOCK) + 1

    def _wait_tok(self, e, tok):
        if tok is None:
            return
        eng = self.eng[e]
        if tok[0] == "c":
            _, se, idx = tok
            if se == e:
                return
            b = idx // SEM_BLOCK
            key = (se, b)
            if self.waited[e].get(key, -1) >= idx:
                return
            self.waited[e][key] = idx
            for bb in range(b):
                self.waited[e][(se, bb)] = 1 << 60
            sem, val = self._sem_for(se, idx)
            eng.wait_ge(sem, val)
            self.n_waits += 1
        else:
            _, si, val = tok
            if self.waited_dma[e].get(si, 0) >= val:
                return
            self.waited_dma[e][si] = val
            eng.wait_ge(self.dma_sems[si], val)
            self.n_waits += 1

    def _deps(self, e, reads, writes):
        for b in reads:
            self._wait_tok(e, b.last_w)
            if b.excl:
                for r in b.readers:
                    self._wait_tok(e, r)
        for b in writes:
            self._wait_tok(e, b.last_w)
            for r in b.readers:
                self._wait_tok(e, r)

    def _commit(self, tok, reads, writes):
        for b in reads:
            b.readers.append(tok)
            if len(b.readers) > 64:
                b.readers = b.readers[-64:] if False else b.readers
        for b in writes:
            b.last_w = tok
            b.readers = []

    def op(self, e, fn, reads=(), writes=()):
        self._deps(e, reads, writes)
        idx = self.count[e]
        inst = fn(self.eng[e])
        sem, _ = self._sem_for(e, idx)
        inst.then_inc(sem, 1)
        self.count[e] = idx + 1
        self._commit(("c", e, idx), reads, writes)

    def dma(self, out, in_, reads=(), writes=(), e="sp", **kw):
        self._deps(e, reads, writes)
        si = self.dma_k % len(self.dma_sems)
        self.dma_k += 1
        prev = self.dma_val[si]
        if prev:
            self._wait_tok(e, ("d", si, prev))
        val = prev + 16
        self.dma_val[si] = val
        self.eng[e].dma_start(out=out, in_=in_, **kw).then_inc(self.dma_sems[si], 16)
        self.n_dma += 1
        tok = ("d", si, val)
        self._commit(tok, reads, writes)
        return tok

    def wait_all_dma(self, e="sp"):
        for si, v in enumerate(self.dma_val):
            if v:
                self._wait_tok(e, ("d", si, v))


def _compact_readers(b):
    pass


def fm_cols(v):
    v = np.asarray(v, np.float32).reshape(-1)
    n = v.size // 128
    return np.ascontiguousarray(v.reshape(n, 128).T)


def tile_w(W, kpad=None):
    K, F = W.shape
    kch = (K + 127) // 128
    if K % 128:
        Wp = np.zeros((kch * 128, F), np.float32)
        Wp[:K] = W
        W = Wp
    nf = F // 128
    return np.ascontiguousarray(W.reshape(kch, 128, nf, 128).transpose(2, 1, 0, 3))


class VecPack:
    def __init__(self):
        self.cols = []
        self.off = {}
        self.n = 0

    def add(self, key, arr2d):
        self.off[key] = self.n
        self.cols.append(np.asarray(arr2d, np.float32))
        self.n += arr2d.shape[1]

    def pack(self):
        return np.ascontiguousarray(np.concatenate(self.cols, axis=1))


def vec_keys_sizes():
    ks = []
    for l in range(DEPTH):
        ks.append((("ng", l), 24))
        ks.append((("adab", l), 72))
    return ks


def build_host_arrays(inp):
    H = {}
    vp = VecPack()
    for l in range(DEPTH):
        vp.add(("ng", l), fm_cols(inp["norm_g"][l]))
        vp.add(("adab", l), fm_cols(inp["ada_b"][l]))
    H["vecs"] = vp
    build_host_mixers(inp, H, vp)
    aw = np.asarray(inp["ada_w"], np.float32)
    H["ada_w"] = np.ascontiguousarray(
        aw.reshape(DEPTH, 8, 128, 18, 512).transpose(0, 3, 2, 1, 4))
    wi = np.asarray(inp["ffn_w_in"], np.float32).reshape(DEPTH, 2, 8, 128, 2, NJ, 128)
    H["ffn_w_in"] = np.ascontiguousarray(wi.transpose(0, 1, 5, 3, 2, 4, 6)).reshape(
        DEPTH, 2, NJ, 128, 8 * 256)
    wo = np.asarray(inp["ffn_w_out"], np.float32).reshape(DEPTH, 2, NJ, 128, 8, 128)
    H["ffn_w_out"] = np.ascontiguousarray(wo.transpose(0, 1, 4, 3, 2, 5)).reshape(
        DEPTH, 2, 8, 128, NJ * 128)
    return H


def build_consts():
    C = {}
    p = np.arange(128)[:, None]
    q = np.arange(128)[None, :]
    C["ident"] = (p == q).astype(np.float32)
    C["blk64"] = ((p // 64) == (q // 64)).astype(np.float32) / 64.0
    C["mcur"] = (q >= p).astype(np.float32)
    C["mprev"] = (p > q).astype(np.float32)
    C["blk1"] = ((p // 64) == (q // 64)).astype(np.float32)
    i64, t64 = p % 64, q % 64
    C["mA"] = np.where(q < 64, i64 < t64, i64 <= t64).astype(np.float32)
    C["mT"] = (t64 < i64).astype(np.float32)[:, 0:64]
    C["rmask"] = np.broadcast_to((q % 64 != 0), (128, 128)).astype(np.float32)
    C["onesf"] = np.ones((128, 128), np.float32)
    off = {}
    cols = []
    n = 0
    for k, v in C.items():
        off[k] = n
        cols.append(v)
        n += v.shape[1]
    return np.ascontiguousarray(np.concatenate(cols, axis=1)), off


def build_host_mixers(inp, H, vp):
    wqkv = np.asarray(inp["sw_w_qkv"][0], np.float32)
    H["sw_wq"] = tile_w(wqkv[:, 0:1024]).reshape(8, 128, 1024)
    wk = wqkv[:, 1024:1280].reshape(1024, 4, 64)
    wkd = np.concatenate([wk, wk], axis=2).reshape(1024, 512)
    H["sw_wk"] = tile_w(wkd).reshape(4, 128, 1024)
    H["sw_wv"] = np.ascontiguousarray(
        wqkv[:, 1280:1536].reshape(8, 128, 256).transpose(1, 0, 2)).reshape(1, 128, 2048)
    H["sw_wo"] = tile_w(np.asarray(inp["sw_w_o"][0], np.float32)).reshape(8, 128, 1024)
    vp.add(("swq",), np.tile(np.asarray(inp["sw_q_norm"][0], np.float32), 2).reshape(128, 1))
    vp.add(("swk",), np.tile(np.asarray(inp["sw_k_norm"][0], np.float32), 2).reshape(128, 1))
    vp.add(("sink",), np.tile(np.asarray(inp["sw_sinks"][0], np.float32)[None, :], (128, 1)))
    build_host_rwkv(inp, H, vp)
    build_host_mamba(inp, H, vp)


def build_host_rwkv(inp, H, vp):
    NA = 2
    def st(fn):
        return np.ascontiguousarray(np.stack([fn(j) for j in range(NA)]))
    f32 = lambda a: np.asarray(a, np.float32)
    for i, nm in enumerate(("rw_wr", "rw_wk", "rw_wv")):
        H[nm] = st(lambda j: tile_w(f32(inp["rw_w_rkv"][j, i])).reshape(8, 128, 1024))
    H["rw_wo"] = st(lambda j: tile_w(f32(inp["rw_w_o"][j])).reshape(8, 128, 1024))
    def l1(key, R):
        return st(lambda j: f32(inp[key][j]).reshape(8, 128, R).transpose(1, 0, 2).reshape(128, 8 * R))
    H["rw_w1"] = l1("rw_w1", 64)
    H["rw_a1"] = l1("rw_a1", 64)
    H["rw_g1"] = l1("rw_g1", 160)
    H["rw_w2"] = st(lambda j: f32(inp["rw_w2"][j]))
    H["rw_a2"] = st(lambda j: f32(inp["rw_a2"][j]))
    H["rw_g2a"] = st(lambda j: f32(inp["rw_g2"][j][0:128]))
    H["rw_g2b"] = st(lambda j: f32(inp["rw_g2"][j][128:160]))
    v1 = f32(inp["rw_v1"][0]).reshape(8, 128, 32).transpose(1, 0, 2).reshape(128, 256)
    H["rw_v1"] = np.ascontiguousarray(np.stack([v1, v1]))
    v2 = f32(inp["rw_v2"][0])
    H["rw_v2"] = np.ascontiguousarray(np.stack([v2, v2]))
    for j in range(NA):
        vp.add(("mu", j), fm_cols(inp["rw_mu"][j]))
        for nm in ("w0", "a0", "k_k", "k_a", "ln_w", "ln_b", "r_k"):
            vp.add((nm, j), fm_cols(inp["rw_" + nm][j]))
    vp.add(("v0", 1), fm_cols(inp["rw_v0"][0]))


def build_host_mamba(inp, H, vp):
    f32 = lambda a: np.asarray(a, np.float32)
    win = f32(inp["mb_w_in"][0])
    H["mb_win"] = tile_w(win[:, 0:6144]).reshape(48, 128, 1024)
    H["mb_wdt"] = np.ascontiguousarray(win[:, 6144:6176].reshape(8, 128, 32).transpose(1, 0, 2)).reshape(1, 128, 256)
    H["mb_wo"] = tile_w(f32(inp["mb_w_out"][0])).reshape(8, 128, 2048)
    cw = f32(inp["mb_conv_w"][0])
    for tap in range(4):
        vp.add(("cw", tap), fm_cols(cw[tap]))
    vp.add(("cbias",), fm_cols(inp["mb_conv_b"][0]))
    vp.add(("mbD",), fm_cols(np.repeat(f32(inp["mb_D"][0]), 64)))
    vp.add(("mbng",), fm_cols(inp["mb_norm_g"][0]))
    vp.add(("dtb",), np.tile(f32(inp["mb_dt_bias"][0])[None, :], (128, 1)))
    vp.add(("alog",), np.tile(f32(inp["mb_A_log"][0])[None, :], (128, 1)))


W_SPECS = {"mb_win": [48, 128, 1024], "mb_wdt": [1, 128, 256], "mb_wo": [8, 128, 2048],
           "rw_wr": [2, 8, 128, 1024], "rw_wk": [2, 8, 128, 1024], "rw_wv": [2, 8, 128, 1024],
           "rw_wo": [2, 8, 128, 1024], "rw_w1": [2, 128, 512], "rw_a1": [2, 128, 512],
           "rw_g1": [2, 128, 1280], "rw_w2": [2, 64, 1024], "rw_a2": [2, 64, 1024],
           "rw_g2a": [2, 128, 1024], "rw_g2b": [2, 32, 1024], "rw_v1": [2, 128, 256],
           "rw_v2": [2, 32, 1024],
           "sw_wq": [8, 128, 1024], "sw_wk": [4, 128, 1024], "sw_wv": [1, 128, 2048],
           "sw_wo": [8, 128, 1024]}


class Builder:
    def __init__(self, nseq, n_layers, vec_off, nvec, mixers=True, dbg=None, kinds=None,
                 const_off=None, nconst=0):
        self.kinds = kinds if kinds is not None else [i % 3 for i in range(DEPTH)]
        self.const_off = const_off
        self.nconst = nconst
        self.nseq = nseq
        self.n_layers = n_layers
        self.vec_off = vec_off
        self.nvec = nvec
        self.mixers = mixers
        self.dbg = dbg

    def build(self):
        nc = bass.Bass("TRN2", target_bir_lowering=False)
        self.nc = nc
        ns = self.nseq
        dt = nc.dram_tensor
        self.d_xT = dt("xT", [ns, KC, 128, S], F32, kind="ExternalInput").ap()
        self.d_cT = dt("cT", [128, KC * ns], F32, kind="ExternalInput").ap()
        self.d_vecs = dt("vecs", [128, self.nvec], F32, kind="ExternalInput").ap()
        self.d_adaw = dt("ada_w", [DEPTH, 18, 128, 8 * 512], F32, kind="ExternalInput").ap()
        self.d_win = dt("ffn_w_in", [DEPTH, 2, NJ, 128, 8 * 256], F32, kind="ExternalInput").ap()
        self.d_wout = dt("ffn_w_out", [DEPTH, 2, 8, 128, NJ * 128], F32, kind="ExternalInput").ap()
        self.d_yT = dt("yT", [ns, KC, 128, S], F32, kind="ExternalOutput").ap()
        self.d_consts = dt("consts", [128, self.nconst], F32, kind="ExternalInput").ap()
        self.dw = {k: dt(k, shp, F32, kind="ExternalInput").ap() for k, shp in W_SPECS.items()}
        self.scr = {}
        self.b_scr = {}
        for nm in ("R", "K", "V", "A", "G", "VF", "LD"):
            self.scr[nm] = dt("scr_" + nm, [KC, 128, S], F32 if nm == "LD" else BF16, kind="Internal").ap()
            self.b_scr[nm] = [Buf(f"scr{nm}{t}") for t in range(4)]
        for nm, n in (("MZ", 16), ("MX", 16), ("MB", 8), ("MC", 8)):
            self.scr[nm] = dt("scr_" + nm, [n, 128, S], BF16, kind="Internal").ap()
            self.b_scr[nm] = [Buf(f"scr{nm}")]
        with ExitStack() as st:
            self.st = st
            self.P = Prog(nc, st)
            self.alloc()
            self.emit()
            self.P.wait_all_dma("sp")
        print("instr counts", self.P.count, "waits", self.P.n_waits, "dmas", self.P.n_dma)
        return nc

    def sb(self, name, shape, dtype):
        return self.st.enter_context(self.nc.sbuf_tensor(name, shape, dtype))

    def alloc(self):
        ns = self.nseq
        self.xT = self.sb("xT_sb", [128, KC, S], F32)
        self.b_xT = [[Buf(f"xT{kc}_{t}") for t in range(4)] for kc in range(KC)]
        self.hb = self.sb("hb", [128, KC, S], BF16)
        self.b_hb = [Buf(f"hb{t}") for t in range(4)]
        self.u = self.sb("u", [128, NJ, 1024], BF16)
        self.b_u = [[Buf(f"u{j}_{t}") for t in range(2)] for j in range(NJ)]
        self.st32 = [self.sb(f"st32_{i}", [128, NJ * 128], F32) for i in range(2)]
        self.stb = [self.sb(f"stb_{i}", [128, NJ * 128], BF16) for i in range(2)]
        self.b_st32 = [Buf(f"st32_{i}") for i in range(2)]
        self.b_stb = [Buf(f"stb_{i}") for i in range(2)]
        self.wslot = 0
        self.tmpf = [self.sb(f"tmpf{i}", [128, 512], F32) for i in range(4)]
        self.b_tmpf = [Buf(f"tmpf{i}") for i in range(4)]
        self.tmpb = [self.sb(f"tmpb{i}", [128, 512], BF16) for i in range(2)]
        self.b_tmpb = [Buf(f"tmpb{i}") for i in range(2)]
        self.vecs = self.sb("vecs_sb", [128, self.nvec], F32)
        self.b_vecs = Buf("vecs")
        self.modT = self.sb("modT", [128, DEPTH, 72, ns], F32)
        self.b_mod = Buf("mod")
        self.aco = self.sb("aco", [128, DEPTH * 3 * ns * 8], F32)
        self.gco = self.sb("gco", [128, DEPTH * 3 * ns * 8], F32)
        self.cact = self.sb("cact", [128, KC * ns], F32)
        self.b_cact = Buf("cact")
        self.ones_b = self.sb("ones_b", [128, 128], BF16)
        self.b_const = Buf("const")
        self.cf = self.sb("consts_f", [128, self.nconst], F32)
        self.cb = self.sb("consts_b", [128, self.nconst], BF16)
        self.ones1 = self.sb("ones1", [128, 128], BF16)
        self.esink = self.sb("esink", [128, 16], F32)
        self.omu = self.sb("omu", [128, 96], F32)
        self.mbdt = self.sb("mbdt", [128, 512], F32)
        self.b_mbdt = Buf("mbdt")
        self.bank_rr = 0
        self.psum = self.st.enter_context(self.nc.psum_tensor("ps", [128, 8, 512], F32))
        self.b_ps = [Buf(f"ps{i}", excl=True) for i in range(8)]

    def vcol(self, key, i=0, n=1):
        o = self.vec_off[key] + i
        return self.vecs[:, o:o + n]

    def co_idx(self, l, sub, s):
        return ((l * 3 + sub) * self.nseq + s) * 8

    def emit(self):
        P = self.P
        P.dma(self.vecs[:], self.d_vecs, writes=[self.b_vecs])
        P.dma(self.cact[:], self.d_cT, writes=[self.b_cact])
        P.op("pool", lambda e: e.memset(self.ones_b[:], 1.0 / 1024.0), writes=[self.b_const])
        P.op("pool", lambda e: e.memset(self.ones1[:], 1.0), writes=[self.b_const])
        P.dma(self.cf[:], self.d_consts, writes=[self.b_const])
        P.op("dve", lambda e: e.tensor_copy(out=self.cb[:], in_=self.cf[:]),
             reads=[self.b_const], writes=[self.b_const])
        for jj in range(2):
            P.op("dve", lambda e, jj=jj: e.tensor_scalar(out=self.omu[:, jj * 48:(jj + 1) * 48],
                                                         in0=self.vcol(("mu", jj), 0, 48), scalar1=-1.0, scalar2=1.0,
                                                         op0=ALU.mult, op1=ALU.add),
                 reads=[self.b_vecs], writes=[self.b_const])
        if 2 in self.kinds[:self.n_layers]:
            P.op("act", lambda e: e.activation(out=self.esink[:], in_=self.vcol(("sink",), 0, 16),
                                               func=AF.Exp),
                 reads=[self.b_vecs], writes=[self.b_const])
        P.op("act", lambda e: e.activation(out=self.cact[:], in_=self.cact[:], func=AF.Silu),
             reads=[self.b_cact], writes=[self.b_cact])
        self.emit_mod()
        for s in range(self.nseq):
            self.emit_seq(s)

    def emit_mod(self):
        P = self.P
        ns = self.nseq
        for l in range(self.n_layers):
            for blk in range(18):
                for half in range(2):
                    slot = self.wslot
                    self.wslot ^= 1
                    sl = self.st32[slot]
                    bsl = self.b_st32[slot]
                    P.dma(sl[:, 0:2048], self.d_adaw[l, blk, :, half * 2048:(half + 1) * 2048],
                          writes=[bsl])
                    for fcl in range(4):
                        for k4 in range(4):
                            kcg = half * 4 + k4
                            P.op("pe", lambda e, sl=sl, k4=k4, fcl=fcl, half=half, kcg=kcg: e.matmul(
                                self.psum[:, half, fcl * ns:(fcl + 1) * ns],
                                lhsT=sl[:, k4 * 512 + fcl * 128:k4 * 512 + (fcl + 1) * 128],
                                rhs=self.cact[:, kcg * ns:(kcg + 1) * ns],
                                start=(k4 == 0), stop=(k4 == 3)),
                                reads=[bsl, self.b_cact], writes=[self.b_ps[half]])
                P.op("act", lambda e: e.activation(out=self.tmpf[0][:, 0:4 * ns],
                                                   in_=self.psum[:, 1, 0:4 * ns], func=AF.Copy),
                     reads=[self.b_ps[1]], writes=[self.b_tmpf[0]])
                for fcl in range(4):
                    fch = blk * 4 + fcl
                    P.op("dve", lambda e, fcl=fcl, fch=fch, l=l: e.scalar_tensor_tensor(
                        out=self.modT[:, l, fch, :], in0=self.psum[:, 0, fcl * ns:(fcl + 1) * ns],
                        scalar=self.vcol(("adab", l), fch), op0=ALU.add,
                        in1=self.tmpf[0][:, fcl * ns:(fcl + 1) * ns], op1=ALU.add),
                        reads=[self.b_ps[0], self.b_vecs, self.b_tmpf[0]], writes=[self.b_mod])
            for sub in range(3):
                for s in range(ns):
                    i0 = self.co_idx(l, sub, s)
                    P.op("dve", lambda e, l=l, sub=sub, s=s, i0=i0: e.scalar_tensor_tensor(
                        out=self.aco[:, i0:i0 + 8], in0=self.modT[:, l, sub * 24 + 8:sub * 24 + 16, s],
                        scalar=1.0, op0=ALU.add, in1=self.vcol(("ng", l), sub * 8, 8), op1=ALU.mult),
                        reads=[self.b_mod, self.b_vecs], writes=[self.b_mod])
                    gsc = 1.0 if sub == 1 else 0.5
                    P.op("dve", lambda e, l=l, sub=sub, s=s, i0=i0, gsc=gsc: e.tensor_scalar(
                        out=self.gco[:, i0:i0 + 8], in0=self.modT[:, l, sub * 24 + 16:sub * 24 + 24, s],
                        scalar1=gsc, scalar2=None, op0=ALU.mult),
                        reads=[self.b_mod], writes=[self.b_mod])

    def emit_seq(self, s):
        P = self.P
        for kc in range(KC):
            P.dma(self.xT[:, kc, :], self.d_xT[s, kc], writes=self.b_xT[kc])
        for l in range(self.n_layers):
            self.emit_adaln(l, 0, s)
            self.emit_ffn(l, 0, s)
            if self.mixers:
                self.emit_adaln(l, 1, s)
                self.emit_mixer(l, s)
            self.emit_adaln(l, 2, s)
            self.emit_ffn(l, 1, s)
        for kc in range(KC):
            P.dma(self.d_yT[s, kc], self.xT[:, kc, :], reads=self.b_xT[kc])

    def emit_mixer(self, l, s):
        kind = self.kinds[l]
        if kind == 2:
            self.emit_swa(l, s)
        elif kind == 1:
            self.emit_mamba(l, s)
        else:
            self.emit_rwkv(l, s)

    def cst(self, key, n=128, bf=True):
        o = self.const_off[key]
        return (self.cb if bf else self.cf)[:, o:o + n]

    def proj(self, wd, nf, kch, rhs_fn, rhs_bufs_fn, tiles, consume, banks=(0, 1, 2, 3)):
        P = self.P
        for fc in range(nf):
            w, bw = self.load_w(wd[fc], kch * 128)
            for ti, (tok, ntok) in enumerate(tiles):
                bank = banks[self.bank_rr % len(banks)]
                self.bank_rr += 1
                for kc in range(kch):
                    P.op("pe", lambda e, kc=kc, bank=bank, w=w, tok=tok, ntok=ntok: e.matmul(
                        self.psum[:, bank, 0:ntok], lhsT=w[:, kc * 128:(kc + 1) * 128],
                        rhs=rhs_fn(kc, tok), start=(kc == 0), stop=(kc == kch - 1)),
                        reads=[bw] + list(rhs_bufs_fn(ti)), writes=[self.b_ps[bank]])
                consume(fc, ti, tok, bank)

    def out_proj(self, wd, l, s, kch=8):
        P = self.P
        i0 = self.co_idx(l, 1, s)
        tiles = [(slice(t * 512, (t + 1) * 512), 512) for t in range(4)]

        def consume(fc, ti, tok, bank):
            P.op("dve", lambda e: e.scalar_tensor_tensor(
                out=self.xT[:, fc, tok], in0=self.psum[:, bank, :],
                scalar=self.gco[:, i0 + fc:i0 + fc + 1], op0=ALU.mult,
                in1=self.xT[:, fc, tok], op1=ALU.add),
                reads=[self.b_ps[bank], self.b_mod, self.b_xT[fc][ti]], writes=[self.b_xT[fc][ti]])
        self.proj(wd, 8, kch, lambda kc, tok: self.hb[:, kc, tok], lambda ti: [self.b_hb[ti]],
                  tiles, consume)

    def emit_swa(self, l, s):
        P = self.P
        uflat = self.u[:].rearrange("p a b -> p (a b)")
        kn = uflat[:, 0:8192].rearrange("p (g t) -> p g t", g=4)
        vt = uflat[:, 8192:12288].rearrange("p (b n) -> p b n", b=16)
        qn = uflat[:, 12288:16384].rearrange("p (c t) -> p c t", c=8)
        pT = [uflat[:, 16384 + i * 256:16384 + (i + 1) * 256] for i in range(2)]
        b_kn = [Buf(f"kn{t}") for t in range(4)]
        b_vt = [Buf(f"vt{t}") for t in range(16)]
        b_qn = Buf("qn")
        b_pT = [Buf("pT0"), Buf("pT1")]
        b_rd = Buf("rd")
        allu = [b for row in self.b_u for b in row]
        tiles = [(slice(t * 512, (t + 1) * 512), 512) for t in range(4)]

        def norm_consume(dst_fn, dst_buf_fn, gkey):
            def consume(fc, ti, tok, bank):
                sq = self.tmpb[fc % 2]
                bsq = self.b_tmpb[fc % 2]
                P.op("act", lambda e: e.activation(out=sq[:], in_=self.psum[:, bank, :], func=AF.Square),
                     reads=[self.b_ps[bank]], writes=[bsq])
                P.op("pe", lambda e: e.matmul(self.psum[:, 6, :], lhsT=self.cst("blk64"), rhs=sq[:],
                                              start=True, stop=True),
                     reads=[bsq, self.b_const], writes=[self.b_ps[6]])
                rs = self.tmpf[2]
                brs = self.b_tmpf[2]
                P.op("act", lambda e: e.activation(out=rs[:], in_=self.psum[:, 6, :], func=AF.Sqrt,
                                                   bias=EPS, scale=1.0),
                     reads=[self.b_ps[6]], writes=[brs])
                P.op("dve", lambda e: e.reciprocal(out=rs[:], in_=rs[:]), reads=[brs], writes=[brs])
                P.op("dve", lambda e: e.scalar_tensor_tensor(
                    out=dst_fn(fc, ti, tok), in0=self.psum[:, bank, :], scalar=self.vcol(gkey),
                    op0=ALU.mult, in1=rs[:], op1=ALU.mult),
                    reads=[self.b_ps[bank], brs, self.b_vecs], writes=[dst_buf_fn(fc, ti)] + (allu if (fc == 0 and ti == 0) else []))
            return consume

        self.proj(self.dw["sw_wk"], 4, 8, lambda kc, tok: self.hb[:, kc, tok], lambda ti: [self.b_hb[ti]],
                  tiles, norm_consume(lambda fc, ti, tok: kn[:, fc, tok], lambda fc, ti: b_kn[ti], ("swk",)))
        wv, bwv = self.load_w(self.dw["sw_wv"][0], 2048)
        for blk in range(16):
            bank = (blk % 2)
            for kc in range(KC):
                P.op("pe", lambda e, kc=kc, blk=blk, bank=bank: e.matmul(
                    self.psum[:, bank, 0:256], lhsT=self.hb[:, kc, blk * 128:(blk + 1) * 128],
                    rhs=wv[:, kc * 256:(kc + 1) * 256], start=(kc == 0), stop=(kc == KC - 1)),
                    reads=[bwv, self.b_hb[blk // 4]], writes=[self.b_ps[bank]])
            P.op("act", lambda e, blk=blk, bank=bank: e.activation(
                out=vt[:, blk, :], in_=self.psum[:, bank, 0:256], func=AF.Copy),
                reads=[self.b_ps[bank]], writes=[b_vt[blk]])
        for t in range(4):
            self.proj(self.dw["sw_wq"], 8, 8, lambda kc, tok: self.hb[:, kc, tok],
                      lambda ti, t=t: [self.b_hb[t]], [tiles[t]],
                      norm_consume(lambda fc, ti, tok: qn[:, fc, :], lambda fc, ti: b_qn, ("swq",)))
            for bl in range(4):
                b = t * 4 + bl
                qtok = slice(bl * 128, (bl + 1) * 128)
                for h in range(16):
                    kc, pb, g = h // 2, (h % 2) * 64, h // 4
                    sbank = 4 + (h % 2)
                    nk = 256 if b > 0 else 128
                    P.op("pe", lambda e, kc=kc, pb=pb, g=g, sbank=sbank, b=b: e.matmul(
                        self.psum[:, sbank, 0:128], lhsT=kn[pb:pb + 64, g, b * 128:(b + 1) * 128],
                        rhs=qn[pb:pb + 64, kc, qtok], start=True, stop=True),
                        reads=[b_kn[b // 4], b_qn], writes=[self.b_ps[sbank]])
                    if b > 0:
                        P.op("pe", lambda e, kc=kc, pb=pb, g=g, sbank=sbank, b=b: e.matmul(
                            self.psum[:, sbank, 128:256], lhsT=kn[pb:pb + 64, g, (b - 1) * 128:b * 128],
                            rhs=qn[pb:pb + 64, kc, qtok], start=True, stop=True),
                            reads=[b_kn[(b - 1) // 4], b_qn], writes=[self.b_ps[sbank]])
                    pt = pT[h % 2]
                    bpt = b_pT[h % 2]
                    P.op("act", lambda e, sbank=sbank, pt=pt, nk=nk: e.activation(
                        out=pt[:, 0:nk], in_=self.psum[:, sbank, 0:nk], func=AF.Exp, scale=0.125),
                        reads=[self.b_ps[sbank]], writes=[bpt])
                    P.op("pool", lambda e, pt=pt: e.tensor_tensor(
                        out=pt[:, 0:128], in0=pt[:, 0:128], in1=self.cst("mcur"), op=ALU.mult),
                        reads=[bpt, self.b_const], writes=[bpt])
                    if b > 0:
                        P.op("pool", lambda e, pt=pt: e.tensor_tensor(
                            out=pt[:, 128:256], in0=pt[:, 128:256], in1=self.cst("mprev"), op=ALU.mult),
                            reads=[bpt, self.b_const], writes=[bpt])
                    P.op("pe", lambda e, pt=pt, b=b: e.matmul(
                        self.psum[:, 6, 0:128], lhsT=self.ones1[:], rhs=pt[:, 0:128],
                        start=True, stop=(b == 0)), reads=[bpt, self.b_const], writes=[self.b_ps[6]])
                    if b > 0:
                        P.op("pe", lambda e, pt=pt: e.matmul(
                            self.psum[:, 6, 0:128], lhsT=self.ones1[:], rhs=pt[:, 128:256],
                            start=False, stop=True), reads=[bpt, self.b_const], writes=[self.b_ps[6]])
                    P.op("pe", lambda e, pt=pt, b=b, g=g, pb=pb: e.matmul(
                        self.psum[pb:pb + 64, 7, 0:128], lhsT=vt[:, b, g * 64:(g + 1) * 64], rhs=pt[:, 0:128],
                        start=True, stop=(b == 0)), reads=[bpt, b_vt[b]], writes=[self.b_ps[7]])
                    if b > 0:
                        P.op("pe", lambda e, pt=pt, b=b, g=g, pb=pb: e.matmul(
                            self.psum[pb:pb + 64, 7, 0:128], lhsT=vt[:, b - 1, g * 64:(g + 1) * 64],
                            rhs=pt[:, 128:256], start=False, stop=True),
                            reads=[bpt, b_vt[b - 1]], writes=[self.b_ps[7]])
                    rd = self.tmpf[3]
                    brd = self.b_tmpf[3]
                    P.op("dve", lambda e, h=h: e.tensor_scalar(
                        out=rd[:, 0:128], in0=self.psum[:, 6, 0:128], scalar1=self.esink[:, h:h + 1],
                        scalar2=None, op0=ALU.add), reads=[self.b_ps[6], self.b_const], writes=[brd])
                    P.op("dve", lambda e: e.reciprocal(out=rd[:, 0:128], in_=rd[:, 0:128]),
                         reads=[brd], writes=[brd])
                    P.op("dve", lambda e, pb=pb, kc=kc, b=b: e.tensor_tensor(
                        out=self.hb[pb:pb + 64, kc, b * 128:(b + 1) * 128], in0=self.psum[pb:pb + 64, 7, 0:128],
                        in1=rd[pb:pb + 64, 0:128], op=ALU.mult),
                        reads=[self.b_ps[7], brd], writes=[self.b_hb[t]])
        self.out_proj(self.dw["sw_wo"], l, s)
        for row in self.b_u:
            for bb in row:
                for x in b_kn + b_vt + [b_qn] + b_pT:
                    if x.last_w is not None:
                        bb.readers.append(x.last_w)
                    bb.readers.extend(x.readers)

    def emit_adaln(self, l, sub, s):
        P = self.P
        i0 = self.co_idx(l, sub, s)
        for t in range(4):
            tok = slice(t * 512, (t + 1) * 512)
            pb = 6 + (t % 2)
            for kc in range(KC):
                sq = self.tmpb[kc % 2]
                bsq = self.b_tmpb[kc % 2]
                P.op("act", lambda e, sq=sq, kc=kc: e.activation(
                    out=sq[:], in_=self.xT[:, kc, tok], func=AF.Square),
                    reads=[self.b_xT[kc][t]], writes=[bsq])
                P.op("pe", lambda e, sq=sq, kc=kc, pb=pb: e.matmul(
                    self.psum[:, pb, :], lhsT=self.ones_b[:], rhs=sq[:],
                    start=(kc == 0), stop=(kc == KC - 1)),
                    reads=[bsq, self.b_const], writes=[self.b_ps[pb]])
            rs = self.tmpf[2]
            brs = self.b_tmpf[2]
            P.op("act", lambda e, pb=pb: e.activation(out=rs[:], in_=self.psum[:, pb, :],
                                                      func=AF.Sqrt, bias=EPS, scale=1.0),
                 reads=[self.b_ps[pb]], writes=[brs])
            P.op("dve", lambda e: e.reciprocal(out=rs[:], in_=rs[:]), reads=[brs], writes=[brs])
            for kc in range(KC):
                tf = self.tmpf[kc % 2]
                btf = self.b_tmpf[kc % 2]
                P.op("dve", lambda e, tf=tf, kc=kc: e.scalar_tensor_tensor(
                    out=tf[:], in0=self.xT[:, kc, tok], scalar=self.aco[:, i0 + kc:i0 + kc + 1],
                    op0=ALU.mult, in1=rs[:], op1=ALU.mult),
                    reads=[self.b_xT[kc][t], brs, self.b_mod], writes=[btf])
                P.op("act", lambda e, tf=tf, kc=kc: e.activation(
                    out=self.hb[:, kc, tok], in_=tf[:], func=AF.Identity,
                    bias=self.modT[:, l, sub * 24 + kc, s:s + 1], scale=1.0),
                    reads=[btf, self.b_mod], writes=[self.b_hb[t]])

    def load_w(self, dram_ap, ncols, npart=128):
        P = self.P
        slot = self.wslot
        self.wslot ^= 1
        P.dma(self.st32[slot][0:npart, 0:ncols], dram_ap, writes=[self.b_st32[slot]])
        P.op("pool", lambda e: e.tensor_copy(out=self.stb[slot][0:npart, 0:ncols],
                                             in_=self.st32[slot][0:npart, 0:ncols]),
             reads=[self.b_st32[slot]], writes=[self.b_stb[slot]])
        return self.stb[slot], self.b_stb[slot]

    @staticmethod
    def handover(olds, news):
        toks = []
        for b in olds:
            if b.last_w is not None:
                toks.append(b.last_w)
            toks.extend(b.readers)
        toks = list(dict.fromkeys(toks))
        for nb in news:
            nb.readers.extend(toks)

    def emit_ffn(self, l, f, s):
        P = self.P
        i0 = self.co_idx(l, 0 if f == 0 else 2, s)
        for half in range(2):
            for j in range(NJ):
                w, bw = self.load_w(self.d_win[l, f, j], 8 * 256)
                for tt in range(2):
                    t = half * 2 + tt
                    tok = slice(t * 512, (t + 1) * 512)
                    pa = 2 * (tt % 2)
                    pbk = pa + 1
                    for which, bank in ((0, pa), (1, pbk)):
                        for kc in range(KC):
                            P.op("pe", lambda e, kc=kc, which=which, bank=bank, w=w: e.matmul(
                                self.psum[:, bank, :],
                                lhsT=w[:, kc * 256 + which * 128:kc * 256 + which * 128 + 128],
                                rhs=self.hb[:, kc, tok], start=(kc == 0), stop=(kc == KC - 1)),
                                reads=[bw, self.b_hb[t]], writes=[self.b_ps[bank]])
                    sa = self.tmpf[2 + tt]
                    bsa = self.b_tmpf[2 + tt]
                    P.op("act", lambda e, sa=sa, pa=pa: e.activation(
                        out=sa[:], in_=self.psum[:, pa, :], func=AF.Silu),
                        reads=[self.b_ps[pa]], writes=[bsa])
                    P.op("dve", lambda e, sa=sa, pbk=pbk, j=j, tt=tt: e.tensor_tensor(
                        out=self.u[:, j, tt * 512:(tt + 1) * 512], in0=sa[:],
                        in1=self.psum[:, pbk, :], op=ALU.mult),
                        reads=[bsa, self.b_ps[pbk]], writes=[self.b_u[j][tt]])
            for c in range(KC):
                w, bw = self.load_w(self.d_wout[l, f, c], NJ * 128)
                for tt in range(2):
                    t = half * 2 + tt
                    tok = slice(t * 512, (t + 1) * 512)
                    bank = 4 + (tt % 2)
                    for j in range(NJ):
                        P.op("pe", lambda e, j=j, bank=bank, w=w, tt=tt: e.matmul(
                            self.psum[:, bank, :], lhsT=w[:, j * 128:(j + 1) * 128],
                            rhs=self.u[:, j, tt * 512:(tt + 1) * 512],
                            start=(j == 0), stop=(j == NJ - 1)),
                            reads=[bw, self.b_u[j][tt]], writes=[self.b_ps[bank]])
                    P.op("dve", lambda e, c=c, bank=bank: e.scalar_tensor_tensor(
                        out=self.xT[:, c, tok], in0=self.psum[:, bank, :],
                        scalar=self.gco[:, i0 + c:i0 + c + 1], op0=ALU.mult,
                        in1=self.xT[:, c, tok], op1=ALU.add),
                        reads=[self.b_ps[bank], self.b_mod, self.b_xT[c][t]],
                        writes=[self.b_xT[c][t]])


    def emit_rwkv(self, l, s):
        j = sum(1 for q in self.kinds[:l] if q == 0)
        import os
        stop = int(os.environ.get("RW_STOP", "9"))
        self.rwkv_phase1(l, s, j)
        if stop >= 2:
            self.rwkv_phase2(l, s, j)
        if stop >= 3:
            self.out_proj(self.dw["rw_wo"][j], l, s)

    def rwkv_phase1(self, l, s, j):
        P = self.P
        ps, bps = self.psum, self.b_ps
        vres = j > 0
        uflat = self.u[:].rearrange("p a b -> p (a b)")
        t1 = uflat[:, 0:4096].rearrange("p (a t) -> p a t", a=2)
        stg = [uflat[:, 4096 + i * 512:4608 + i * 512] for i in range(4)]
        stgf = [uflat[:, 6144 + i * 1024:7168 + i * 1024].bitcast(F32) for i in range(2)]
        v2b = uflat[:, 8192:9216]
        vgt = uflat[:, 9216:10240].bitcast(F32)
        b_t1, b_v2b, b_vgt = Buf("t1"), Buf("v2b"), Buf("vgt")
        b_stg = [Buf(f"stg{i}") for i in range(4)]
        b_stgf = [Buf(f"stgf{i}") for i in range(2)]
        news = [b_t1, b_v2b, b_vgt] + b_stg + b_stgf
        allu = [b for row in self.b_u for b in row]
        self.handover(allu, news)
        st = {"stg": 0, "stgf": 0}
        pbanks = (0, 1, 2, 3)
        tiles = [(slice(t * 512, (t + 1) * 512), 512) for t in range(4)]

        def nstg():
            i = st["stg"]
            st["stg"] = (i + 1) % 4
            return i

        def load_mix(dram_ap, C, which):
            slot = self.wslot
            self.wslot ^= 1
            n = 8 * C
            P.dma(self.st32[slot][:, 0:n], dram_ap, writes=[self.b_st32[slot]])
            src3 = self.st32[slot][:, 0:n].rearrange("p (k c) -> p k c", k=8)
            for half, vec in ((0, self.omu[:, j * 48 + which * 8:j * 48 + which * 8 + 8]),
                              (1, self.vcol(("mu", j), which * 8, 8))):
                P.op("pool", lambda e, half=half, vec=vec: e.tensor_tensor(
                    out=self.stb[slot][:, half * n:(half + 1) * n].rearrange("p (k c) -> p k c", k=8),
                    in0=src3, in1=vec.unsqueeze(2).broadcast_to([128, 8, C]), op=ALU.mult),
                    reads=[self.b_st32[slot], self.b_vecs, self.b_const], writes=[self.b_stb[slot]])
            return self.stb[slot], self.b_stb[slot]

        def mixproj(w, bw, C, col0, ncol, ti, out_ap_fn):
            tok, _ = tiles[ti]
            t0 = ti * 512
            bank = pbanks[self.bank_rr % 4]
            self.bank_rr += 1
            rd = [bw, self.b_hb[ti]] + ([self.b_hb[ti - 1]] if ti else [])
            for kc in range(KC):
                P.op("pe", lambda e, kc=kc: e.matmul(
                    ps[0:ncol, bank, :], lhsT=w[:, kc * C + col0:kc * C + col0 + ncol], rhs=self.hb[:, kc, tok],
                    start=(kc == 0), stop=False), reads=rd, writes=[bps[bank]])
            c0 = 1 if ti == 0 else 0
            for kc in range(KC):
                P.op("pe", lambda e, kc=kc: e.matmul(
                    ps[0:ncol, bank, c0:512], lhsT=w[:, 8 * C + kc * C + col0:8 * C + kc * C + col0 + ncol],
                    rhs=self.hb[:, kc, t0 - 1 + c0:t0 + 511], start=False, stop=(kc == KC - 1)),
                    reads=rd, writes=[bps[bank]])
            return bank

        def bigproj(wkey, which, consume):
            for fc in range(8):
                w, bw = load_mix(self.dw[wkey][j][fc], 128, which)
                for ti in range(4):
                    bank = mixproj(w, bw, 128, 0, 128, ti, None)
                    consume(fc, ti, tiles[ti][0], bank)

        def to_scratch(names, func=AF.Copy, bias_key=None):
            def consume(fc, ti, tok, bank):
                i = nstg()
                kw = {}
                if bias_key is not None:
                    kw = dict(bias=self.vcol(bias_key, fc), scale=1.0)
                P.op("act", lambda e: e.activation(out=stg[i][:], in_=ps[:, bank, :], func=func, **kw),
                     reads=[bps[bank], self.b_vecs], writes=[b_stg[i]])
                for nm in names:
                    P.dma(self.scr[nm][fc, :, tok], stg[i][:], reads=[b_stg[i]], writes=[self.b_scr[nm][ti]])
            return consume

        def lora1(wkey, R, which, func, dst):
            w, bw = load_mix(self.dw[wkey][j], R, which)
            for (c0, cn, slot) in dst:
                for ti in range(4):
                    bank = mixproj(w, bw, R, c0, cn, ti, None)
                    P.op("act", lambda e, cn=cn, slot=slot, ti=ti, bank=bank: e.activation(
                        out=t1[0:cn, slot, tiles[ti][0]], in_=ps[0:cn, bank, :], func=func),
                        reads=[bps[bank]], writes=[b_t1])

        def lora2(parts, consume):
            ws = [self.load_w(self.dw[wk][j], 1024, npart=npart) + (npart, slot) for (wk, npart, slot) in parts]
            for fc in range(8):
                for ti in range(4):
                    tok = tiles[ti][0]
                    bank = pbanks[self.bank_rr % 4]
                    self.bank_rr += 1
                    for pi, (w2, bw2, npart, slot) in enumerate(ws):
                        P.op("pe", lambda e, w2=w2, npart=npart, slot=slot, pi=pi: e.matmul(
                            ps[:, bank, :], lhsT=w2[0:npart, fc * 128:(fc + 1) * 128],
                            rhs=t1[0:npart, slot, tok], start=(pi == 0), stop=(pi == len(ws) - 1)),
                            reads=[bw2, b_t1], writes=[bps[bank]])
                    consume(fc, ti, tok, bank)

        bigproj("rw_wr", 0, to_scratch(["R"]))
        bigproj("rw_wk", 2, to_scratch(["K"]))
        if vres:
            lora1("rw_v1", 32, 3, AF.Copy, [(0, 32, 0)])
            wv2, bwv2 = self.load_w(self.dw["rw_v2"][j], 1024, npart=32)
            P.op("pool", lambda e: e.tensor_copy(out=v2b[0:32, :], in_=wv2[0:32, 0:1024]), reads=[bwv2], writes=[b_v2b])

            def v_consume(fc, ti, tok, bank):
                ia, ib = nstg(), nstg()
                fi = st["stgf"]
                st["stgf"] ^= 1
                P.dma(stg[ia][:], self.scr["VF"][fc, :, tok], reads=[self.b_scr["VF"][ti]], writes=[b_stg[ia]])
                P.op("pe", lambda e: e.matmul(ps[:, 4, :], lhsT=v2b[0:32, fc * 128:(fc + 1) * 128], rhs=t1[0:32, 0, tok],
                                              start=True, stop=True), reads=[b_v2b, b_t1], writes=[bps[4]])
                P.op("act", lambda e: e.activation(out=vgt, in_=ps[:, 4, :], func=AF.Sigmoid,
                                                   bias=self.vcol(("v0", 1), fc), scale=1.0),
                     reads=[bps[4], self.b_vecs], writes=[b_vgt])
                P.op("dve", lambda e: e.tensor_tensor(out=stgf[fi][:], in0=stg[ia][:], in1=ps[:, bank, :], op=ALU.subtract),
                     reads=[b_stg[ia], bps[bank]], writes=[b_stgf[fi]])
                P.op("dve", lambda e: e.tensor_tensor(out=stgf[fi][:], in0=stgf[fi][:], in1=vgt, op=ALU.mult),
                     reads=[b_vgt], writes=[b_stgf[fi]])
                P.op("dve", lambda e: e.tensor_tensor(out=stg[ib][:], in0=stgf[fi][:], in1=ps[:, bank, :], op=ALU.add),
                     reads=[b_stgf[fi], bps[bank]], writes=[b_stg[ib]])
                P.dma(self.scr["V"][fc, :, tok], stg[ib][:], reads=[b_stg[ib]], writes=[self.b_scr["V"][ti]])
            bigproj("rw_wv", 3, v_consume)
        else:
            bigproj("rw_wv", 3, to_scratch(["V", "VF"]))
        lora1("rw_w1", 64, 1, AF.Tanh, [(0, 64, 0)])

        def ld_consume(fc, ti, tok, bank):
            fi = st["stgf"]
            st["stgf"] ^= 1
            P.op("act", lambda e: e.activation(out=stgf[fi][:], in_=ps[:, bank, :], func=AF.Sigmoid,
                                               bias=self.vcol(("w0", j), fc), scale=1.0),
                 reads=[bps[bank], self.b_vecs], writes=[b_stgf[fi]])
            P.op("pool", lambda e: e.tensor_scalar(out=stgf[fi][:], in0=stgf[fi][:], scalar1=-math.exp(-0.5),
                                                   scalar2=None, op0=ALU.mult), reads=[], writes=[b_stgf[fi]])
            P.dma(self.scr["LD"][fc, :, tok], stgf[fi][:], reads=[b_stgf[fi]], writes=[self.b_scr["LD"][ti]])
        lora2([("rw_w2", 64, 0)], ld_consume)
        lora1("rw_a1", 64, 4, AF.Copy, [(0, 64, 1)])
        lora2([("rw_a2", 64, 1)], to_scratch(["A"], AF.Sigmoid, ("a0", j)))
        lora1("rw_g1", 160, 5, AF.Sigmoid, [(0, 128, 0), (128, 32, 1)])
        lora2([("rw_g2a", 128, 0), ("rw_g2b", 32, 1)], to_scratch(["G"]))
        self.handover(news, allu)

    def rwkv_phase2(self, l, s, j):
        P = self.P
        uflat = self.u[:].rearrange("p a b -> p (a b)")
        v3 = lambda a: a.rearrange("p (c t) -> p c t", c=8)
        Rl, Kl, Al, Vl = [v3(uflat[:, i * 1024:(i + 1) * 1024]) for i in range(4)]
        LDl = v3(uflat[:, 4096:6144].bitcast(F32))
        AR = uflat[:, 6144:8192].rearrange("p (k c n) -> p k c n", k=8, c=2)
        BK = uflat[:, 8192:10240].rearrange("p (k c n) -> p k c n", k=8, c=2)
        bonus = v3(uflat[:, 10240:11264])
        BKT = uflat[:, 11264:13312].rearrange("p (c h n) -> p c h n", c=2, h=16)
        ZT = uflat[:, 13312:15360].rearrange("p (c h n) -> p c h n", c=2, h=16)
        AMs = uflat[:, 15360:17408].rearrange("p (h n) -> p h n", h=16)
        ATs = uflat[:, 17408:18432].rearrange("p (h n) -> p h n", h=16)
        Tm = uflat[:, 18432:19456].rearrange("p (h n) -> p h n", h=16)
        Gl = v3(uflat[:, 19456:20480])
        PLt = uflat[:, 20480:20512].bitcast(F32).rearrange("p (k c) -> p k c", k=8)
        WTs = uflat[:, 20544:21568].rearrange("p (h n) -> p h n", h=16)
        s0 = self.st32[0]
        SS = s0[:, 0:512].rearrange("p (k n) -> p k n", k=8)
        Yf = s0[:, 512:1536].rearrange("p (k n) -> p k n", k=8)
        SSb = s0[:, 1536:1792].bitcast(BF16).rearrange("p (k n) -> p k n", k=8)
        NSETS = 1
        s1f = self.st32[1]
        sb1f = self.stb[1][:, :].bitcast(F32)
        sb0f = self.stb[0][:, 768:2816].bitcast(F32)
        pool_f = [s0[:, 1792 + i * 128:1920 + i * 128] for i in range(8)] + \
                 [s1f[:, 2048 + i * 128:2176 + i * 128] for i in range(6)] + \
                 [sb1f[:, i * 128:(i + 1) * 128] for i in range(11)] + \
                 [sb0f[:, i * 128:(i + 1) * 128] for i in range(8)]
        Tsets = [pool_f[i * 8:(i + 1) * 8] for i in range(NSETS)]
        Tbsets = [[self.stb[0][:, (2 * i + q) * 128:(2 * i + q + 1) * 128] for q in range(2)] for i in range(NSETS)]
        s1 = self.st32[1][:, :].bitcast(BF16)
        Apow = [[s1[:, (2 * a + b) * 1024:(2 * a + b + 1) * 1024].rearrange("p (h n) -> p h n", h=16)
                 for b in range(2)] for a in range(2)]
        G1 = v3(self.stb[1][:, 0:2048].bitcast(F32))
        G2 = v3(self.stb[0][:, 768:2816].bitcast(F32))
        Yb = v3(self.st32[1][:, 2048:2560].bitcast(BF16))
        names = ["in", "ld", "AR", "BK", "bonus", "PL", "BKT", "ZTu", "ZTv", "AMs", "ATs", "Tm", "WTs",
                 "SS", "SSb", "Yf", "Gl", "G1", "G2", "Yb", "vin"] + [f"Tb{i}_{q}" for i in range(3) for q in range(2)] + \
                [f"T{i}_{q}" for i in range(3) for q in range(8)] + ["Ap00", "Ap01", "Ap10", "Ap11"]
        B = {n: Buf(n) for n in names}
        allold = [b for row in self.b_u for b in row] + self.b_st32 + self.b_stb
        self.handover(allold, list(B.values()))
        bAp = [[B["Ap00"], B["Ap01"]], [B["Ap10"], B["Ap11"]]]
        ps = self.psum
        bps = self.b_ps
        identb = self.cst("ident")
        psbf = lambda bank: ps[:, bank, :].bitcast(BF16)
        PAv = ps[:, 0:4, :].rearrange("p b (h n) -> p (b h) n", n=128)
        v64 = lambda b0: ps[0:64, b0:b0 + 2, :].rearrange("p b (h n) -> p (b h) n", n=64)
        mAb = self.cst("mA").unsqueeze(1).broadcast_to([128, 16, 128])
        mTb = self.cst("mT", 64)[0:64, :].unsqueeze(1).broadcast_to([64, 16, 64])
        idb = identb[0:64, 0:64].unsqueeze(1).broadcast_to([64, 16, 64])
        vc = lambda key, kc: self.vcol((key, j), kc)

        P.op("pool", lambda e: e.memset(SS[:], 0.0), writes=[B["SS"]])
        P.op("pool", lambda e: e.memset(SSb[:], 0.0), writes=[B["SSb"]])

        for ti in range(16):
            t0 = ti * 128
            tk = slice(t0, t0 + 128)
            tq = ti // 4
            for nm, dst in (("R", Rl), ("K", Kl), ("A", Al), ("V", Vl)):
                P.dma(dst[:, :, :], self.scr[nm][:, :, tk].rearrange("k p t -> p k t"),
                      reads=[self.b_scr[nm][tq]], writes=[B["vin"] if nm == "V" else B["in"]])
            P.dma(LDl[:, :, :], self.scr["LD"][:, :, tk].rearrange("k p t -> p k t"),
                  reads=[self.b_scr["LD"][tq]], writes=[B["ld"]])
            for kc in range(KC):
                K_, R_, A_, V_, LD_ = Kl[:, kc, :], Rl[:, kc, :], Al[:, kc, :], Vl[:, kc, :], LDl[:, kc, :]
                c2 = lambda a: a.rearrange("p (c n) -> p c n", c=2)
                si = (ti * KC + kc) % NSETS
                T = Tsets[si]
                Tb = Tbsets[si]
                bT = [B[f"T{si}_{q}"] for q in range(8)]
                bTb = [B[f"Tb{si}_{q}"] for q in range(2)]
                P.op("dve", lambda e: e.tensor_scalar(out=T[0], in0=K_, scalar1=vc("k_k", kc), scalar2=None,
                                                      op0=ALU.mult), reads=[B["in"], self.b_vecs], writes=[bT[0]])
                P.op("act", lambda e: e.activation(out=Tb[0], in_=T[0], func=AF.Square),
                     reads=[bT[0]], writes=[bTb[0]])
                P.op("pe", lambda e: e.matmul(ps[:, 6, 0:128], lhsT=self.cst("blk1"), rhs=Tb[0], start=True, stop=True),
                     reads=[bTb[0], self.b_const], writes=[bps[6]])
                P.op("act", lambda e: e.activation(out=T[1], in_=ps[:, 6, 0:128], func=AF.Sqrt),
                     reads=[bps[6]], writes=[bT[1]])
                P.op("dve", lambda e: e.tensor_scalar(out=T[1], in0=T[1], scalar1=1e-12, scalar2=None, op0=ALU.max),
                     reads=[], writes=[bT[1]])
                P.op("dve", lambda e: e.reciprocal(out=T[1], in_=T[1]), reads=[], writes=[bT[1]])
                P.op("dve", lambda e: e.tensor_tensor(out=T[2], in0=T[0], in1=T[1], op=ALU.mult),
                     reads=[bT[0], bT[1]], writes=[bT[2]])
                P.op("pool", lambda e: e.tensor_scalar(out=T[3], in0=A_, scalar1=-1.0, scalar2=vc("k_a", kc),
                                                       op0=ALU.add, op1=ALU.mult),
                     reads=[B["in"], self.b_vecs], writes=[bT[3]])
                P.op("dve", lambda e: e.scalar_tensor_tensor(out=T[3], in0=T[3], scalar=1.0, op0=ALU.add,
                                                             in1=K_, op1=ALU.mult),
                     reads=[bT[3], B["in"]], writes=[bT[3]])
                P.op("dve", lambda e: e.scalar_tensor_tensor(out=Tb[1], in0=R_, scalar=vc("r_k", kc), op0=ALU.mult,
                                                             in1=T[3], op1=ALU.mult),
                     reads=[bT[3], B["in"], self.b_vecs], writes=[bTb[1]])
                P.op("pe", lambda e: e.matmul(ps[:, 7, 0:128], lhsT=self.cst("blk1"), rhs=Tb[1], start=True, stop=True),
                     reads=[bTb[1], self.b_const], writes=[bps[7]])
                P.op("dve", lambda e: e.tensor_tensor(out=bonus[:, kc, :], in0=ps[:, 7, 0:128], in1=V_, op=ALU.mult),
                     reads=[bps[7], B["vin"]], writes=[B["bonus"]])
                P.op("dve", lambda e: e.tensor_tensor_scan(out=T[4], data0=self.cst("rmask", bf=False), data1=LD_,
                                                           initial=0.0, op0=ALU.mult, op1=ALU.add),
                     reads=[B["ld"], self.b_const], writes=[bT[4]])
                P.op("act", lambda e: e.activation(out=T[5], in_=T[4], func=AF.Exp), reads=[bT[4]], writes=[bT[5]])
                P.op("act", lambda e: e.activation(out=PLt[:, kc, :], in_=T[5][:, 63:128:64], func=AF.Copy),
                     reads=[bT[5]], writes=[B["PL"]])
                P.op("dve", lambda e: e.tensor_tensor(out=AR[:, kc, :, 64:128], in0=c2(R_), in1=c2(T[5]), op=ALU.mult),
                     reads=[bT[5], B["in"]], writes=[B["AR"]])
                P.op("act", lambda e: e.activation(out=T[6], in_=T[4], func=AF.Exp, scale=-1.0),
                     reads=[bT[4]], writes=[bT[6]])
                P.op("pool", lambda e: e.tensor_tensor(out=BK[:, kc, :, 64:128], in0=c2(T[3]), in1=c2(T[6]), op=ALU.mult),
                     reads=[bT[3], bT[6]], writes=[B["BK"]])
                P.op("pool", lambda e: e.tensor_tensor(out=T[7], in0=T[2], in1=A_, op=ALU.mult),
                     reads=[bT[2], B["in"]], writes=[bT[7]])
                P.op("pool", lambda e: e.tensor_tensor(out=BK[:, kc, :, 0:64], in0=c2(T[7]), in1=c2(T[6]), op=ALU.mult),
                     reads=[bT[7], bT[6]], writes=[B["BK"]])
                P.op("pool", lambda e: e.tensor_tensor(out=T[4], in0=T[4], in1=LD_, op=ALU.subtract),
                     reads=[B["ld"], bT[5], bT[6]], writes=[bT[4]])
                P.op("act", lambda e: e.activation(out=T[5], in_=T[4], func=AF.Exp),
                     reads=[bT[4], B["AR"], B["PL"]], writes=[bT[5]])
                P.op("dve", lambda e: e.scalar_tensor_tensor(out=AR[:, kc, :, 0:64], in0=c2(T[2]), scalar=-1.0,
                                                             op0=ALU.mult, in1=c2(T[5]), op1=ALU.mult),
                     reads=[bT[2], bT[5]], writes=[B["AR"]])
            import os
            if int(os.environ.get("RW_STOP", "9")) == 2 and os.environ.get("RW_PREP_ONLY"):
                continue
            for c in range(2):
                for hidx in range(16):
                    par, kc = hidx // 8, hidx % 8
                    pb = par * 64
                    P.op("pe", lambda e, par=par, kc=kc, pb=pb, c=c: e.transpose(
                        psbf(6 + par)[:, kc * 64:(kc + 1) * 64], BK[pb:pb + 64, kc, c, :],
                        identb[pb:pb + 64, pb:pb + 64]),
                        reads=[B["BK"], self.b_const], writes=[bps[6 + par]])
                for par in range(2):
                    P.op("act", lambda e, par=par, c=c: e.activation(
                        out=BKT[:, c, par * 8:(par + 1) * 8, :],
                        in_=psbf(6 + par)[:, 0:512].rearrange("p (k n) -> p k n", k=8), func=AF.Copy),
                        reads=[bps[6 + par]], writes=[B["BKT"]])
                for kc in range(KC):
                    P.op("pe", lambda e, kc=kc, c=c: e.transpose(
                        psbf(4)[64:128, kc * 128:(kc + 1) * 128], Vl[:, kc, c * 64:(c + 1) * 64], identb),
                        reads=[B["vin"], self.b_const], writes=[bps[4]])
                P.op("dve", lambda e, c=c: e.tensor_copy(
                    out=ZT[64:128, c, :, :].rearrange("p (par k) n -> p k par n", par=2),
                    in_=psbf(4)[64:128, :].rearrange("p (k par n) -> p k par n", k=8, par=2)),
                    reads=[bps[4]], writes=[B["ZTv"]])
            for c in range(2):
                hd = lambda hidx: (hidx // 8, hidx % 8, (hidx // 8) * 64)
                for hidx in range(16):
                    par, kc, pb = hd(hidx)
                    P.op("pe", lambda e, hidx=hidx, kc=kc, pb=pb: e.matmul(
                        PAv[:, hidx, :], lhsT=BK[pb:pb + 64, kc, c, :], rhs=AR[pb:pb + 64, kc, c, :],
                        start=True, stop=True), reads=[B["BK"], B["AR"]], writes=[bps[hidx // 4]])
                for hidx in range(16):
                    par, kc, pb = hd(hidx)
                    P.op("pe", lambda e, par=par, kc=kc, pb=pb: e.matmul(
                        ps[0:64, 4 + par, kc * 64:(kc + 1) * 64], lhsT=AR[pb:pb + 64, kc, c, 0:64],
                        rhs=BK[pb:pb + 64, kc, c, 0:64], start=True, stop=True),
                        reads=[B["BK"], B["AR"]], writes=[bps[4 + par]])
                P.op("dve", lambda e: e.tensor_tensor(out=AMs[:, :, :], in0=PAv, in1=mAb, op=ALU.mult),
                     reads=[bps[0], bps[1], bps[2], bps[3], self.b_const], writes=[B["AMs"]])
                P.op("dve", lambda e: e.tensor_tensor(out=ATs[0:64, :, :], in0=v64(4), in1=mTb, op=ALU.mult),
                     reads=[bps[4], bps[5], self.b_const], writes=[B["ATs"]])
                P.op("pool", lambda e: e.tensor_tensor(out=Tm[0:64, :, :], in0=AMs[0:64, :, 0:64], in1=idb, op=ALU.add),
                     reads=[B["AMs"], self.b_const], writes=[B["Tm"]])
                pA = lambda hidx: AMs[0:64, hidx, 0:64]
                pAT = lambda hidx: ATs[0:64, hidx, :]
                bA, bAT = B["AMs"], B["ATs"]
                for jj in range(1, 6):
                    a = jj % 2
                    if jj < 5:
                        for hidx in range(16):
                            P.op("pe", lambda e, hidx=hidx, pA=pA, pAT=pAT: e.matmul(
                                v64(0)[:, hidx, :], lhsT=pAT(hidx), rhs=pA(hidx), start=True, stop=True),
                                reads=[bA, bAT], writes=[bps[hidx // 8]])
                    for hidx in range(16):
                        P.op("pe", lambda e, hidx=hidx, pA=pA, pAT=pAT: e.matmul(
                            v64(2)[:, hidx, :], lhsT=pA(hidx), rhs=pAT(hidx), start=True, stop=True),
                            reads=[bA, bAT], writes=[bps[2 + hidx // 8]])
                    if jj < 5:
                        P.op("act", lambda e, a=a: e.activation(out=Apow[a][0][0:64, :, :], in_=v64(0), func=AF.Copy),
                             reads=[bps[0], bps[1]], writes=[bAp[a][0]])
                    P.op("dve", lambda e, a=a: e.tensor_copy(out=Apow[a][1][0:64, :, :], in_=v64(2)),
                         reads=[bps[2], bps[3]], writes=[bAp[a][1]])
                    for hidx in range(16):
                        P.op("pe", lambda e, hidx=hidx, a=a: e.matmul(
                            v64(4)[:, hidx, :], lhsT=Apow[a][1][0:64, hidx, :], rhs=Tm[0:64, hidx, :],
                            start=True, stop=True), reads=[bAp[a][1], B["Tm"]], writes=[bps[4 + hidx // 8]])
                    P.op("dve", lambda e: e.tensor_tensor(out=Tm[0:64, :, :], in0=v64(4), in1=Tm[0:64, :, :], op=ALU.add),
                         reads=[bps[4], bps[5]], writes=[B["Tm"]])
                    pA = (lambda a: (lambda hidx: Apow[a][0][0:64, hidx, :]))(a)
                    pAT = (lambda a: (lambda hidx: Apow[a][1][0:64, hidx, :]))(a)
                    bA, bAT = bAp[a][0], bAp[a][1]
                for hidx in range(16):
                    par, kc, pb = hd(hidx)
                    P.op("pe", lambda e, hidx=hidx, kc=kc, pb=pb: e.matmul(
                        v64(6)[:, hidx, :], lhsT=AR[pb:pb + 64, kc, c, 0:64], rhs=SSb[pb:pb + 64, kc, :],
                        start=True, stop=True), reads=[B["AR"], B["SSb"]], writes=[bps[6 + hidx // 8]])
                for hidx in range(16):
                    P.op("pe", lambda e, hidx=hidx: e.matmul(
                        v64(0)[:, hidx, :], lhsT=AMs[64:128, hidx, 0:64], rhs=ZT[64:128, c, hidx, :],
                        start=True, stop=True), reads=[B["AMs"], B["ZTv"]], writes=[bps[hidx // 8]])
                P.op("act", lambda e: e.activation(out=WTs[0:64, :, :], in_=v64(6), func=AF.Copy),
                     reads=[bps[6], bps[7]], writes=[B["WTs"]])
                P.op("dve", lambda e: e.tensor_tensor(out=WTs[0:64, :, :], in0=v64(0), in1=WTs[0:64, :, :], op=ALU.add),
                     reads=[bps[0], bps[1]], writes=[B["WTs"]])
                for hidx in range(16):
                    P.op("pe", lambda e, hidx=hidx: e.matmul(
                        v64(2)[:, hidx, :], lhsT=Tm[0:64, hidx, :], rhs=WTs[0:64, hidx, :], start=True, stop=True),
                        reads=[B["Tm"], B["WTs"]], writes=[bps[2 + hidx // 8]])
                P.op("act", lambda e: e.activation(out=ZT[0:64, c, :, :], in_=v64(2), func=AF.Copy),
                     reads=[bps[2], bps[3]], writes=[B["ZTu"]])
                for hidx in range(16):
                    par, kc, pb = hd(hidx)
                    P.op("pe", lambda e, par=par, kc=kc, pb=pb: e.matmul(
                        ps[pb:pb + 64, 6 + par, kc * 64:(kc + 1) * 64], lhsT=SSb[pb:pb + 64, kc, :],
                        rhs=AR[pb:pb + 64, kc, c, 64:128], start=True, stop=True),
                        reads=[B["AR"], B["SSb"]], writes=[bps[6 + par]])
                for hidx in range(16):
                    par, kc, pb = hd(hidx)
                    P.op("pe", lambda e, hidx=hidx, kc=kc, pb=pb: e.matmul(
                        ps[pb:pb + 64, 4, kc * 64:(kc + 1) * 64], lhsT=ZT[:, c, hidx, :], rhs=AMs[:, hidx, 64:128],
                        start=True, stop=True), reads=[B["ZTu"], B["ZTv"], B["AMs"]], writes=[bps[4]])
                ysl = slice(c * 64, (c + 1) * 64)
                k8 = lambda a: a.rearrange("p (k n) -> p k n", k=8)
                P.op("act", lambda e: e.activation(out=Yf[:, :, ysl], in_=k8(ps[:, 4, :]), func=AF.Copy),
                     reads=[bps[4]], writes=[B["Yf"]])
                P.op("dve", lambda e: e.tensor_tensor(out=Yf[0:64, :, ysl], in0=k8(ps[0:64, 6, :]), in1=Yf[0:64, :, ysl],
                                                      op=ALU.add), reads=[bps[6]], writes=[B["Yf"]])
                P.op("dve", lambda e: e.tensor_tensor(out=Yf[64:128, :, ysl], in0=k8(ps[64:128, 7, :]),
                                                      in1=Yf[64:128, :, ysl], op=ALU.add),
                     reads=[bps[7]], writes=[B["Yf"]])
                for hidx in range(16):
                    par, kc, pb = hd(hidx)
                    P.op("pe", lambda e, hidx=hidx, kc=kc, pb=pb: e.matmul(
                        ps[pb:pb + 64, 5, kc * 64:(kc + 1) * 64], lhsT=BKT[:, c, hidx, :], rhs=ZT[:, c, hidx, :],
                        start=True, stop=True), reads=[B["BKT"], B["ZTu"], B["ZTv"]], writes=[bps[5]])
                P.op("dve", lambda e: e.tensor_tensor(out=SS[:, :, :], in0=k8(ps[:, 5, :]), in1=SS[:, :, :], op=ALU.add),
                     reads=[bps[5]], writes=[B["SS"]])
                P.op("dve", lambda e, c=c: e.tensor_tensor(out=SS[:, :, :], in0=SS[:, :, :],
                                                           in1=PLt[:, :, c:c + 1].broadcast_to([128, 8, 64]), op=ALU.mult),
                     reads=[B["PL"]], writes=[B["SS"]])
                P.op("act", lambda e: e.activation(out=SSb[:, :, :], in_=SS[:, :, :], func=AF.Copy),
                     reads=[B["SS"]], writes=[B["SSb"]])
            P.dma(Gl[:, :, :], self.scr["G"][:, :, tk].rearrange("k p t -> p k t"),
                  reads=[self.b_scr["G"][tq]], writes=[B["Gl"]])
            f2 = lambda a, h: a[:, 4 * h:4 * h + 4, :]
            P.op("act", lambda e: e.activation(out=Yb[:, :, :], in_=Yf[:, :, :], func=AF.Copy),
                 reads=[B["Yf"]], writes=[B["Yb"]])
            for h2 in range(2):
                P.op("pe", lambda e, h2=h2: e.matmul(ps[:, 6 + h2, :], lhsT=self.cst("blk64"), rhs=f2(Yb, h2),
                                                     start=True, stop=True),
                     reads=[B["Yb"], self.b_const], writes=[bps[6 + h2]])
            P.op("dve", lambda e: e.tensor_tensor(out=G1[:, :, :], in0=Yf[:, :, :],
                                                  in1=ps[:, 6:8, :].rearrange("p b (k n) -> p (b k) n", n=128),
                                                  op=ALU.subtract),
                 reads=[B["Yf"], bps[6], bps[7]], writes=[B["G1"]])
            P.op("act", lambda e: e.activation(out=Yb[:, :, :], in_=G1[:, :, :], func=AF.Square),
                 reads=[B["G1"]], writes=[B["Yb"]])
            for h2 in range(2):
                P.op("pe", lambda e, h2=h2: e.matmul(ps[:, 6 + h2, :], lhsT=self.cst("blk64"), rhs=f2(Yb, h2),
                                                     start=True, stop=True),
                     reads=[B["Yb"], self.b_const], writes=[bps[6 + h2]])
            P.op("act", lambda e: e.activation(out=G2[:, :, :],
                                               in_=ps[:, 6:8, :].rearrange("p b (k n) -> p (b k) n", n=128),
                                               func=AF.Sqrt, bias=64e-5, scale=1.0),
                 reads=[bps[6], bps[7]], writes=[B["G2"]])
            P.op("dve", lambda e: e.reciprocal(out=G2[:, :, :], in_=G2[:, :, :]), reads=[], writes=[B["G2"]])
            P.op("dve", lambda e: e.tensor_tensor(out=G1[:, :, :], in0=G1[:, :, :], in1=G2[:, :, :], op=ALU.mult),
                 reads=[B["G2"]], writes=[B["G1"]])
            for kc in range(KC):
                P.op("pool", lambda e, kc=kc: e.tensor_scalar(out=G1[:, kc, :], in0=G1[:, kc, :],
                                                              scalar1=vc("ln_w", kc), scalar2=vc("ln_b", kc),
                                                              op0=ALU.mult, op1=ALU.add),
                     reads=[self.b_vecs], writes=[B["G1"]])
            P.op("pool", lambda e: e.tensor_tensor(out=G1[:, :, :], in0=G1[:, :, :], in1=bonus[:, :, :], op=ALU.add),
                 reads=[B["bonus"]], writes=[B["G1"]])
            P.op("dve", lambda e: e.tensor_tensor(out=self.hb[:, :, tk], in0=G1[:, :, :], in1=Gl[:, :, :], op=ALU.mult),
                 reads=[B["G1"], B["Gl"]], writes=[self.b_hb[tq]])
        self.handover(list(B.values()), allold)


    def emit_mamba(self, l, s):
        P = self.P
        ps, bps = self.psum, self.b_ps
        uflat = self.u[:].rearrange("p a b -> p (a b)")
        allu = [b for row in self.b_u for b in row]
        pre = uflat[:, 0:4104].bitcast(F32)
        acc = uflat[:, 4104:8200].bitcast(F32)
        ob = [uflat[:, 8200 + i * 2048:10248 + i * 2048] for i in range(2)]
        wdt_sb = uflat[:, 12296:12552]
        b_pre, b_acc, b_wdt = Buf("pre"), Buf("acc"), Buf("wdt")
        b_ob = [Buf("ob0"), Buf("ob1")]
        news = [b_pre, b_acc, b_wdt] + b_ob
        self.handover(allu, news)
        P.op("pool", lambda e: e.memset(pre[:, 0:3], 0.0), writes=[b_pre])
        tiles = [(slice(t * 512, (t + 1) * 512), 512) for t in range(4)]
        obi = [0]

        def cons_z(fc, ti, tok, bank):
            if ti == 0:
                obi[0] ^= 1
            i = obi[0]
            P.op("act", lambda e: e.activation(out=ob[i][:, tok], in_=ps[:, bank, :], func=AF.Silu),
                 reads=[bps[bank]], writes=[b_ob[i]])
            if ti == 3:
                P.dma(self.scr["MZ"][fc], ob[i][:], reads=[b_ob[i]], writes=[self.b_scr["MZ"][0]])

        def cons_conv(fc, ti, tok, bank):
            ch = fc
            P.op("act", lambda e: e.activation(out=pre[:, 3 + ti * 512:3 + (ti + 1) * 512], in_=ps[:, bank, :],
                                               func=AF.Copy), reads=[bps[bank]], writes=[b_pre])
            if ti == 3:
                obi[0] ^= 1
                i = obi[0]
                P.op("dve", lambda e: e.tensor_scalar(out=acc[:, :], in0=pre[:, 3:2051], scalar1=self.vcol(("cw", 3), ch),
                                                      scalar2=None, op0=ALU.mult),
                     reads=[b_pre, self.b_vecs], writes=[b_acc])
                for tap in (2, 1, 0):
                    P.op("dve", lambda e, tap=tap: e.scalar_tensor_tensor(
                        out=acc[:, :], in0=pre[:, tap:tap + 2048], scalar=self.vcol(("cw", tap), ch), op0=ALU.mult,
                        in1=acc[:, :], op1=ALU.add), reads=[b_pre, self.b_vecs], writes=[b_acc])
                P.op("act", lambda e: e.activation(out=ob[i][:], in_=acc[:, :], func=AF.Silu,
                                                   bias=self.vcol(("cbias",), ch), scale=1.0),
                     reads=[b_acc, self.b_vecs], writes=[b_ob[i]])
                nm, idx = ("MX", ch) if ch < 16 else (("MB", ch - 16) if ch < 24 else ("MC", ch - 24))
                P.dma(self.scr[nm][idx], ob[i][:], reads=[b_ob[i]], writes=[self.b_scr[nm][0]])

        rhs = lambda kc, tok: self.hb[:, kc, tok]
        rb = lambda ti: [self.b_hb[ti]]
        self.proj(self.dw["mb_win"][0:16], 16, 8, rhs, rb, tiles, cons_z)
        self.proj(self.dw["mb_win"][16:48], 32, 8, rhs, rb, tiles, cons_conv)
        P.dma(self.st32[0][:, 0:256], self.dw["mb_wdt"][0], writes=[self.b_st32[0]])
        P.op("pool", lambda e: e.tensor_copy(out=wdt_sb, in_=self.st32[0][:, 0:256]),
             reads=[self.b_st32[0]], writes=[b_wdt])
        for c in range(16):
            for kc in range(KC):
                P.op("pe", lambda e, c=c, kc=kc: e.matmul(
                    ps[:, 4, c * 32:(c + 1) * 32], lhsT=self.hb[:, kc, c * 128:(c + 1) * 128],
                    rhs=wdt_sb[:, kc * 32:(kc + 1) * 32], start=(kc == 0), stop=(kc == KC - 1)),
                    reads=[b_wdt, self.b_hb[c // 4]], writes=[bps[4]])
        dtk, dA, ea, dend, etot = self.mbdt, self.tmpf[0], self.tmpf[1], self.tmpf[2], self.tmpf[3]
        bdt = [self.b_mbdt] + self.b_tmpf
        c3 = lambda a: a[:, :].rearrange("p (c h) -> p c h", c=16)
        bc16 = lambda key: self.vcol((key,), 0, 32).unsqueeze(1).broadcast_to([128, 16, 32])
        P.op("dve", lambda e: e.tensor_tensor(out=c3(dtk), in0=c3(ps[:, 4, :]), in1=bc16("dtb"), op=ALU.add),
             reads=[bps[4], self.b_vecs], writes=[bdt[0]])
        P.op("act", lambda e: e.activation(out=dtk[:, :], in_=dtk[:, :], func=AF.Exp), reads=[], writes=[bdt[0]])
        P.op("act", lambda e: e.activation(out=dtk[:, :], in_=dtk[:, :], func=AF.Ln, bias=1.0, scale=1.0),
             reads=[], writes=[bdt[0]])
        P.op("act", lambda e: e.activation(out=ea[:, 0:32], in_=self.vcol(("alog",), 0, 32), func=AF.Exp),
             reads=[self.b_vecs], writes=[bdt[2]])
        P.op("dve", lambda e: e.scalar_tensor_tensor(
            out=c3(dA), in0=c3(dtk), scalar=-1.0, op0=ALU.mult,
            in1=ea[:, 0:32].unsqueeze(1).broadcast_to([128, 16, 32]), op1=ALU.mult),
            reads=[bdt[0], bdt[2]], writes=[bdt[1]])
        P.op("pe", lambda e: e.matmul(ps[:, 5, :], lhsT=self.cst("mcur", bf=False), rhs=dA[:, :], start=True, stop=True),
             reads=[bdt[1], self.b_const], writes=[bps[5]])
        P.op("pe", lambda e: e.matmul(ps[:, 6, :], lhsT=self.cst("onesf", bf=False), rhs=dA[:, :], start=True, stop=True),
             reads=[bdt[1], self.b_const], writes=[bps[6]])
        P.op("act", lambda e: e.activation(out=ea[:, :], in_=ps[:, 5, :], func=AF.Exp), reads=[bps[5]], writes=[bdt[2]])
        P.op("act", lambda e: e.activation(out=etot[:, :], in_=ps[:, 6, :], func=AF.Exp), reads=[bps[6]], writes=[bdt[4]])
        P.op("act", lambda e: e.activation(out=dend[:, :], in_=ps[:, 5, :], func=AF.Copy), reads=[bps[5]], writes=[bdt[3]])
        P.op("dve", lambda e: e.tensor_tensor(out=dend[:, :], in0=ps[:, 6, :], in1=dend[:, :], op=ALU.subtract),
             reads=[bps[6]], writes=[bdt[3]])
        P.op("act", lambda e: e.activation(out=dend[:, :], in_=dend[:, :], func=AF.Exp), reads=[], writes=[bdt[3]])

        hflat = self.hb[:].rearrange("p a b -> p (a b)")
        Sst = hflat[:, 0:4096].bitcast(F32).rearrange("p (g n) -> p g n", g=8)
        Sb = hflat[:, 4096:6144].rearrange("p (g n) -> p g n", g=8)
        yT = hflat[:, 6144:14336].rearrange("p (c t) -> p c t", c=16)
        xsT = uflat[:, 0:2048].rearrange("p (c t) -> p c t", c=16)
        CT = uflat[:, 2048:3072].rearrange("p (g t) -> p g t", g=8)
        BT = uflat[:, 3072:4096].rearrange("p (g t) -> p g t", g=8)
        xtok = uflat[:, 4096:6144]
        Btok = uflat[:, 6144:7168].rearrange("p (g n) -> p g n", g=8)
        X = uflat[:, 7168:9216]
        Xd = uflat[:, 9216:11264]
        Yb = uflat[:, 11264:13312]
        CBm = uflat[:, 13312:14336].rearrange("p (g t) -> p g t", g=8)
        Mh = [uflat[:, 14336 + i * 128:14464 + i * 128] for i in range(2)]
        Dm = [uflat[:, 14592 + i * 256:14848 + i * 256].bitcast(F32) for i in range(2)]
        lh = [uflat[:, 15104 + i * 256:15360 + i * 256].bitcast(F32) for i in range(2)]
        Yg = uflat[:, 15616:16128].bitcast(F32)
        gz = [uflat[:, 16128 + i * 512:16640 + i * 512] for i in range(2)]
        gx = [uflat[:, 17152 + i * 512:17664 + i * 512] for i in range(2)]
        gt = [uflat[:, 18176 + i * 1024:19200 + i * 1024].bitcast(F32) for i in range(2)]
        gsq = uflat[:, 20224:20736]
        grs = uflat[:, 20736:21760].bitcast(F32)
        nm2 = ["S", "Sb", "yT", "xsT", "CT", "BT", "xtok", "Btok", "X", "Xd", "Yb", "CBm", "Mh0", "Mh1", "Dm0", "Dm1",
               "lh0", "lh1", "Yg", "gz0", "gz1", "gx0", "gx1", "gt0", "gt1", "gsq", "grs"]
        B = {n: Buf(n) for n in nm2}
        self.handover(news + self.b_hb, list(B.values()))
        identb = self.cst("ident")
        psbf = lambda bank: ps[:, bank, :].bitcast(BF16)
        P.op("pool", lambda e: e.memset(Sst[:, :, :], 0.0), writes=[B["S"]])
        P.op("pool", lambda e: e.memset(Sb[:, :, :], 0.0), writes=[B["Sb"]])
        i0 = self.co_idx(l, 1, s)
        h64 = lambda a: a.rearrange("p (h n) -> p h n", n=64)

        for c in range(16):
            tk = slice(c * 128, (c + 1) * 128)
            P.dma(xsT[:, :, :], self.scr["MX"][:, :, tk].rearrange("k p t -> p k t"),
                  reads=[self.b_scr["MX"][0]], writes=[B["xsT"]])
            P.dma(CT[:, :, :], self.scr["MC"][:, :, tk].rearrange("k p t -> p k t"),
                  reads=[self.b_scr["MC"][0]], writes=[B["CT"]])
            P.dma(BT[:, :, :], self.scr["MB"][:, :, tk].rearrange("k p t -> p k t"),
                  reads=[self.b_scr["MB"][0]], writes=[B["BT"]])
            for ch in range(16):
                P.op("pe", lambda e, ch=ch: e.transpose(psbf(ch // 8)[:, (ch % 8) * 128:(ch % 8 + 1) * 128],
                                                        xsT[:, ch, :], identb),
                     reads=[B["xsT"], self.b_const], writes=[bps[ch // 8]])
            for hf in range(2):
                P.op("act" if hf else "dve", (lambda e, hf=hf: e.activation(out=xtok[:, hf * 1024:(hf + 1) * 1024],
                     in_=psbf(hf)[:, :], func=AF.Copy)) if hf else
                     (lambda e, hf=hf: e.tensor_copy(out=xtok[:, hf * 1024:(hf + 1) * 1024], in_=psbf(hf)[:, :])),
                     reads=[bps[hf]], writes=[B["xtok"]])
            for g in range(8):
                P.op("pe", lambda e, g=g: e.transpose(psbf(2)[:, g * 128:(g + 1) * 128], BT[:, g, :], identb),
                     reads=[B["BT"], self.b_const], writes=[bps[2]])
            P.op("act", lambda e: e.activation(out=Btok[:, :, :], in_=psbf(2)[:, :].rearrange("p (g n) -> p g n", g=8),
                                               func=AF.Copy), reads=[bps[2]], writes=[B["Btok"]])
            P.op("dve", lambda e, c=c: e.tensor_tensor(
                out=h64(X), in0=h64(xtok), in1=dtk[:, c * 32:(c + 1) * 32].unsqueeze(2).broadcast_to([128, 32, 64]),
                op=ALU.mult), reads=[B["xtok"], bdt[0]], writes=[B["X"]])
            P.op("pool", lambda e, c=c: e.tensor_tensor(
                out=h64(Xd), in0=h64(X), in1=dend[:, c * 32:(c + 1) * 32].unsqueeze(2).broadcast_to([128, 32, 64]),
                op=ALU.mult), reads=[B["X"], bdt[3]], writes=[B["Xd"]])
            for g in range(8):
                P.op("pe", lambda e, g=g: e.matmul(ps[:, 6 + g // 4, (g % 4) * 128:(g % 4 + 1) * 128], lhsT=BT[:, g, :],
                                                   rhs=CT[:, g, :], start=True, stop=True),
                     reads=[B["BT"], B["CT"]], writes=[bps[6 + g // 4]])
            P.op("dve", lambda e: e.tensor_tensor(
                out=CBm[:, :, :], in0=ps[:, 6:8, :].rearrange("p b (g t) -> p (b g) t", t=128),
                in1=self.cst("mcur").unsqueeze(1).broadcast_to([128, 8, 128]), op=ALU.mult),
                reads=[bps[6], bps[7], self.b_const], writes=[B["CBm"]])
            for g in range(8):
                yb_ = 3 + (g % 2)
                for hh in range(4):
                    h = g * 4 + hh
                    a = h % 2
                    P.op("act", lambda e, a=a, h=h, c=c: e.activation(
                        out=lh[a], in_=self.cst("mprev", bf=False), func=AF.Copy,
                        scale=dA[:, c * 32 + h:c * 32 + h + 1]),
                        reads=[bdt[1], self.b_const], writes=[B[f"lh{a}"]])
                    P.op("pe", lambda e, a=a: e.matmul(ps[:, a, 0:128], lhsT=lh[a], rhs=self.cst("mcur", bf=False),
                                                       start=True, stop=True),
                         reads=[B[f"lh{a}"], self.b_const], writes=[bps[a]])
                    P.op("act", lambda e, a=a: e.activation(out=Dm[a], in_=ps[:, a, 0:128], func=AF.Exp),
                         reads=[bps[a]], writes=[B[f"Dm{a}"]])
                    P.op("dve", lambda e, a=a, g=g: e.tensor_tensor(out=Mh[a], in0=CBm[:, g, :], in1=Dm[a], op=ALU.mult),
                         reads=[B["CBm"], B[f"Dm{a}"]], writes=[B[f"Mh{a}"]])
                    P.op("pe", lambda e, a=a, h=h, hh=hh, yb_=yb_: e.matmul(
                        ps[:, yb_, hh * 64:(hh + 1) * 64], lhsT=Mh[a], rhs=X[:, h * 64:(h + 1) * 64],
                        start=True, stop=True), reads=[B[f"Mh{a}"], B["X"]], writes=[bps[yb_]])
                P.op("pe", lambda e, g=g: e.matmul(ps[:, 5, 0:256], lhsT=CT[:, g, :], rhs=Sb[:, g, :], start=True, stop=True),
                     reads=[B["CT"], B["Sb"]], writes=[bps[5]])
                P.op("dve", lambda e, g=g, c=c: e.tensor_tensor(
                    out=h64(Yg), in0=h64(ps[:, 5, 0:256]),
                    in1=ea[:, c * 32 + g * 4:c * 32 + g * 4 + 4].unsqueeze(2).broadcast_to([128, 4, 64]), op=ALU.mult),
                    reads=[bps[5], bdt[2]], writes=[B["Yg"]])
                P.op("dve", lambda e, g=g, yb_=yb_: e.tensor_tensor(out=Yb[:, g * 256:(g + 1) * 256], in0=ps[:, yb_, 0:256],
                                                                    in1=Yg, op=ALU.add),
                     reads=[bps[yb_], B["Yg"]], writes=[B["Yb"]])
                P.op("pe", lambda e, g=g: e.matmul(ps[:, 2, 0:256], lhsT=Btok[:, g, :], rhs=Xd[:, g * 256:(g + 1) * 256],
                                                   start=True, stop=True),
                     reads=[B["Btok"], B["Xd"]], writes=[bps[2]])
                P.op("pool", lambda e, g=g, c=c: e.tensor_tensor(
                    out=h64(Sst[:, g, :]), in0=h64(Sst[:, g, :]),
                    in1=etot[:, c * 32 + g * 4:c * 32 + g * 4 + 4].unsqueeze(2).broadcast_to([128, 4, 64]), op=ALU.mult),
                    reads=[bdt[4], B["Sb"]], writes=[B["S"]])
                P.op("dve", lambda e, g=g: e.tensor_tensor(out=Sst[:, g, :], in0=ps[:, 2, 0:256], in1=Sst[:, g, :], op=ALU.add),
                     reads=[bps[2]], writes=[B["S"]])
                P.op("act", lambda e, g=g: e.activation(out=Sb[:, g, :], in_=Sst[:, g, :], func=AF.Copy),
                     reads=[B["S"], bps[5]], writes=[B["Sb"]])
            cl = c % 4
            for ch in range(16):
                P.op("pe", lambda e, ch=ch: e.transpose(psbf(ch // 8)[:, (ch % 8) * 128:(ch % 8 + 1) * 128],
                                                        Yb[:, ch * 128:(ch + 1) * 128], identb),
                     reads=[B["Yb"], self.b_const], writes=[bps[ch // 8]])
            for hf in range(2):
                P.op("act", lambda e, hf=hf, cl=cl: e.activation(
                    out=yT[:, hf * 8:(hf + 1) * 8, cl * 128:(cl + 1) * 128],
                    in_=psbf(hf)[:, :].rearrange("p (k n) -> p k n", k=8), func=AF.Copy),
                    reads=[bps[hf]], writes=[B["yT"]])
            if cl != 3:
                continue
            t = c // 4
            tok = slice(t * 512, (t + 1) * 512)
            for g2 in range(8):
                for q in range(2):
                    ch = g2 * 2 + q
                    P.dma(gz[q][:], self.scr["MZ"][ch, :, tok], reads=[self.b_scr["MZ"][0]], writes=[B[f"gz{q}"]])
                    P.dma(gx[q][:], self.scr["MX"][ch, :, tok], reads=[self.b_scr["MX"][0]], writes=[B[f"gx{q}"]])
                    P.op("dve", lambda e, q=q, ch=ch: e.scalar_tensor_tensor(
                        out=gt[q], in0=gx[q][:], scalar=self.vcol(("mbD",), ch), op0=ALU.mult, in1=yT[:, ch, :], op1=ALU.add),
                        reads=[B[f"gx{q}"], B["yT"], self.b_vecs], writes=[B[f"gt{q}"]])
                    P.op("pool", lambda e, q=q: e.tensor_tensor(out=gt[q], in0=gt[q], in1=gz[q][:], op=ALU.mult),
                         reads=[B[f"gz{q}"]], writes=[B[f"gt{q}"]])
                    P.op("act", lambda e, q=q: e.activation(out=gsq, in_=gt[q], func=AF.Square),
                         reads=[B[f"gt{q}"]], writes=[B["gsq"]])
                    P.op("pe", lambda e, q=q: e.matmul(ps[:, 3, :], lhsT=self.ones_b[:], rhs=gsq, start=(q == 0), stop=(q == 1)),
                         reads=[B["gsq"], self.b_const], writes=[bps[3]])
                P.op("act", lambda e: e.activation(out=grs, in_=ps[:, 3, :], func=AF.Sqrt, bias=EPS, scale=4.0),
                     reads=[bps[3]], writes=[B["grs"]])
                P.op("dve", lambda e: e.reciprocal(out=grs, in_=grs), reads=[], writes=[B["grs"]])
                for q in range(2):
                    ch = g2 * 2 + q
                    P.op("dve", lambda e, q=q, ch=ch: e.scalar_tensor_tensor(
                        out=yT[:, ch, :], in0=gt[q], scalar=self.vcol(("mbng",), ch), op0=ALU.mult, in1=grs, op1=ALU.mult),
                        reads=[B[f"gt{q}"], B["grs"], self.b_vecs], writes=[B["yT"]])
            for fc in range(8):
                w, bw = self.load_w(self.dw["mb_wo"][fc], 2048)
                bank = 4 + (fc % 2)
                for ch in range(16):
                    P.op("pe", lambda e, ch=ch, w=w, bank=bank: e.matmul(
                        ps[:, bank, :], lhsT=w[:, ch * 128:(ch + 1) * 128], rhs=yT[:, ch, :],
                        start=(ch == 0), stop=(ch == 15)), reads=[bw, B["yT"]], writes=[bps[bank]])
                P.op("dve", lambda e, fc=fc, bank=bank: e.scalar_tensor_tensor(
                    out=self.xT[:, fc, tok], in0=ps[:, bank, :], scalar=self.gco[:, i0 + fc:i0 + fc + 1], op0=ALU.mult,
                    in1=self.xT[:, fc, tok], op1=ALU.add),
                    reads=[bps[bank], self.b_mod, self.b_xT[fc][t]], writes=[self.b_xT[fc][t]])
        self.handover(list(B.values()), allu + self.b_hb)


def make_in_maps(inp, ncores, nseq):
    H = build_host_arrays(inp)
    vecs = H["vecs"].pack()
    consts, coff = build_consts()
    x = np.asarray(inp["x"], np.float32)
    c = np.asarray(inp["c"], np.float32)
    maps = []
    for core in range(ncores):
        xs = x[core * nseq:(core + 1) * nseq]
        xT = np.ascontiguousarray(xs.transpose(0, 2, 1)).reshape(nseq, KC, 128, S)
        cs = c[core * nseq:(core + 1) * nseq]
        cT = np.ascontiguousarray(cs.reshape(nseq, KC, 128).transpose(2, 1, 0)).reshape(128, KC * nseq)
        m = {"xT": xT, "cT": cT, "vecs": vecs, "ada_w": H["ada_w"].reshape(DEPTH, 18, 128, 4096),
             "ffn_w_in": H["ffn_w_in"], "ffn_w_out": H["ffn_w_out"], "consts": consts}
        for k in W_SPECS:
            m[k] = H[k]
        maps.append(m)
    return maps, H["vecs"].off, vecs.shape[1], coff, consts.shape[1]


def run(inp, ncores=NCORES, nseq=NSEQ_CORE, n_layers=DEPTH, mixers=True, kinds=None, trace=False):
    maps, voff, nvec, coff, nconst = make_in_maps(inp, ncores, nseq)
    b = Builder(nseq, n_layers, voff, nvec, mixers=mixers, kinds=kinds, const_off=coff, nconst=nconst)
    nc = b.build()
    if trace:
        res = run_bass_kernel_spmd(nc, maps, core_ids=list(range(ncores)), trace=True)
        print('EXEC_TIME_NS', res.exec_time_ns)
    else:
        res = run_bass_kernel_spmd(nc, maps, core_ids=list(range(ncores)))
    outs = []
    for r in res.results:
        yT = r["yT"].reshape(nseq, D, S)
        outs.append(yT.transpose(0, 2, 1))
    return np.ascontiguousarray(np.concatenate(outs, axis=0))


def kernel(**inputs):
    return run(inputs)
```

```python
import math
from contextlib import ExitStack
import numpy as np
import concourse.bass as bass
import concourse.mybir as mybir
from concourse.bass_utils import run_bass_kernel_spmd

F32 = mybir.dt.float32
BF16 = mybir.dt.bfloat16
AF = mybir.ActivationFunctionType
ALU = mybir.AluOpType
AX = mybir.AxisListType

D = 1024
S = 2048
KC = 8
DFF = 2816
NJ = 22
DEPTH = 4
NCORES = 8
NSEQ_CORE = 4
EPS = 1e-6

SEM_BLOCK = 30000
ENGS = ("pe", "dve", "act", "pool", "sp")


class Buf:
    __slots__ = ("name", "last_w", "readers", "excl")

    def __init__(self, name="", excl=False):
        self.name = name
        self.last_w = None
        self.readers = []
        self.excl = excl


class Prog:
    def __init__(self, nc, stack, n_dma_sems=32):
        self.nc = nc
        self.stack = stack
        self.eng = {"pe": nc.tensor, "dve": nc.vector, "act": nc.scalar,
                    "pool": nc.gpsimd, "sp": nc.sync}
        self.count = {e: 0 for e in ENGS}
        self.sems = {e: [] for e in ENGS}
        self.waited = {e: {} for e in ENGS}
        self.waited_dma = {e: {} for e in ENGS}
        self.dma_sems = [stack.enter_context(nc.semaphore(f"dsem{i}")) for i in range(n_dma_sems)]
        self.dma_val = [0] * n_dma_sems
        self.dma_k = 0
        self.n_waits = 0
        self.n_dma = 0

    def _sem_for(self, e, idx):
        b = idx // SEM_BLOCK
        while len(self.sems[e]) <= b:
            self.sems[e].append(self.stack.enter_context(
                self.nc.semaphore(f"s_{e}_{len(self.sems[e])}")))
        return self.sems[e][b], (idx % SEM_BLOCK) + 1

    def _wait_tok(self, e, tok, same=False):
        if tok is None:
            return
        eng = self.eng[e]
        if tok[0] == "c":
            _, se, idx = tok
            if se == e and not same:
                return
            b = idx // SEM_BLOCK
            key = (se, b)
            if self.waited[e].get(key, -1) >= idx:
                return
            self.waited[e][key] = idx
            for bb in range(b):
                self.waited[e][(se, bb)] = 1 << 60
            sem, val = self._sem_for(se, idx)
            eng.wait_ge(sem, val)
            self.n_waits += 1
        else:
            _, si, val = tok
            if self.waited_dma[e].get(si, 0) >= val:
                return
            self.waited_dma[e][si] = val
            eng.wait_ge(self.dma_sems[si], val)
            self.n_waits += 1

    def _deps(self, e, reads, writes, same=False):
        for b in reads:
            self._wait_tok(e, b.last_w, same)
            if b.excl:
                for r in b.readers:
                    self._wait_tok(e, r, same)
        for b in writes:
            self._wait_tok(e, b.last_w, same)
            for r in b.readers:
                self._wait_tok(e, r, same)

    def _commit(self, tok, reads, writes):
        for b in reads:
            b.readers.append(tok)
            if len(b.readers) > 64:
                b.readers = b.readers[-64:] if False else b.readers
        for b in writes:
            b.last_w = tok
            b.readers = []

    def op(self, e, fn, reads=(), writes=()):
        self._deps(e, reads, writes)
        idx = self.count[e]
        inst = fn(self.eng[e])
        sem, _ = self._sem_for(e, idx)
        inst.then_inc(sem, 1)
        self.count[e] = idx + 1
        self._commit(("c", e, idx), reads, writes)

    def dma(self, out, in_, reads=(), writes=(), e="sp", **kw):
        self._deps(e, reads, writes, same=True)
        si = self.dma_k % len(self.dma_sems)
        self.dma_k += 1
        prev = self.dma_val[si]
        if prev:
            self._wait_tok(e, ("d", si, prev))
        val = prev + 16
        self.dma_val[si] = val
        self.eng[e].dma_start(out=out, in_=in_, **kw).then_inc(self.dma_sems[si], 16)
        self.n_dma += 1
        tok = ("d", si, val)
        self._commit(tok, reads, writes)
        return tok

    def wait_all_dma(self, e="sp"):
        for si, v in enumerate(self.dma_val):
            if v:
                self._wait_tok(e, ("d", si, v))


def _compact_readers(b):
    pass


def fm_cols(v):
    v = np.asarray(v, np.float32).reshape(-1)
    n = v.size // 128
    return np.ascontiguousarray(v.reshape(n, 128).T)


def tile_w(W, kpad=None):
    K, F = W.shape
    kch = (K + 127) // 128
    if K % 128:
        Wp = np.zeros((kch * 128, F), np.float32)
        Wp[:K] = W
        W = Wp
    nf = F // 128
    return np.ascontiguousarray(W.reshape(kch, 128, nf, 128).transpose(2, 1, 0, 3))


class VecPack:
    def __init__(self):
        self.cols = []
        self.off = {}
        self.n = 0

    def add(self, key, arr2d):
        self.off[key] = self.n
        self.cols.append(np.asarray(arr2d, np.float32))
        self.n += arr2d.shape[1]

    def pack(self):
        return np.ascontiguousarray(np.concatenate(self.cols, axis=1))


def vec_keys_sizes():
    ks = []
    for l in range(DEPTH):
        ks.append((("ng", l), 24))
        ks.append((("adab", l), 72))
    return ks


def build_host_arrays(inp):
    H = {}
    vp = VecPack()
    for l in range(DEPTH):
        vp.add(("ng", l), fm_cols(inp["norm_g"][l]))
        vp.add(("adab", l), fm_cols(inp["ada_b"][l]))
    H["vecs"] = vp
    build_host_mixers(inp, H, vp)
    aw = np.asarray(inp["ada_w"], np.float32)
    H["ada_w"] = np.ascontiguousarray(
        aw.reshape(DEPTH, 8, 128, 18, 512).transpose(0, 3, 2, 1, 4))
    wi = np.asarray(inp["ffn_w_in"], np.float32).reshape(DEPTH, 2, 8, 128, 2, NJ, 128)
    H["ffn_w_in"] = np.ascontiguousarray(wi.transpose(0, 1, 5, 3, 2, 4, 6)).reshape(
        DEPTH, 2, NJ, 128, 8 * 256)
    wo = np.asarray(inp["ffn_w_out"], np.float32).reshape(DEPTH, 2, NJ, 128, 8, 128)
    H["ffn_w_out"] = np.ascontiguousarray(wo.transpose(0, 1, 4, 3, 2, 5)).reshape(
        DEPTH, 2, 8, 128, NJ * 128)
    return H


def build_consts():
    C = {}
    p = np.arange(128)[:, None]
    q = np.arange(128)[None, :]
    C["ident"] = (p == q).astype(np.float32)
    C["blk64"] = ((p // 64) == (q // 64)).astype(np.float32) / 64.0
    C["mcur"] = (q >= p).astype(np.float32)
    C["mprev"] = (p > q).astype(np.float32)
    C["blk1"] = ((p // 64) == (q // 64)).astype(np.float32)
    i64, t64 = p % 64, q % 64
    C["mA"] = np.where(q < 64, i64 < t64, i64 <= t64).astype(np.float32)
    C["mT"] = (t64 < i64).astype(np.float32)[:, 0:64]
    C["rmask"] = np.broadcast_to((q % 64 != 0), (128, 128)).astype(np.float32)
    C["onesf"] = np.ones((128, 128), np.float32)
    off = {}
    cols = []
    n = 0
    for k, v in C.items():
        off[k] = n
        cols.append(v)
        n += v.shape[1]
    return np.ascontiguousarray(np.concatenate(cols, axis=1)), off


def build_host_mixers(inp, H, vp):
    wqkv = np.asarray(inp["sw_w_qkv"][0], np.float32)
    H["sw_wq"] = tile_w(wqkv[:, 0:1024]).reshape(8, 128, 1024)
    wk = wqkv[:, 1024:1280].reshape(1024, 4, 64)
    wkd = np.concatenate([wk, wk], axis=2).reshape(1024, 512)
    H["sw_wk"] = tile_w(wkd).reshape(4, 128, 1024)
    H["sw_wv"] = np.ascontiguousarray(
        wqkv[:, 1280:1536].reshape(8, 128, 256).transpose(1, 0, 2)).reshape(1, 128, 2048)
    H["sw_wo"] = tile_w(np.asarray(inp["sw_w_o"][0], np.float32)).reshape(8, 128, 1024)
    vp.add(("swq",), np.tile(np.asarray(inp["sw_q_norm"][0], np.float32), 2).reshape(128, 1))
    vp.add(("swk",), np.tile(np.asarray(inp["sw_k_norm"][0], np.float32), 2).reshape(128, 1))
    vp.add(("sink",), np.tile(np.asarray(inp["sw_sinks"][0], np.float32)[None, :], (128, 1)))
    build_host_rwkv(inp, H, vp)
    build_host_mamba(inp, H, vp)


def build_host_rwkv(inp, H, vp):
    NA = 2
    def st(fn):
        return np.ascontiguousarray(np.stack([fn(j) for j in range(NA)]))
    f32 = lambda a: np.asarray(a, np.float32)
    for i, nm in enumerate(("rw_wr", "rw_wk", "rw_wv")):
        H[nm] = st(lambda j: tile_w(f32(inp["rw_w_rkv"][j, i])).reshape(8, 128, 1024))
    H["rw_wo"] = st(lambda j: tile_w(f32(inp["rw_w_o"][j])).reshape(8, 128, 1024))
    def l1(key, R):
        return st(lambda j: f32(inp[key][j]).reshape(8, 128, R).transpose(1, 0, 2).reshape(128, 8 * R))
    H["rw_w1"] = l1("rw_w1", 64)
    H["rw_a1"] = l1("rw_a1", 64)
    H["rw_g1"] = l1("rw_g1", 160)
    H["rw_w2"] = st(lambda j: f32(inp["rw_w2"][j]))
    H["rw_a2"] = st(lambda j: f32(inp["rw_a2"][j]))
    H["rw_g2a"] = st(lambda j: f32(inp["rw_g2"][j][0:128]))
    H["rw_g2b"] = st(lambda j: f32(inp["rw_g2"][j][128:160]))
    v1 = f32(inp["rw_v1"][0]).reshape(8, 128, 32).transpose(1, 0, 2).reshape(128, 256)
    H["rw_v1"] = np.ascontiguousarray(np.stack([v1, v1]))
    v2 = f32(inp["rw_v2"][0])
    H["rw_v2"] = np.ascontiguousarray(np.stack([v2, v2]))
    for j in range(NA):
        vp.add(("mu", j), fm_cols(inp["rw_mu"][j]))
        for nm in ("w0", "a0", "k_k", "k_a", "ln_w", "ln_b", "r_k"):
            vp.add((nm, j), fm_cols(inp["rw_" + nm][j]))
    vp.add(("v0", 1), fm_cols(inp["rw_v0"][0]))


def build_host_mamba(inp, H, vp):
    f32 = lambda a: np.asarray(a, np.float32)
    win = f32(inp["mb_w_in"][0])
    H["mb_win"] = tile_w(win[:, 0:6144]).reshape(48, 128, 1024)
    H["mb_wdt"] = np.ascontiguousarray(win[:, 6144:6176].reshape(8, 128, 32).transpose(1, 0, 2)).reshape(1, 128, 256)
    H["mb_wo"] = tile_w(f32(inp["mb_w_out"][0])).reshape(8, 128, 2048)
    cw = f32(inp["mb_conv_w"][0])
    for tap in range(4):
        vp.add(("cw", tap), fm_cols(cw[tap]))
    vp.add(("cbias",), fm_cols(inp["mb_conv_b"][0]))
    vp.add(("mbD",), fm_cols(np.repeat(f32(inp["mb_D"][0]), 64)))
    vp.add(("mbng",), fm_cols(inp["mb_norm_g"][0]))
    vp.add(("dtb",), np.tile(f32(inp["mb_dt_bias"][0])[None, :], (128, 1)))
    vp.add(("alog",), np.tile(f32(inp["mb_A_log"][0])[None, :], (128, 1)))


W_SPECS = {"mb_win": [48, 128, 1024], "mb_wdt": [1, 128, 256], "mb_wo": [8, 128, 2048],
           "rw_wr": [2, 8, 128, 1024], "rw_wk": [2, 8, 128, 1024], "rw_wv": [2, 8, 128, 1024],
           "rw_wo": [2, 8, 128, 1024], "rw_w1": [2, 128, 512], "rw_a1": [2, 128, 512],
           "rw_g1": [2, 128, 1280], "rw_w2": [2, 64, 1024], "rw_a2": [2, 64, 1024],
           "rw_g2a": [2, 128, 1024], "rw_g2b": [2, 32, 1024], "rw_v1": [2, 128, 256],
           "rw_v2": [2, 32, 1024],
           "sw_wq": [8, 128, 1024], "sw_wk": [4, 128, 1024], "sw_wv": [1, 128, 2048],
           "sw_wo": [8, 128, 1024]}


class Builder:
    def __init__(self, nseq, n_layers, vec_off, nvec, mixers=True, dbg=None, kinds=None,
                 const_off=None, nconst=0):
        self.kinds = kinds if kinds is not None else [i % 3 for i in range(DEPTH)]
        self.const_off = const_off
        self.nconst = nconst
        self.nseq = nseq
        self.n_layers = n_layers
        self.vec_off = vec_off
        self.nvec = nvec
        self.mixers = mixers
        self.dbg = dbg

    def build(self):
        nc = bass.Bass("TRN2", target_bir_lowering=False)
        self.nc = nc
        ns = self.nseq
        dt = nc.dram_tensor
        self.d_xT = dt("xT", [ns, KC, 128, S], F32, kind="ExternalInput").ap()
        self.d_cT = dt("cT", [128, KC * ns], F32, kind="ExternalInput").ap()
        self.d_vecs = dt("vecs", [128, self.nvec], F32, kind="ExternalInput").ap()
        self.d_adaw = dt("ada_w", [DEPTH, 18, 128, 8 * 512], F32, kind="ExternalInput").ap()
        self.d_win = dt("ffn_w_in", [DEPTH, 2, NJ, 128, 8 * 256], F32, kind="ExternalInput").ap()
        self.d_wout = dt("ffn_w_out", [DEPTH, 2, 8, 128, NJ * 128], F32, kind="ExternalInput").ap()
        self.d_yT = dt("yT", [ns, KC, 128, S], F32, kind="ExternalOutput").ap()
        self.d_consts = dt("consts", [128, self.nconst], F32, kind="ExternalInput").ap()
        self.dw = {k: dt(k, shp, F32, kind="ExternalInput").ap() for k, shp in W_SPECS.items()}
        self.scr = {}
        self.b_scr = {}
        for nm in ("R", "K", "V", "A", "G", "VF", "LD"):
            self.scr[nm] = dt("scr_" + nm, [KC, 128, S], F32 if nm == "LD" else BF16, kind="Internal").ap()
            self.b_scr[nm] = [Buf(f"scr{nm}{t}") for t in range(4)]
        for nm, n in (("MZ", 16), ("MX", 16), ("MB", 8), ("MC", 8)):
            self.scr[nm] = dt("scr_" + nm, [n, 128, S], BF16, kind="Internal").ap()
            self.b_scr[nm] = [Buf(f"scr{nm}{i}") for i in range(n)]
        with ExitStack() as st:
            self.st = st
            self.P = Prog(nc, st)
            self.alloc()
            self.emit()
            self.P.wait_all_dma("sp")
        print("instr counts", self.P.count, "waits", self.P.n_waits, "dmas", self.P.n_dma)
        return nc

    def sb(self, name, shape, dtype):
        return self.st.enter_context(self.nc.sbuf_tensor(name, shape, dtype))

    def alloc(self):
        ns = self.nseq
        self.xT = self.sb("xT_sb", [128, KC, S], F32)
        self.b_xT = [[Buf(f"xT{kc}_{t}") for t in range(4)] for kc in range(KC)]
        self.hb = self.sb("hb", [128, KC, S], BF16)
        self.b_hb = [Buf(f"hb{t}") for t in range(4)]
        self.u = self.sb("u", [128, NJ, 1024], BF16)
        self.b_u = [[Buf(f"u{j}_{t}") for t in range(2)] for j in range(NJ)]
        self.st32 = [self.sb(f"st32_{i}", [128, NJ * 128], F32) for i in range(2)]
        self.stb = [self.sb(f"stb_{i}", [128, NJ * 128], BF16) for i in range(2)]
        self.b_st32 = [Buf(f"st32_{i}") for i in range(2)]
        self.b_stb = [Buf(f"stb_{i}") for i in range(2)]
        self.wslot = 0
        self.tmpf = [self.sb(f"tmpf{i}", [128, 512], F32) for i in range(4)]
        self.b_tmpf = [Buf(f"tmpf{i}") for i in range(4)]
        self.tmpb = [self.sb(f"tmpb{i}", [128, 512], BF16) for i in range(2)]
        self.b_tmpb = [Buf(f"tmpb{i}") for i in range(2)]
        self.vecs = self.sb("vecs_sb", [128, self.nvec], F32)
        self.b_vecs = Buf("vecs")
        self.modT = self.sb("modT", [128, DEPTH, 72, ns], F32)
        self.b_mod = Buf("mod")
        self.aco = self.sb("aco", [128, DEPTH * 3 * ns * 8], F32)
        self.gco = self.sb("gco", [128, DEPTH * 3 * ns * 8], F32)
        self.cact = self.sb("cact", [128, KC * ns], F32)
        self.b_cact = Buf("cact")
        self.ones_b = self.sb("ones_b", [128, 128], BF16)
        self.b_const = Buf("const")
        self.cf = self.sb("consts_f", [128, self.nconst], F32)
        self.cb = self.sb("consts_b", [128, self.nconst], BF16)
        self.ones1 = self.sb("ones1", [128, 128], BF16)
        self.esink = self.sb("esink", [128, 16], F32)
        self.omu = self.sb("omu", [128, 96], F32)
        self.mbdt = self.sb("mbdt", [128, 512], F32)
        self.b_mbdt = Buf("mbdt")
        self.bank_rr = 0
        self.psum = self.st.enter_context(self.nc.psum_tensor("ps", [128, 8, 512], F32))
        self.b_ps = [Buf(f"ps{i}", excl=True) for i in range(8)]

    def vcol(self, key, i=0, n=1):
        o = self.vec_off[key] + i
        return self.vecs[:, o:o + n]

    def co_idx(self, l, sub, s):
        return ((l * 3 + sub) * self.nseq + s) * 8

    def emit(self):
        P = self.P
        P.dma(self.vecs[:], self.d_vecs, writes=[self.b_vecs])
        P.dma(self.cact[:], self.d_cT, writes=[self.b_cact])
        P.op("pool", lambda e: e.memset(self.ones_b[:], 1.0 / 1024.0), writes=[self.b_const])
        P.op("pool", lambda e: e.memset(self.ones1[:], 1.0), writes=[self.b_const])
        P.dma(self.cf[:], self.d_consts, writes=[self.b_const])
        P.op("dve", lambda e: e.tensor_copy(out=self.cb[:], in_=self.cf[:]),
             reads=[self.b_const], writes=[self.b_const])
        for jj in range(2):
            P.op("dve", lambda e, jj=jj: e.tensor_scalar(out=self.omu[:, jj * 48:(jj + 1) * 48],
                                                         in0=self.vcol(("mu", jj), 0, 48), scalar1=-1.0, scalar2=1.0,
                                                         op0=ALU.mult, op1=ALU.add),
                 reads=[self.b_vecs], writes=[self.b_const])
        if 2 in self.kinds[:self.n_layers]:
            P.op("act", lambda e: e.activation(out=self.esink[:], in_=self.vcol(("sink",), 0, 16),
                                               func=AF.Exp),
                 reads=[self.b_vecs], writes=[self.b_const])
        P.op("act", lambda e: e.activation(out=self.cact[:], in_=self.cact[:], func=AF.Silu),
             reads=[self.b_cact], writes=[self.b_cact])
        self.emit_mod()
        for s in range(self.nseq):
            self.emit_seq(s)

    def emit_mod(self):
        P = self.P
        ns = self.nseq
        for l in range(self.n_layers):
            for blk in range(18):
                for half in range(2):
                    slot = self.wslot
                    self.wslot ^= 1
                    sl = self.st32[slot]
                    bsl = self.b_st32[slot]
                    P.dma(sl[:, 0:2048], self.d_adaw[l, blk, :, half * 2048:(half + 1) * 2048],
                          writes=[bsl])
                    for fcl in range(4):
                        for k4 in range(4):
                            kcg = half * 4 + k4
                            P.op("pe", lambda e, sl=sl, k4=k4, fcl=fcl, half=half, kcg=kcg: e.matmul(
                                self.psum[:, half, fcl * ns:(fcl + 1) * ns],
                                lhsT=sl[:, k4 * 512 + fcl * 128:k4 * 512 + (fcl + 1) * 128],
                                rhs=self.cact[:, kcg * ns:(kcg + 1) * ns],
                                start=(k4 == 0), stop=(k4 == 3)),
                                reads=[bsl, self.b_cact], writes=[self.b_ps[half]])
                P.op("act", lambda e: e.activation(out=self.tmpf[0][:, 0:4 * ns],
                                                   in_=self.psum[:, 1, 0:4 * ns], func=AF.Copy),
                     reads=[self.b_ps[1]], writes=[self.b_tmpf[0]])
                for fcl in range(4):
                    fch = blk * 4 + fcl
                    P.op("dve", lambda e, fcl=fcl, fch=fch, l=l: e.scalar_tensor_tensor(
                        out=self.modT[:, l, fch, :], in0=self.psum[:, 0, fcl * ns:(fcl + 1) * ns],
                        scalar=self.vcol(("adab", l), fch), op0=ALU.add,
                        in1=self.tmpf[0][:, fcl * ns:(fcl + 1) * ns], op1=ALU.add),
                        reads=[self.b_ps[0], self.b_vecs, self.b_tmpf[0]], writes=[self.b_mod])
            for sub in range(3):
                for s in range(ns):
                    i0 = self.co_idx(l, sub, s)
                    P.op("dve", lambda e, l=l, sub=sub, s=s, i0=i0: e.scalar_tensor_tensor(
                        out=self.aco[:, i0:i0 + 8], in0=self.modT[:, l, sub * 24 + 8:sub * 24 + 16, s],
                        scalar=1.0, op0=ALU.add, in1=self.vcol(("ng", l), sub * 8, 8), op1=ALU.mult),
                        reads=[self.b_mod, self.b_vecs], writes=[self.b_mod])
                    gsc = 1.0 if sub == 1 else 0.5
                    P.op("dve", lambda e, l=l, sub=sub, s=s, i0=i0, gsc=gsc: e.tensor_scalar(
                        out=self.gco[:, i0:i0 + 8], in0=self.modT[:, l, sub * 24 + 16:sub * 24 + 24, s],
                        scalar1=gsc, scalar2=None, op0=ALU.mult),
                        reads=[self.b_mod], writes=[self.b_mod])

    def emit_seq(self, s):
        P = self.P
        for kc in range(KC):
            P.dma(self.xT[:, kc, :], self.d_xT[s, kc], writes=self.b_xT[kc])
        for l in range(self.n_layers):
            self.emit_adaln(l, 0, s)
            self.emit_ffn(l, 0, s)
            if self.mixers:
                self.emit_adaln(l, 1, s)
                self.emit_mixer(l, s)
            self.emit_adaln(l, 2, s)
            self.emit_ffn(l, 1, s)
        for kc in range(KC):
            P.dma(self.d_yT[s, kc], self.xT[:, kc, :], reads=self.b_xT[kc])

    def emit_mixer(self, l, s):
        kind = self.kinds[l]
        if kind == 2:
            self.emit_swa(l, s)
        elif kind == 1:
            self.emit_mamba(l, s)
        else:
            self.emit_rwkv(l, s)

    def cst(self, key, n=128, bf=True):
        o = self.const_off[key]
        return (self.cb if bf else self.cf)[:, o:o + n]

    def proj(self, wd, nf, kch, rhs_fn, rhs_bufs_fn, tiles, consume, banks=(0, 1, 2, 3)):
        P = self.P
        for fc in range(nf):
            w, bw = self.load_w(wd[fc], kch * 128)
            for ti, (tok, ntok) in enumerate(tiles):
                bank = banks[self.bank_rr % len(banks)]
                self.bank_rr += 1
                for kc in range(kch):
                    P.op("pe", lambda e, kc=kc, bank=bank, w=w, tok=tok, ntok=ntok: e.matmul(
                        self.psum[:, bank, 0:ntok], lhsT=w[:, kc * 128:(kc + 1) * 128],
                        rhs=rhs_fn(kc, tok), start=(kc == 0), stop=(kc == kch - 1)),
                        reads=[bw] + list(rhs_bufs_fn(ti)), writes=[self.b_ps[bank]])
                consume(fc, ti, tok, bank)

    def out_proj(self, wd, l, s, kch=8):
        P = self.P
        i0 = self.co_idx(l, 1, s)
        tiles = [(slice(t * 512, (t + 1) * 512), 512) for t in range(4)]

        def consume(fc, ti, tok, bank):
            P.op("dve", lambda e: e.scalar_tensor_tensor(
                out=self.xT[:, fc, tok], in0=self.psum[:, bank, :],
                scalar=self.gco[:, i0 + fc:i0 + fc + 1], op0=ALU.mult,
                in1=self.xT[:, fc, tok], op1=ALU.add),
                reads=[self.b_ps[bank], self.b_mod, self.b_xT[fc][ti]], writes=[self.b_xT[fc][ti]])
        self.proj(wd, 8, kch, lambda kc, tok: self.hb[:, kc, tok], lambda ti: [self.b_hb[ti]],
                  tiles, consume)

    def emit_swa(self, l, s):
        P = self.P
        uflat = self.u[:].rearrange("p a b -> p (a b)")
        kn = uflat[:, 0:8192].rearrange("p (g t) -> p g t", g=4)
        vt = uflat[:, 8192:12288].rearrange("p (b n) -> p b n", b=16)
        qn = uflat[:, 12288:16384].rearrange("p (c t) -> p c t", c=8)
        pT = [uflat[:, 16384 + i * 256:16384 + (i + 1) * 256] for i in range(2)]
        b_kn = [Buf(f"kn{t}") for t in range(4)]
        b_vt = [Buf(f"vt{t}") for t in range(16)]
        b_qn = Buf("qn")
        b_pT = [Buf("pT0"), Buf("pT1")]
        b_rd = Buf("rd")
        allu = [b for row in self.b_u for b in row]
        tiles = [(slice(t * 512, (t + 1) * 512), 512) for t in range(4)]

        def norm_consume(dst_fn, dst_buf_fn, gkey):
            def consume(fc, ti, tok, bank):
                sq = self.tmpb[fc % 2]
                bsq = self.b_tmpb[fc % 2]
                P.op("act", lambda e: e.activation(out=sq[:], in_=self.psum[:, bank, :], func=AF.Square),
                     reads=[self.b_ps[bank]], writes=[bsq])
                P.op("pe", lambda e: e.matmul(self.psum[:, 6, :], lhsT=self.cst("blk64"), rhs=sq[:],
                                              start=True, stop=True),
                     reads=[bsq, self.b_const], writes=[self.b_ps[6]])
                rs = self.tmpf[2]
                brs = self.b_tmpf[2]
                P.op("act", lambda e: e.activation(out=rs[:], in_=self.psum[:, 6, :], func=AF.Sqrt,
                                                   bias=EPS, scale=1.0),
                     reads=[self.b_ps[6]], writes=[brs])
                P.op("dve", lambda e: e.reciprocal(out=rs[:], in_=rs[:]), reads=[brs], writes=[brs])
                P.op("dve", lambda e: e.scalar_tensor_tensor(
                    out=dst_fn(fc, ti, tok), in0=self.psum[:, bank, :], scalar=self.vcol(gkey),
                    op0=ALU.mult, in1=rs[:], op1=ALU.mult),
                    reads=[self.b_ps[bank], brs, self.b_vecs], writes=[dst_buf_fn(fc, ti)] + (allu if (fc == 0 and ti == 0) else []))
            return consume

        self.proj(self.dw["sw_wk"], 4, 8, lambda kc, tok: self.hb[:, kc, tok], lambda ti: [self.b_hb[ti]],
                  tiles, norm_consume(lambda fc, ti, tok: kn[:, fc, tok], lambda fc, ti: b_kn[ti], ("swk",)))
        wv, bwv = self.load_w(self.dw["sw_wv"][0], 2048)
        for blk in range(16):
            bank = (blk % 2)
            for kc in range(KC):
                P.op("pe", lambda e, kc=kc, blk=blk, bank=bank: e.matmul(
                    self.psum[:, bank, 0:256], lhsT=self.hb[:, kc, blk * 128:(blk + 1) * 128],
                    rhs=wv[:, kc * 256:(kc + 1) * 256], start=(kc == 0), stop=(kc == KC - 1)),
                    reads=[bwv, self.b_hb[blk // 4]], writes=[self.b_ps[bank]])
            P.op("act", lambda e, blk=blk, bank=bank: e.activation(
                out=vt[:, blk, :], in_=self.psum[:, bank, 0:256], func=AF.Copy),
                reads=[self.b_ps[bank]], writes=[b_vt[blk]])
        for t in range(4):
            self.proj(self.dw["sw_wq"], 8, 8, lambda kc, tok: self.hb[:, kc, tok],
                      lambda ti, t=t: [self.b_hb[t]], [tiles[t]],
                      norm_consume(lambda fc, ti, tok: qn[:, fc, :], lambda fc, ti: b_qn, ("swq",)))
            for bl in range(4):
                b = t * 4 + bl
                qtok = slice(bl * 128, (bl + 1) * 128)
                for h in range(16):
                    kc, pb, g = h // 2, (h % 2) * 64, h // 4
                    sbank = 4 + (h % 2)
                    nk = 256 if b > 0 else 128
                    P.op("pe", lambda e, kc=kc, pb=pb, g=g, sbank=sbank, b=b: e.matmul(
                        self.psum[:, sbank, 0:128], lhsT=kn[pb:pb + 64, g, b * 128:(b + 1) * 128],
                        rhs=qn[pb:pb + 64, kc, qtok], start=True, stop=True),
                        reads=[b_kn[b // 4], b_qn], writes=[self.b_ps[sbank]])
                    if b > 0:
                        P.op("pe", lambda e, kc=kc, pb=pb, g=g, sbank=sbank, b=b: e.matmul(
                            self.psum[:, sbank, 128:256], lhsT=kn[pb:pb + 64, g, (b - 1) * 128:b * 128],
                            rhs=qn[pb:pb + 64, kc, qtok], start=True, stop=True),
                            reads=[b_kn[(b - 1) // 4], b_qn], writes=[self.b_ps[sbank]])
                    pt = pT[h % 2]
                    bpt = b_pT[h % 2]
                    P.op("act", lambda e, sbank=sbank, pt=pt, nk=nk: e.activation(
                        out=pt[:, 0:nk], in_=self.psum[:, sbank, 0:nk], func=AF.Exp, scale=0.125),
                        reads=[self.b_ps[sbank]], writes=[bpt])
                    P.op("pool", lambda e, pt=pt: e.tensor_tensor(
                        out=pt[:, 0:128], in0=pt[:, 0:128], in1=self.cst("mcur"), op=ALU.mult),
                        reads=[bpt, self.b_const], writes=[bpt])
                    if b > 0:
                        P.op("pool", lambda e, pt=pt: e.tensor_tensor(
                            out=pt[:, 128:256], in0=pt[:, 128:256], in1=self.cst("mprev"), op=ALU.mult),
                            reads=[bpt, self.b_const], writes=[bpt])
                    P.op("pe", lambda e, pt=pt, b=b: e.matmul(
                        self.psum[:, 6, 0:128], lhsT=self.ones1[:], rhs=pt[:, 0:128],
                        start=True, stop=(b == 0)), reads=[bpt, self.b_const], writes=[self.b_ps[6]])
                    if b > 0:
                        P.op("pe", lambda e, pt=pt: e.matmul(
                            self.psum[:, 6, 0:128], lhsT=self.ones1[:], rhs=pt[:, 128:256],
                            start=False, stop=True), reads=[bpt, self.b_const], writes=[self.b_ps[6]])
                    P.op("pe", lambda e, pt=pt, b=b, g=g, pb=pb: e.matmul(
                        self.psum[pb:pb + 64, 7, 0:128], lhsT=vt[:, b, g * 64:(g + 1) * 64], rhs=pt[:, 0:128],
                        start=True, stop=(b == 0)), reads=[bpt, b_vt[b]], writes=[self.b_ps[7]])
                    if b > 0:
                        P.op("pe", lambda e, pt=pt, b=b, g=g, pb=pb: e.matmul(
                            self.psum[pb:pb + 64, 7, 0:128], lhsT=vt[:, b - 1, g * 64:(g + 1) * 64],
                            rhs=pt[:, 128:256], start=False, stop=True),
                            reads=[bpt, b_vt[b - 1]], writes=[self.b_ps[7]])
                    rd = self.tmpf[3]
                    brd = self.b_tmpf[3]
                    P.op("dve", lambda e, h=h: e.tensor_scalar(
                        out=rd[:, 0:128], in0=self.psum[:, 6, 0:128], scalar1=self.esink[:, h:h + 1],
                        scalar2=None, op0=ALU.add), reads=[self.b_ps[6], self.b_const], writes=[brd])
                    P.op("dve", lambda e: e.reciprocal(out=rd[:, 0:128], in_=rd[:, 0:128]),
                         reads=[brd], writes=[brd])
                    P.op("dve", lambda e, pb=pb, kc=kc, b=b: e.tensor_tensor(
                        out=self.hb[pb:pb + 64, kc, b * 128:(b + 1) * 128], in0=self.psum[pb:pb + 64, 7, 0:128],
                        in1=rd[pb:pb + 64, 0:128], op=ALU.mult),
                        reads=[self.b_ps[7], brd], writes=[self.b_hb[t]])
        self.out_proj(self.dw["sw_wo"], l, s)
        for row in self.b_u:
            for bb in row:
                for x in b_kn + b_vt + [b_qn] + b_pT:
                    if x.last_w is not None:
                        bb.readers.append(x.last_w)
                    bb.readers.extend(x.readers)

    def emit_adaln(self, l, sub, s):
        P = self.P
        i0 = self.co_idx(l, sub, s)
        for t in range(4):
            tok = slice(t * 512, (t + 1) * 512)
            pb = 6 + (t % 2)
            for kc in range(KC):
                sq = self.tmpb[kc % 2]
                bsq = self.b_tmpb[kc % 2]
                P.op("act", lambda e, sq=sq, kc=kc: e.activation(
                    out=sq[:], in_=self.xT[:, kc, tok], func=AF.Square),
                    reads=[self.b_xT[kc][t]], writes=[bsq])
                P.op("pe", lambda e, sq=sq, kc=kc, pb=pb: e.matmul(
                    self.psum[:, pb, :], lhsT=self.ones_b[:], rhs=sq[:],
                    start=(kc == 0), stop=(kc == KC - 1)),
                    reads=[bsq, self.b_const], writes=[self.b_ps[pb]])
            rs = self.tmpf[2]
            brs = self.b_tmpf[2]
            P.op("act", lambda e, pb=pb: e.activation(out=rs[:], in_=self.psum[:, pb, :],
                                                      func=AF.Sqrt, bias=EPS, scale=1.0),
                 reads=[self.b_ps[pb]], writes=[brs])
            P.op("dve", lambda e: e.reciprocal(out=rs[:], in_=rs[:]), reads=[brs], writes=[brs])
            for kc in range(KC):
                tf = self.tmpf[kc % 2]
                btf = self.b_tmpf[kc % 2]
                P.op("dve", lambda e, tf=tf, kc=kc: e.scalar_tensor_tensor(
                    out=tf[:], in0=self.xT[:, kc, tok], scalar=self.aco[:, i0 + kc:i0 + kc + 1],
                    op0=ALU.mult, in1=rs[:], op1=ALU.mult),
                    reads=[self.b_xT[kc][t], brs, self.b_mod], writes=[btf])
                P.op("act", lambda e, tf=tf, kc=kc: e.activation(
                    out=self.hb[:, kc, tok], in_=tf[:], func=AF.Identity,
                    bias=self.modT[:, l, sub * 24 + kc, s:s + 1], scale=1.0),
                    reads=[btf, self.b_mod], writes=[self.b_hb[t]])

    def load_w(self, dram_ap, ncols, npart=128):
        P = self.P
        slot = self.wslot
        self.wslot ^= 1
        P.dma(self.st32[slot][0:npart, 0:ncols], dram_ap, writes=[self.b_st32[slot]])
        P.op("pool", lambda e: e.tensor_copy(out=self.stb[slot][0:npart, 0:ncols],
                                             in_=self.st32[slot][0:npart, 0:ncols]),
             reads=[self.b_st32[slot]], writes=[self.b_stb[slot]])
        return self.stb[slot], self.b_stb[slot]

    @staticmethod
    def handover(olds, news):
        toks = []
        for b in olds:
            if b.last_w is not None:
                toks.append(b.last_w)
            toks.extend(b.readers)
        toks = list(dict.fromkeys(toks))
        for nb in news:
            nb.readers.extend(toks)

    def emit_ffn(self, l, f, s):
        P = self.P
        i0 = self.co_idx(l, 0 if f == 0 else 2, s)
        for half in range(2):
            for j in range(NJ):
                w, bw = self.load_w(self.d_win[l, f, j], 8 * 256)
                for tt in range(2):
                    t = half * 2 + tt
                    tok = slice(t * 512, (t + 1) * 512)
                    pa = 2 * (tt % 2)
                    pbk = pa + 1
                    for which, bank in ((0, pa), (1, pbk)):
                        for kc in range(KC):
                            P.op("pe", lambda e, kc=kc, which=which, bank=bank, w=w: e.matmul(
                                self.psum[:, bank, :],
                                lhsT=w[:, kc * 256 + which * 128:kc * 256 + which * 128 + 128],
                                rhs=self.hb[:, kc, tok], start=(kc == 0), stop=(kc == KC - 1)),
                                reads=[bw, self.b_hb[t]], writes=[self.b_ps[bank]])
                    sa = self.tmpf[2 + tt]
                    bsa = self.b_tmpf[2 + tt]
                    P.op("act", lambda e, sa=sa, pa=pa: e.activation(
                        out=sa[:], in_=self.psum[:, pa, :], func=AF.Silu),
                        reads=[self.b_ps[pa]], writes=[bsa])
                    P.op("dve", lambda e, sa=sa, pbk=pbk, j=j, tt=tt: e.tensor_tensor(
                        out=self.u[:, j, tt * 512:(tt + 1) * 512], in0=sa[:],
                        in1=self.psum[:, pbk, :], op=ALU.mult),
                        reads=[bsa, self.b_ps[pbk]], writes=[self.b_u[j][tt]])
            for c in range(KC):
                w, bw = self.load_w(self.d_wout[l, f, c], NJ * 128)
                for tt in range(2):
                    t = half * 2 + tt
                    tok = slice(t * 512, (t + 1) * 512)
                    bank = 4 + (tt % 2)
                    for j in range(NJ):
                        P.op("pe", lambda e, j=j, bank=bank, w=w, tt=tt: e.matmul(
                            self.psum[:, bank, :], lhsT=w[:, j * 128:(j + 1) * 128],
                            rhs=self.u[:, j, tt * 512:(tt + 1) * 512],
                            start=(j == 0), stop=(j == NJ - 1)),
                            reads=[bw, self.b_u[j][tt]], writes=[self.b_ps[bank]])
                    P.op("dve", lambda e, c=c, bank=bank: e.scalar_tensor_tensor(
                        out=self.xT[:, c, tok], in0=self.psum[:, bank, :],
                        scalar=self.gco[:, i0 + c:i0 + c + 1], op0=ALU.mult,
                        in1=self.xT[:, c, tok], op1=ALU.add),
                        reads=[self.b_ps[bank], self.b_mod, self.b_xT[c][t]],
                        writes=[self.b_xT[c][t]])


    def emit_rwkv(self, l, s):
        j = sum(1 for q in self.kinds[:l] if q == 0)
        import os
        stop = int(os.environ.get("RW_STOP", "9"))
        self.rwkv_phase1(l, s, j)
        if stop >= 2:
            self.rwkv_phase2(l, s, j)
        if stop >= 3:
            self.out_proj(self.dw["rw_wo"][j], l, s)

    def rwkv_phase1(self, l, s, j):
        P = self.P
        ps, bps = self.psum, self.b_ps
        vres = j > 0
        uflat = self.u[:].rearrange("p a b -> p (a b)")
        t1 = uflat[:, 0:4096].rearrange("p (a t) -> p a t", a=2)
        stg = [uflat[:, 4096 + i * 512:4608 + i * 512] for i in range(4)]
        stgf = [uflat[:, 6144 + i * 1024:7168 + i * 1024].bitcast(F32) for i in range(2)]
        v2b = uflat[:, 8192:9216]
        vgt = uflat[:, 9216:10240].bitcast(F32)
        b_t1, b_v2b, b_vgt = Buf("t1"), Buf("v2b"), Buf("vgt")
        b_stg = [Buf(f"stg{i}") for i in range(4)]
        b_stgf = [Buf(f"stgf{i}") for i in range(2)]
        news = [b_t1, b_v2b, b_vgt] + b_stg + b_stgf
        allu = [b for row in self.b_u for b in row]
        self.handover(allu, news)
        st = {"stg": 0, "stgf": 0}
        pbanks = (0, 1, 2, 3)
        tiles = [(slice(t * 512, (t + 1) * 512), 512) for t in range(4)]

        def nstg():
            i = st["stg"]
            st["stg"] = (i + 1) % 4
            return i

        def load_mix(dram_ap, C, which):
            slot = self.wslot
            self.wslot ^= 1
            n = 8 * C
            P.dma(self.st32[slot][:, 0:n], dram_ap, writes=[self.b_st32[slot]])
            src3 = self.st32[slot][:, 0:n].rearrange("p (k c) -> p k c", k=8)
            for half, vec in ((0, self.omu[:, j * 48 + which * 8:j * 48 + which * 8 + 8]),
                              (1, self.vcol(("mu", j), which * 8, 8))):
                P.op("pool", lambda e, half=half, vec=vec: e.tensor_tensor(
                    out=self.stb[slot][:, half * n:(half + 1) * n].rearrange("p (k c) -> p k c", k=8),
                    in0=src3, in1=vec.unsqueeze(2).broadcast_to([128, 8, C]), op=ALU.mult),
                    reads=[self.b_st32[slot], self.b_vecs, self.b_const], writes=[self.b_stb[slot]])
            return self.stb[slot], self.b_stb[slot]

        def mixproj(w, bw, C, col0, ncol, ti, out_ap_fn):
            tok, _ = tiles[ti]
            t0 = ti * 512
            bank = pbanks[self.bank_rr % 4]
            self.bank_rr += 1
            rd = [bw, self.b_hb[ti]] + ([self.b_hb[ti - 1]] if ti else [])
            for kc in range(KC):
                P.op("pe", lambda e, kc=kc: e.matmul(
                    ps[0:ncol, bank, :], lhsT=w[:, kc * C + col0:kc * C + col0 + ncol], rhs=self.hb[:, kc, tok],
                    start=(kc == 0), stop=False), reads=rd, writes=[bps[bank]])
            c0 = 1 if ti == 0 else 0
            for kc in range(KC):
                P.op("pe", lambda e, kc=kc: e.matmul(
                    ps[0:ncol, bank, c0:512], lhsT=w[:, 8 * C + kc * C + col0:8 * C + kc * C + col0 + ncol],
                    rhs=self.hb[:, kc, t0 - 1 + c0:t0 + 511], start=False, stop=(kc == KC - 1)),
                    reads=rd, writes=[bps[bank]])
            return bank

        def bigproj(wkey, which, consume):
            for fc in range(8):
                w, bw = load_mix(self.dw[wkey][j][fc], 128, which)
                for ti in range(4):
                    bank = mixproj(w, bw, 128, 0, 128, ti, None)
                    consume(fc, ti, tiles[ti][0], bank)

        def to_scratch(names, func=AF.Copy, bias_key=None):
            def consume(fc, ti, tok, bank):
                i = nstg()
                kw = {}
                if bias_key is not None:
                    kw = dict(bias=self.vcol(bias_key, fc), scale=1.0)
                P.op("act", lambda e: e.activation(out=stg[i][:], in_=ps[:, bank, :], func=func, **kw),
                     reads=[bps[bank], self.b_vecs], writes=[b_stg[i]])
                for nm in names:
                    P.dma(self.scr[nm][fc, :, tok], stg[i][:], reads=[b_stg[i]], writes=[self.b_scr[nm][ti]], e="act")
            return consume

        def lora1(wkey, R, which, func, dst):
            w, bw = load_mix(self.dw[wkey][j], R, which)
            for (c0, cn, slot) in dst:
                for ti in range(4):
                    bank = mixproj(w, bw, R, c0, cn, ti, None)
                    P.op("act", lambda e, cn=cn, slot=slot, ti=ti, bank=bank: e.activation(
                        out=t1[0:cn, slot, tiles[ti][0]], in_=ps[0:cn, bank, :], func=func),
                        reads=[bps[bank]], writes=[b_t1])

        def lora2(parts, consume):
            ws = [self.load_w(self.dw[wk][j], 1024, npart=npart) + (npart, slot) for (wk, npart, slot) in parts]
            for fc in range(8):
                for ti in range(4):
                    tok = tiles[ti][0]
                    bank = pbanks[self.bank_rr % 4]
                    self.bank_rr += 1
                    for pi, (w2, bw2, npart, slot) in enumerate(ws):
                        P.op("pe", lambda e, w2=w2, npart=npart, slot=slot, pi=pi: e.matmul(
                            ps[:, bank, :], lhsT=w2[0:npart, fc * 128:(fc + 1) * 128],
                            rhs=t1[0:npart, slot, tok], start=(pi == 0), stop=(pi == len(ws) - 1)),
                            reads=[bw2, b_t1], writes=[bps[bank]])
                    consume(fc, ti, tok, bank)

        bigproj("rw_wr", 0, to_scratch(["R"]))
        bigproj("rw_wk", 2, to_scratch(["K"]))
        if vres:
            lora1("rw_v1", 32, 3, AF.Copy, [(0, 32, 0)])
            wv2, bwv2 = self.load_w(self.dw["rw_v2"][j], 1024, npart=32)
            P.op("pool", lambda e: e.tensor_copy(out=v2b[0:32, :], in_=wv2[0:32, 0:1024]), reads=[bwv2], writes=[b_v2b])

            def v_consume(fc, ti, tok, bank):
                ia, ib = nstg(), nstg()
                fi = st["stgf"]
                st["stgf"] ^= 1
                P.dma(stg[ia][:], self.scr["VF"][fc, :, tok], reads=[self.b_scr["VF"][ti]], writes=[b_stg[ia]])
                P.op("pe", lambda e: e.matmul(ps[:, 4, :], lhsT=v2b[0:32, fc * 128:(fc + 1) * 128], rhs=t1[0:32, 0, tok],
                                              start=True, stop=True), reads=[b_v2b, b_t1], writes=[bps[4]])
                P.op("act", lambda e: e.activation(out=vgt, in_=ps[:, 4, :], func=AF.Sigmoid,
                                                   bias=self.vcol(("v0", 1), fc), scale=1.0),
                     reads=[bps[4], self.b_vecs], writes=[b_vgt])
                P.op("dve", lambda e: e.tensor_tensor(out=stgf[fi][:], in0=stg[ia][:], in1=ps[:, bank, :], op=ALU.subtract),
                     reads=[b_stg[ia], bps[bank]], writes=[b_stgf[fi]])
                P.op("dve", lambda e: e.tensor_tensor(out=stgf[fi][:], in0=stgf[fi][:], in1=vgt, op=ALU.mult),
                     reads=[b_vgt], writes=[b_stgf[fi]])
                P.op("dve", lambda e: e.tensor_tensor(out=stg[ib][:], in0=stgf[fi][:], in1=ps[:, bank, :], op=ALU.add),
                     reads=[b_stgf[fi], bps[bank]], writes=[b_stg[ib]])
                P.dma(self.scr["V"][fc, :, tok], stg[ib][:], reads=[b_stg[ib]], writes=[self.b_scr["V"][ti]], e="act")
            bigproj("rw_wv", 3, v_consume)
        else:
            bigproj("rw_wv", 3, to_scratch(["V", "VF"]))
        lora1("rw_w1", 64, 1, AF.Tanh, [(0, 64, 0)])

        def ld_consume(fc, ti, tok, bank):
            fi = st["stgf"]
            st["stgf"] ^= 1
            P.op("act", lambda e: e.activation(out=stgf[fi][:], in_=ps[:, bank, :], func=AF.Sigmoid,
                                               bias=self.vcol(("w0", j), fc), scale=1.0),
                 reads=[bps[bank], self.b_vecs], writes=[b_stgf[fi]])
            P.op("pool", lambda e: e.tensor_scalar(out=stgf[fi][:], in0=stgf[fi][:], scalar1=-math.exp(-0.5),
                                                   scalar2=None, op0=ALU.mult), reads=[], writes=[b_stgf[fi]])
            P.dma(self.scr["LD"][fc, :, tok], stgf[fi][:], reads=[b_stgf[fi]], writes=[self.b_scr["LD"][ti]], e="act")
        lora2([("rw_w2", 64, 0)], ld_consume)
        lora1("rw_a1", 64, 4, AF.Copy, [(0, 64, 1)])
        lora2([("rw_a2", 64, 1)], to_scratch(["A"], AF.Sigmoid, ("a0", j)))
        lora1("rw_g1", 160, 5, AF.Sigmoid, [(0, 128, 0), (128, 32, 1)])
        lora2([("rw_g2a", 128, 0), ("rw_g2b", 32, 1)], to_scratch(["G"]))
        self.handover(news, allu)

    def rwkv_phase2(self, l, s, j):
        P = self.P
        uflat = self.u[:].rearrange("p a b -> p (a b)")
        v3 = lambda a: a.rearrange("p (c t) -> p c t", c=8)
        Rl, Kl, Al, Vl = [v3(uflat[:, i * 1024:(i + 1) * 1024]) for i in range(4)]
        LDl = v3(uflat[:, 4096:6144].bitcast(F32))
        AR = uflat[:, 6144:8192].rearrange("p (k c n) -> p k c n", k=8, c=2)
        BK = uflat[:, 8192:10240].rearrange("p (k c n) -> p k c n", k=8, c=2)
        bonus = v3(uflat[:, 10240:11264])
        BKT = uflat[:, 11264:13312].rearrange("p (c h n) -> p c h n", c=2, h=16)
        ZT = uflat[:, 13312:15360].rearrange("p (c h n) -> p c h n", c=2, h=16)
        AMs = uflat[:, 15360:17408].rearrange("p (h n) -> p h n", h=16)
        ATs = uflat[:, 17408:18432].rearrange("p (h n) -> p h n", h=16)
        Tm = uflat[:, 18432:19456].rearrange("p (h n) -> p h n", h=16)
        Gl = v3(uflat[:, 19456:20480])
        PLt = uflat[:, 20480:20512].bitcast(F32).rearrange("p (k c) -> p k c", k=8)
        WTs = uflat[:, 20544:21568].rearrange("p (h n) -> p h n", h=16)
        s0 = self.st32[0]
        SS = s0[:, 0:512].rearrange("p (k n) -> p k n", k=8)
        Yf = s0[:, 512:1536].rearrange("p (k n) -> p k n", k=8)
        SSb = s0[:, 1536:1792].bitcast(BF16).rearrange("p (k n) -> p k n", k=8)
        NSETS = 1
        s1f = self.st32[1]
        sb1f = self.stb[1][:, :].bitcast(F32)
        sb0f = self.stb[0][:, 768:2816].bitcast(F32)
        pool_f = [s0[:, 1792 + i * 128:1920 + i * 128] for i in range(8)] + \
                 [s1f[:, 2048 + i * 128:2176 + i * 128] for i in range(6)] + \
                 [sb1f[:, i * 128:(i + 1) * 128] for i in range(11)] + \
                 [sb0f[:, i * 128:(i + 1) * 128] for i in range(8)]
        Tsets = [pool_f[i * 8:(i + 1) * 8] for i in range(NSETS)]
        Tbsets = [[self.stb[0][:, (2 * i + q) * 128:(2 * i + q + 1) * 128] for q in range(2)] for i in range(NSETS)]
        s1 = self.st32[1][:, :].bitcast(BF16)
        Apow = [[s1[:, (2 * a + b) * 1024:(2 * a + b + 1) * 1024].rearrange("p (h n) -> p h n", h=16)
                 for b in range(2)] for a in range(2)]
        G1 = v3(self.stb[1][:, 0:2048].bitcast(F32))
        G2 = v3(self.stb[0][:, 768:2816].bitcast(F32))
        Yb = v3(self.st32[1][:, 2048:2560].bitcast(BF16))
        names = ["in", "ld", "AR", "BK", "bonus", "PL", "BKT", "ZTu", "ZTv", "AMs", "ATs", "Tm", "WTs",
                 "SS", "SSb", "Yf", "Gl", "G1", "G2", "Yb", "vin"] + [f"Tb{i}_{q}" for i in range(3) for q in range(2)] + \
                [f"T{i}_{q}" for i in range(3) for q in range(8)] + ["Ap00", "Ap01", "Ap10", "Ap11"]
        B = {n: Buf(n) for n in names}
        allold = [b for row in self.b_u for b in row] + self.b_st32 + self.b_stb
        self.handover(allold, list(B.values()))
        bAp = [[B["Ap00"], B["Ap01"]], [B["Ap10"], B["Ap11"]]]
        ps = self.psum
        bps = self.b_ps
        identb = self.cst("ident")
        psbf = lambda bank: ps[:, bank, :].bitcast(BF16)
        PAv = ps[:, 0:4, :].rearrange("p b (h n) -> p (b h) n", n=128)
        v64 = lambda b0: ps[0:64, b0:b0 + 2, :].rearrange("p b (h n) -> p (b h) n", n=64)
        mAb = self.cst("mA").unsqueeze(1).broadcast_to([128, 16, 128])
        mTb = self.cst("mT", 64)[0:64, :].unsqueeze(1).broadcast_to([64, 16, 64])
        idb = identb[0:64, 0:64].unsqueeze(1).broadcast_to([64, 16, 64])
        vc = lambda key, kc: self.vcol((key, j), kc)

        P.op("pool", lambda e: e.memset(SS[:], 0.0), writes=[B["SS"]])
        P.op("pool", lambda e: e.memset(SSb[:], 0.0), writes=[B["SSb"]])

        for ti in range(16):
            t0 = ti * 128
            tk = slice(t0, t0 + 128)
            tq = ti // 4
            for nm, dst in (("R", Rl), ("K", Kl), ("A", Al), ("V", Vl)):
                P.dma(dst[:, :, :], self.scr[nm][:, :, tk].rearrange("k p t -> p k t"),
                      reads=[self.b_scr[nm][tq]], writes=[B["vin"] if nm == "V" else B["in"]])
            P.dma(LDl[:, :, :], self.scr["LD"][:, :, tk].rearrange("k p t -> p k t"),
                  reads=[self.b_scr["LD"][tq]], writes=[B["ld"]])
            for kc in range(KC):
                K_, R_, A_, V_, LD_ = Kl[:, kc, :], Rl[:, kc, :], Al[:, kc, :], Vl[:, kc, :], LDl[:, kc, :]
                c2 = lambda a: a.rearrange("p (c n) -> p c n", c=2)
                si = (ti * KC + kc) % NSETS
                T = Tsets[si]
                Tb = Tbsets[si]
                bT = [B[f"T{si}_{q}"] for q in range(8)]
                bTb = [B[f"Tb{si}_{q}"] for q in range(2)]
                P.op("dve", lambda e: e.tensor_scalar(out=T[0], in0=K_, scalar1=vc("k_k", kc), scalar2=None,
                                                      op0=ALU.mult), reads=[B["in"], self.b_vecs], writes=[bT[0]])
                P.op("act", lambda e: e.activation(out=Tb[0], in_=T[0], func=AF.Square),
                     reads=[bT[0]], writes=[bTb[0]])
                P.op("pe", lambda e: e.matmul(ps[:, 6, 0:128], lhsT=self.cst("blk1"), rhs=Tb[0], start=True, stop=True),
                     reads=[bTb[0], self.b_const], writes=[bps[6]])
                P.op("act", lambda e: e.activation(out=T[1], in_=ps[:, 6, 0:128], func=AF.Sqrt),
                     reads=[bps[6]], writes=[bT[1]])
                P.op("dve", lambda e: e.tensor_scalar(out=T[1], in0=T[1], scalar1=1e-12, scalar2=None, op0=ALU.max),
                     reads=[], writes=[bT[1]])
                P.op("dve", lambda e: e.reciprocal(out=T[1], in_=T[1]), reads=[], writes=[bT[1]])
                P.op("dve", lambda e: e.tensor_tensor(out=T[2], in0=T[0], in1=T[1], op=ALU.mult),
                     reads=[bT[0], bT[1]], writes=[bT[2]])
                P.op("pool", lambda e: e.tensor_scalar(out=T[3], in0=A_, scalar1=-1.0, scalar2=vc("k_a", kc),
                                                       op0=ALU.add, op1=ALU.mult),
                     reads=[B["in"], self.b_vecs], writes=[bT[3]])
                P.op("dve", lambda e: e.scalar_tensor_tensor(out=T[3], in0=T[3], scalar=1.0, op0=ALU.add,
                                                             in1=K_, op1=ALU.mult),
                     reads=[bT[3], B["in"]], writes=[bT[3]])
                P.op("dve", lambda e: e.scalar_tensor_tensor(out=Tb[1], in0=R_, scalar=vc("r_k", kc), op0=ALU.mult,
                                                             in1=T[3], op1=ALU.mult),
                     reads=[bT[3], B["in"], self.b_vecs], writes=[bTb[1]])
                P.op("pe", lambda e: e.matmul(ps[:, 7, 0:128], lhsT=self.cst("blk1"), rhs=Tb[1], start=True, stop=True),
                     reads=[bTb[1], self.b_const], writes=[bps[7]])
                P.op("dve", lambda e: e.tensor_tensor(out=bonus[:, kc, :], in0=ps[:, 7, 0:128], in1=V_, op=ALU.mult),
                     reads=[bps[7], B["vin"]], writes=[B["bonus"]])
                P.op("dve", lambda e: e.tensor_tensor_scan(out=T[4], data0=self.cst("rmask", bf=False), data1=LD_,
                                                           initial=0.0, op0=ALU.mult, op1=ALU.add),
                     reads=[B["ld"], self.b_const], writes=[bT[4]])
                P.op("act", lambda e: e.activation(out=T[5], in_=T[4], func=AF.Exp), reads=[bT[4]], writes=[bT[5]])
                P.op("act", lambda e: e.activation(out=PLt[:, kc, :], in_=T[5][:, 63:128:64], func=AF.Copy),
                     reads=[bT[5]], writes=[B["PL"]])
                P.op("dve", lambda e: e.tensor_tensor(out=AR[:, kc, :, 64:128], in0=c2(R_), in1=c2(T[5]), op=ALU.mult),
                     reads=[bT[5], B["in"]], writes=[B["AR"]])
                P.op("act", lambda e: e.activation(out=T[6], in_=T[4], func=AF.Exp, scale=-1.0),
                     reads=[bT[4]], writes=[bT[6]])
                P.op("pool", lambda e: e.tensor_tensor(out=BK[:, kc, :, 64:128], in0=c2(T[3]), in1=c2(T[6]), op=ALU.mult),
                     reads=[bT[3], bT[6]], writes=[B["BK"]])
                P.op("pool", lambda e: e.tensor_tensor(out=T[7], in0=T[2], in1=A_, op=ALU.mult),
                     reads=[bT[2], B["in"]], writes=[bT[7]])
                P.op("pool", lambda e: e.tensor_tensor(out=BK[:, kc, :, 0:64], in0=c2(T[7]), in1=c2(T[6]), op=ALU.mult),
                     reads=[bT[7], bT[6]], writes=[B["BK"]])
                P.op("pool", lambda e: e.tensor_tensor(out=T[4], in0=T[4], in1=LD_, op=ALU.subtract),
                     reads=[B["ld"], bT[5], bT[6]], writes=[bT[4]])
                P.op("act", lambda e: e.activation(out=T[5], in_=T[4], func=AF.Exp),
                     reads=[bT[4], B["AR"], B["PL"]], writes=[bT[5]])
                P.op("dve", lambda e: e.scalar_tensor_tensor(out=AR[:, kc, :, 0:64], in0=c2(T[2]), scalar=-1.0,
                                                             op0=ALU.mult, in1=c2(T[5]), op1=ALU.mult),
                     reads=[bT[2], bT[5]], writes=[B["AR"]])
            import os
            if int(os.environ.get("RW_STOP", "9")) == 2 and os.environ.get("RW_PREP_ONLY"):
                continue
            for c in range(2):
                for hidx in range(16):
                    par, kc = hidx // 8, hidx % 8
                    pb = par * 64
                    P.op("pe", lambda e, par=par, kc=kc, pb=pb, c=c: e.transpose(
                        psbf(6 + par)[:, kc * 64:(kc + 1) * 64], BK[pb:pb + 64, kc, c, :],
                        identb[pb:pb + 64, pb:pb + 64]),
                        reads=[B["BK"], self.b_const], writes=[bps[6 + par]])
                for par in range(2):
                    P.op("act", lambda e, par=par, c=c: e.activation(
                        out=BKT[:, c, par * 8:(par + 1) * 8, :],
                        in_=psbf(6 + par)[:, 0:512].rearrange("p (k n) -> p k n", k=8), func=AF.Copy),
                        reads=[bps[6 + par]], writes=[B["BKT"]])
                for kc in range(KC):
                    P.op("pe", lambda e, kc=kc, c=c: e.transpose(
                        psbf(4)[64:128, kc * 128:(kc + 1) * 128], Vl[:, kc, c * 64:(c + 1) * 64], identb),
                        reads=[B["vin"], self.b_const], writes=[bps[4]])
                P.op("dve", lambda e, c=c: e.tensor_copy(
                    out=ZT[64:128, c, :, :].rearrange("p (par k) n -> p k par n", par=2),
                    in_=psbf(4)[64:128, :].rearrange("p (k par n) -> p k par n", k=8, par=2)),
                    reads=[bps[4]], writes=[B["ZTv"]])
            for c in range(2):
                hd = lambda hidx: (hidx // 8, hidx % 8, (hidx // 8) * 64)
                for hidx in range(16):
                    par, kc, pb = hd(hidx)
                    P.op("pe", lambda e, hidx=hidx, kc=kc, pb=pb: e.matmul(
                        PAv[:, hidx, :], lhsT=BK[pb:pb + 64, kc, c, :], rhs=AR[pb:pb + 64, kc, c, :],
                        start=True, stop=True), reads=[B["BK"], B["AR"]], writes=[bps[hidx // 4]])
                for hidx in range(16):
                    par, kc, pb = hd(hidx)
                    P.op("pe", lambda e, par=par, kc=kc, pb=pb: e.matmul(
                        ps[0:64, 4 + par, kc * 64:(kc + 1) * 64], lhsT=AR[pb:pb + 64, kc, c, 0:64],
                        rhs=BK[pb:pb + 64, kc, c, 0:64], start=True, stop=True),
                        reads=[B["BK"], B["AR"]], writes=[bps[4 + par]])
                P.op("dve", lambda e: e.tensor_tensor(out=AMs[:, :, :], in0=PAv, in1=mAb, op=ALU.mult),
                     reads=[bps[0], bps[1], bps[2], bps[3], self.b_const], writes=[B["AMs"]])
                P.op("dve", lambda e: e.tensor_tensor(out=ATs[0:64, :, :], in0=v64(4), in1=mTb, op=ALU.mult),
                     reads=[bps[4], bps[5], self.b_const], writes=[B["ATs"]])
                P.op("pool", lambda e: e.tensor_tensor(out=Tm[0:64, :, :], in0=AMs[0:64, :, 0:64], in1=idb, op=ALU.add),
                     reads=[B["AMs"], self.b_const], writes=[B["Tm"]])
                pA = lambda hidx: AMs[0:64, hidx, 0:64]
                pAT = lambda hidx: ATs[0:64, hidx, :]
                bA, bAT = B["AMs"], B["ATs"]
                for jj in range(1, 6):
                    a = jj % 2
                    if jj < 5:
                        for hidx in range(16):
                            P.op("pe", lambda e, hidx=hidx, pA=pA, pAT=pAT: e.matmul(
                                v64(0)[:, hidx, :], lhsT=pAT(hidx), rhs=pA(hidx), start=True, stop=True),
                                reads=[bA, bAT], writes=[bps[hidx // 8]])
                    for hidx in range(16):
                        P.op("pe", lambda e, hidx=hidx, pA=pA, pAT=pAT: e.matmul(
                            v64(2)[:, hidx, :], lhsT=pA(hidx), rhs=pAT(hidx), start=True, stop=True),
                            reads=[bA, bAT], writes=[bps[2 + hidx // 8]])
                    if jj < 5:
                        P.op("act", lambda e, a=a: e.activation(out=Apow[a][0][0:64, :, :], in_=v64(0), func=AF.Copy),
                             reads=[bps[0], bps[1]], writes=[bAp[a][0]])
                    P.op("dve", lambda e, a=a: e.tensor_copy(out=Apow[a][1][0:64, :, :], in_=v64(2)),
                         reads=[bps[2], bps[3]], writes=[bAp[a][1]])
                    for hidx in range(16):
                        P.op("pe", lambda e, hidx=hidx, a=a: e.matmul(
                            v64(4)[:, hidx, :], lhsT=Apow[a][1][0:64, hidx, :], rhs=Tm[0:64, hidx, :],
                            start=True, stop=True), reads=[bAp[a][1], B["Tm"]], writes=[bps[4 + hidx // 8]])
                    P.op("dve", lambda e: e.tensor_tensor(out=Tm[0:64, :, :], in0=v64(4), in1=Tm[0:64, :, :], op=ALU.add),
                         reads=[bps[4], bps[5]], writes=[B["Tm"]])
                    pA = (lambda a: (lambda hidx: Apow[a][0][0:64, hidx, :]))(a)
                    pAT = (lambda a: (lambda hidx: Apow[a][1][0:64, hidx, :]))(a)
                    bA, bAT = bAp[a][0], bAp[a][1]
                for hidx in range(16):
                    par, kc, pb = hd(hidx)
                    P.op("pe", lambda e, hidx=hidx, kc=kc, pb=pb: e.matmul(
                        v64(6)[:, hidx, :], lhsT=AR[pb:pb + 64, kc, c, 0:64], rhs=SSb[pb:pb + 64, kc, :],
                        start=True, stop=True), reads=[B["AR"], B["SSb"]], writes=[bps[6 + hidx // 8]])
                for hidx in range(16):
                    P.op("pe", lambda e, hidx=hidx: e.matmul(
                        v64(0)[:, hidx, :], lhsT=AMs[64:128, hidx, 0:64], rhs=ZT[64:128, c, hidx, :],
                        start=True, stop=True), reads=[B["AMs"], B["ZTv"]], writes=[bps[hidx // 8]])
                P.op("act", lambda e: e.activation(out=WTs[0:64, :, :], in_=v64(6), func=AF.Copy),
                     reads=[bps[6], bps[7]], writes=[B["WTs"]])
                P.op("dve", lambda e: e.tensor_tensor(out=WTs[0:64, :, :], in0=v64(0), in1=WTs[0:64, :, :], op=ALU.add),
                     reads=[bps[0], bps[1]], writes=[B["WTs"]])
                for hidx in range(16):
                    P.op("pe", lambda e, hidx=hidx: e.matmul(
                        v64(2)[:, hidx, :], lhsT=Tm[0:64, hidx, :], rhs=WTs[0:64, hidx, :], start=True, stop=True),
                        reads=[B["Tm"], B["WTs"]], writes=[bps[2 + hidx // 8]])
                P.op("act", lambda e: e.activation(out=ZT[0:64, c, :, :], in_=v64(2), func=AF.Copy),
                     reads=[bps[2], bps[3]], writes=[B["ZTu"]])
                for hidx in range(16):
                    par, kc, pb = hd(hidx)
                    P.op("pe", lambda e, par=par, kc=kc, pb=pb: e.matmul(
                        ps[pb:pb + 64, 6 + par, kc * 64:(kc + 1) * 64], lhsT=SSb[pb:pb + 64, kc, :],
                        rhs=AR[pb:pb + 64, kc, c, 64:128], start=True, stop=True),
                        reads=[B["AR"], B["SSb"]], writes=[bps[6 + par]])
                for hidx in range(16):
                    par, kc, pb = hd(hidx)
                    P.op("pe", lambda e, hidx=hidx, kc=kc, pb=pb: e.matmul(
                        ps[pb:pb + 64, 4, kc * 64:(kc + 1) * 64], lhsT=ZT[:, c, hidx, :], rhs=AMs[:, hidx, 64:128],
                        start=True, stop=True), reads=[B["ZTu"], B["ZTv"], B["AMs"]], writes=[bps[4]])
                ysl = slice(c * 64, (c + 1) * 64)
                k8 = lambda a: a.rearrange("p (k n) -> p k n", k=8)
                P.op("act", lambda e: e.activation(out=Yf[:, :, ysl], in_=k8(ps[:, 4, :]), func=AF.Copy),
                     reads=[bps[4]], writes=[B["Yf"]])
                P.op("dve", lambda e: e.tensor_tensor(out=Yf[0:64, :, ysl], in0=k8(ps[0:64, 6, :]), in1=Yf[0:64, :, ysl],
                                                      op=ALU.add), reads=[bps[6]], writes=[B["Yf"]])
                P.op("dve", lambda e: e.tensor_tensor(out=Yf[64:128, :, ysl], in0=k8(ps[64:128, 7, :]),
                                                      in1=Yf[64:128, :, ysl], op=ALU.add),
                     reads=[bps[7]], writes=[B["Yf"]])
                for hidx in range(16):
                    par, kc, pb = hd(hidx)
                    P.op("pe", lambda e, hidx=hidx, kc=kc, pb=pb: e.matmul(
                        ps[pb:pb + 64, 5, kc * 64:(kc + 1) * 64], lhsT=BKT[:, c, hidx, :], rhs=ZT[:, c, hidx, :],
                        start=True, stop=True), reads=[B["BKT"], B["ZTu"], B["ZTv"]], writes=[bps[5]])
                P.op("dve", lambda e: e.tensor_tensor(out=SS[:, :, :], in0=k8(ps[:, 5, :]), in1=SS[:, :, :], op=ALU.add),
                     reads=[bps[5]], writes=[B["SS"]])
                P.op("dve", lambda e, c=c: e.tensor_tensor(out=SS[:, :, :], in0=SS[:, :, :],
                                                           in1=PLt[:, :, c:c + 1].broadcast_to([128, 8, 64]), op=ALU.mult),
                     reads=[B["PL"]], writes=[B["SS"]])
                P.op("act", lambda e: e.activation(out=SSb[:, :, :], in_=SS[:, :, :], func=AF.Copy),
                     reads=[B["SS"]], writes=[B["SSb"]])
            P.dma(Gl[:, :, :], self.scr["G"][:, :, tk].rearrange("k p t -> p k t"),
                  reads=[self.b_scr["G"][tq]], writes=[B["Gl"]])
            f2 = lambda a, h: a[:, 4 * h:4 * h + 4, :]
            P.op("act", lambda e: e.activation(out=Yb[:, :, :], in_=Yf[:, :, :], func=AF.Copy),
                 reads=[B["Yf"]], writes=[B["Yb"]])
            for h2 in range(2):
                P.op("pe", lambda e, h2=h2: e.matmul(ps[:, 6 + h2, :], lhsT=self.cst("blk64"), rhs=f2(Yb, h2),
                                                     start=True, stop=True),
                     reads=[B["Yb"], self.b_const], writes=[bps[6 + h2]])
            P.op("dve", lambda e: e.tensor_tensor(out=G1[:, :, :], in0=Yf[:, :, :],
                                                  in1=ps[:, 6:8, :].rearrange("p b (k n) -> p (b k) n", n=128),
                                                  op=ALU.subtract),
                 reads=[B["Yf"], bps[6], bps[7]], writes=[B["G1"]])
            P.op("act", lambda e: e.activation(out=Yb[:, :, :], in_=G1[:, :, :], func=AF.Square),
                 reads=[B["G1"]], writes=[B["Yb"]])
            for h2 in range(2):
                P.op("pe", lambda e, h2=h2: e.matmul(ps[:, 6 + h2, :], lhsT=self.cst("blk64"), rhs=f2(Yb, h2),
                                                     start=True, stop=True),
                     reads=[B["Yb"], self.b_const], writes=[bps[6 + h2]])
            P.op("act", lambda e: e.activation(out=G2[:, :, :],
                                               in_=ps[:, 6:8, :].rearrange("p b (k n) -> p (b k) n", n=128),
                                               func=AF.Sqrt, bias=64e-5, scale=1.0),
                 reads=[bps[6], bps[7]], writes=[B["G2"]])
            P.op("dve", lambda e: e.reciprocal(out=G2[:, :, :], in_=G2[:, :, :]), reads=[], writes=[B["G2"]])
            P.op("dve", lambda e: e.tensor_tensor(out=G1[:, :, :], in0=G1[:, :, :], in1=G2[:, :, :], op=ALU.mult),
                 reads=[B["G2"]], writes=[B["G1"]])
            for kc in range(KC):
                P.op("pool", lambda e, kc=kc: e.tensor_scalar(out=G1[:, kc, :], in0=G1[:, kc, :],
                                                              scalar1=vc("ln_w", kc), scalar2=vc("ln_b", kc),
                                                              op0=ALU.mult, op1=ALU.add),
                     reads=[self.b_vecs], writes=[B["G1"]])
            P.op("pool", lambda e: e.tensor_tensor(out=G1[:, :, :], in0=G1[:, :, :], in1=bonus[:, :, :], op=ALU.add),
                 reads=[B["bonus"]], writes=[B["G1"]])
            P.op("dve", lambda e: e.tensor_tensor(out=self.hb[:, :, tk], in0=G1[:, :, :], in1=Gl[:, :, :], op=ALU.mult),
                 reads=[B["G1"], B["Gl"]], writes=[self.b_hb[tq]])
        self.handover(list(B.values()), allold)


    def emit_mamba(self, l, s):
        P = self.P
        ps, bps = self.psum, self.b_ps
        uflat = self.u[:].rearrange("p a b -> p (a b)")
        allu = [b for row in self.b_u for b in row]
        preb = [uflat[:, i * 2052:(i + 1) * 2052] for i in range(2)]
        dg = [uflat[:, 4104 + i * 512:4616 + i * 512].rearrange("p (t n) -> p t n", t=4) for i in range(2)]
        ob = [uflat[:, 8200 + i * 2048:10248 + i * 2048] for i in range(2)]
        wdt_sb = uflat[:, 12296:12552]
        b_pre, b_dg, b_wdt = [Buf("pre0"), Buf("pre1")], [Buf("dg0"), Buf("dg1")], Buf("wdt")
        b_ob = [Buf("ob0"), Buf("ob1")]
        news = b_pre + b_dg + [b_wdt] + b_ob
        self.handover(allu, news)
        for i in range(2):
            P.op("pool", lambda e, i=i: e.memset(preb[i][:, 0:3], 0.0), writes=[b_pre[i]])
        tiles = [(slice(t * 512, (t + 1) * 512), 512) for t in range(4)]
        obi = [0]

        def cons_z(fc, ti, tok, bank):
            if ti == 0:
                obi[0] ^= 1
            i = obi[0]
            P.op("act", lambda e: e.activation(out=ob[i][:, tok], in_=ps[:, bank, :], func=AF.Silu),
                 reads=[bps[bank]], writes=[b_ob[i]])
            if ti == 3:
                P.dma(self.scr["MZ"][fc], ob[i][:], reads=[b_ob[i]], writes=[self.b_scr["MZ"][fc]], e="act")

        def cons_conv(fc, ti, tok, bank):
            ch = fc
            pi = fc % 2
            P.op("act", lambda e: e.activation(out=preb[pi][:, 3 + ti * 512:3 + (ti + 1) * 512], in_=ps[:, bank, :],
                                               func=AF.Copy), reads=[bps[bank]], writes=[b_pre[pi]])
            if ti == 0:
                for tap in range(4):
                    P.op("pool", lambda e, tap=tap: e.tensor_scalar(
                        out=dg[pi][:, tap, :], in0=self.cst("ident"), scalar1=self.vcol(("cw", tap), ch), scalar2=None,
                        op0=ALU.mult), reads=[self.b_const, self.b_vecs], writes=[b_dg[pi]])
            if ti == 3:
                obi[0] ^= 1
                i = obi[0]
                for t2 in range(4):
                    cb_ = 4 + (t2 % 2)
                    for tap in range(4):
                        P.op("pe", lambda e, tap=tap, t2=t2, cb_=cb_: e.matmul(
                            ps[:, cb_, :], lhsT=dg[pi][:, tap, :], rhs=preb[pi][:, tap + t2 * 512:tap + t2 * 512 + 512],
                            start=(tap == 0), stop=(tap == 3)), reads=[b_dg[pi], b_pre[pi]], writes=[bps[cb_]])
                    P.op("act", lambda e, t2=t2, cb_=cb_: e.activation(
                        out=ob[i][:, t2 * 512:(t2 + 1) * 512], in_=ps[:, cb_, :], func=AF.Silu,
                        bias=self.vcol(("cbias",), ch), scale=1.0),
                        reads=[bps[cb_], self.b_vecs], writes=[b_ob[i]])
                nm, idx = ("MX", ch) if ch < 16 else (("MB", ch - 16) if ch < 24 else ("MC", ch - 24))
                P.dma(self.scr[nm][idx], ob[i][:], reads=[b_ob[i]], writes=[self.b_scr[nm][idx]], e="act")

        rhs = lambda kc, tok: self.hb[:, kc, tok]
        rb = lambda ti: [self.b_hb[ti]]
        self.proj(self.dw["mb_win"][0:16], 16, 8, rhs, rb, tiles, cons_z)
        self.proj(self.dw["mb_win"][16:48], 32, 8, rhs, rb, tiles, cons_conv)
        P.dma(self.st32[0][:, 0:256], self.dw["mb_wdt"][0], writes=[self.b_st32[0]])
        P.op("pool", lambda e: e.tensor_copy(out=wdt_sb, in_=self.st32[0][:, 0:256]),
             reads=[self.b_st32[0]], writes=[b_wdt])
        for c in range(16):
            for kc in range(KC):
                P.op("pe", lambda e, c=c, kc=kc: e.matmul(
                    ps[:, 4, c * 32:(c + 1) * 32], lhsT=self.hb[:, kc, c * 128:(c + 1) * 128],
                    rhs=wdt_sb[:, kc * 32:(kc + 1) * 32], start=(kc == 0), stop=(kc == KC - 1)),
                    reads=[b_wdt, self.b_hb[c // 4]], writes=[bps[4]])
        dtk, dA, ea, dend, etot = self.mbdt, self.tmpf[0], self.tmpf[1], self.tmpf[2], self.tmpf[3]
        bdt = [self.b_mbdt] + self.b_tmpf
        c3 = lambda a: a[:, :].rearrange("p (c h) -> p c h", c=16)
        bc16 = lambda key: self.vcol((key,), 0, 32).unsqueeze(1).broadcast_to([128, 16, 32])
        P.op("dve", lambda e: e.tensor_tensor(out=c3(dtk), in0=c3(ps[:, 4, :]), in1=bc16("dtb"), op=ALU.add),
             reads=[bps[4], self.b_vecs], writes=[bdt[0]])
        P.op("act", lambda e: e.activation(out=dtk[:, :], in_=dtk[:, :], func=AF.Exp), reads=[], writes=[bdt[0]])
        P.op("act", lambda e: e.activation(out=dtk[:, :], in_=dtk[:, :], func=AF.Ln, bias=1.0, scale=1.0),
             reads=[], writes=[bdt[0]])
        P.op("act", lambda e: e.activation(out=ea[:, 0:32], in_=self.vcol(("alog",), 0, 32), func=AF.Exp),
             reads=[self.b_vecs], writes=[bdt[2]])
        P.op("dve", lambda e: e.scalar_tensor_tensor(
            out=c3(dA), in0=c3(dtk), scalar=-1.0, op0=ALU.mult,
            in1=ea[:, 0:32].unsqueeze(1).broadcast_to([128, 16, 32]), op1=ALU.mult),
            reads=[bdt[0], bdt[2]], writes=[bdt[1]])
        P.op("pe", lambda e: e.matmul(ps[:, 5, :], lhsT=self.cst("mcur", bf=False), rhs=dA[:, :], start=True, stop=True),
             reads=[bdt[1], self.b_const], writes=[bps[5]])
        P.op("pe", lambda e: e.matmul(ps[:, 6, :], lhsT=self.cst("onesf", bf=False), rhs=dA[:, :], start=True, stop=True),
             reads=[bdt[1], self.b_const], writes=[bps[6]])
        P.op("act", lambda e: e.activation(out=ea[:, :], in_=ps[:, 5, :], func=AF.Exp), reads=[bps[5]], writes=[bdt[2]])
        P.op("act", lambda e: e.activation(out=etot[:, :], in_=ps[:, 6, :], func=AF.Exp), reads=[bps[6]], writes=[bdt[4]])
        P.op("act", lambda e: e.activation(out=dend[:, :], in_=ps[:, 5, :], func=AF.Copy), reads=[bps[5]], writes=[bdt[3]])
        P.op("dve", lambda e: e.tensor_tensor(out=dend[:, :], in0=ps[:, 6, :], in1=dend[:, :], op=ALU.subtract),
             reads=[bps[6]], writes=[bdt[3]])
        P.op("act", lambda e: e.activation(out=dend[:, :], in_=dend[:, :], func=AF.Exp), reads=[], writes=[bdt[3]])

        import os
        if os.environ.get("MB_STOP") == "1":
            return
        hflat = self.hb[:].rearrange("p a b -> p (a b)")
        Sst = hflat[:, 0:4096].bitcast(F32).rearrange("p (g n) -> p g n", g=8)
        Sb = hflat[:, 4096:6144].rearrange("p (g n) -> p g n", g=8)
        yT = hflat[:, 6144:14336].rearrange("p (c t) -> p c t", c=16)
        xsT = uflat[:, 0:2048].rearrange("p (c t) -> p c t", c=16)
        CT = uflat[:, 2048:3072].rearrange("p (g t) -> p g t", g=8)
        BT = uflat[:, 3072:4096].rearrange("p (g t) -> p g t", g=8)
        xtok = uflat[:, 4096:6144]
        Btok = uflat[:, 6144:7168].rearrange("p (g n) -> p g n", g=8)
        X = uflat[:, 7168:9216]
        Xd = uflat[:, 9216:11264]
        Yb = uflat[:, 11264:13312]
        CBm = uflat[:, 13312:14336].rearrange("p (g t) -> p g t", g=8)
        Mh = [uflat[:, 14336 + i * 128:14464 + i * 128] for i in range(2)]
        Dm = [uflat[:, 14592 + i * 256:14848 + i * 256].bitcast(F32) for i in range(2)]
        lh = [uflat[:, 15104 + i * 256:15360 + i * 256].bitcast(F32) for i in range(2)]
        Yg = uflat[:, 15616:16128].bitcast(F32)
        gz = [uflat[:, 16128 + i * 512:16640 + i * 512] for i in range(2)]
        gx = [uflat[:, 17152 + i * 512:17664 + i * 512] for i in range(2)]
        gt = [uflat[:, 18176 + i * 1024:19200 + i * 1024].bitcast(F32) for i in range(2)]
        gsq = uflat[:, 20224:20736]
        grs = uflat[:, 20736:21760].bitcast(F32)
        nm2 = ["S", "Sb", "yT", "xsT", "CT", "BT", "xtok", "Btok", "X", "Xd", "Yb", "CBm", "Mh0", "Mh1", "Dm0", "Dm1",
               "lh0", "lh1", "Yg", "gz0", "gz1", "gx0", "gx1", "gt0", "gt1", "gsq", "grs"]
        B = {n: Buf(n) for n in nm2}
        self.handover(news + self.b_hb, list(B.values()))
        identb = self.cst("ident")
        psbf = lambda bank: ps[:, bank, :].bitcast(BF16)
        P.op("pool", lambda e: e.memset(Sst[:, :, :], 0.0), writes=[B["S"]])
        P.op("pool", lambda e: e.memset(Sb[:, :, :], 0.0), writes=[B["Sb"]])
        i0 = self.co_idx(l, 1, s)
        h64 = lambda a: a.rearrange("p (h n) -> p h n", n=64)

        for c in range(16):
            tk = slice(c * 128, (c + 1) * 128)
            P.dma(xsT[:, :, :], self.scr["MX"][:, :, tk].rearrange("k p t -> p k t"),
                  reads=self.b_scr["MX"], writes=[B["xsT"]])
            P.dma(CT[:, :, :], self.scr["MC"][:, :, tk].rearrange("k p t -> p k t"),
                  reads=self.b_scr["MC"], writes=[B["CT"]])
            P.dma(BT[:, :, :], self.scr["MB"][:, :, tk].rearrange("k p t -> p k t"),
                  reads=self.b_scr["MB"], writes=[B["BT"]])
            for ch in range(16):
                P.op("pe", lambda e, ch=ch: e.transpose(psbf(ch // 8)[:, (ch % 8) * 128:(ch % 8 + 1) * 128],
                                                        xsT[:, ch, :], identb),
                     reads=[B["xsT"], self.b_const], writes=[bps[ch // 8]])
            for hf in range(2):
                P.op("act" if hf else "dve", (lambda e, hf=hf: e.activation(out=xtok[:, hf * 1024:(hf + 1) * 1024],
                     in_=psbf(hf)[:, :], func=AF.Copy)) if hf else
                     (lambda e, hf=hf: e.tensor_copy(out=xtok[:, hf * 1024:(hf + 1) * 1024], in_=psbf(hf)[:, :])),
                     reads=[bps[hf]], writes=[B["xtok"]])
            for g in range(8):
                P.op("pe", lambda e, g=g: e.transpose(psbf(2)[:, g * 128:(g + 1) * 128], BT[:, g, :], identb),
                     reads=[B["BT"], self.b_const], writes=[bps[2]])
            P.op("act", lambda e: e.activation(out=Btok[:, :, :], in_=psbf(2)[:, :].rearrange("p (g n) -> p g n", g=8),
                                               func=AF.Copy), reads=[bps[2]], writes=[B["Btok"]])
            P.op("dve", lambda e, c=c: e.tensor_tensor(
                out=h64(X), in0=h64(xtok), in1=dtk[:, c * 32:(c + 1) * 32].unsqueeze(2).broadcast_to([128, 32, 64]),
                op=ALU.mult), reads=[B["xtok"], bdt[0]], writes=[B["X"]])
            P.op("pool", lambda e, c=c: e.tensor_tensor(
                out=h64(Xd), in0=h64(X), in1=dend[:, c * 32:(c + 1) * 32].unsqueeze(2).broadcast_to([128, 32, 64]),
                op=ALU.mult), reads=[B["X"], bdt[3]], writes=[B["Xd"]])
            for g in range(8):
                P.op("pe", lambda e, g=g: e.matmul(ps[:, 6 + g // 4, (g % 4) * 128:(g % 4 + 1) * 128], lhsT=BT[:, g, :],
                                                   rhs=CT[:, g, :], start=True, stop=True),
                     reads=[B["BT"], B["CT"]], writes=[bps[6 + g // 4]])
            P.op("dve", lambda e: e.tensor_tensor(
                out=CBm[:, :, :], in0=ps[:, 6:8, :].rearrange("p b (g t) -> p (b g) t", t=128),
                in1=self.cst("mcur").unsqueeze(1).broadcast_to([128, 8, 128]), op=ALU.mult),
                reads=[bps[6], bps[7], self.b_const], writes=[B["CBm"]])
            for g in range(8):
                yb_ = 3 + (g % 2)
                for hh in range(4):
                    h = g * 4 + hh
                    a = h % 2
                    P.op("act", lambda e, a=a, h=h, c=c: e.activation(
                        out=lh[a], in_=self.cst("mprev", bf=False), func=AF.Copy,
                        scale=dA[:, c * 32 + h:c * 32 + h + 1]),
                        reads=[bdt[1], self.b_const], writes=[B[f"lh{a}"]])
                    P.op("pe", lambda e, a=a: e.matmul(ps[:, a, 0:128], lhsT=lh[a], rhs=self.cst("mcur", bf=False),
                                                       start=True, stop=True),
                         reads=[B[f"lh{a}"], self.b_const], writes=[bps[a]])
                    P.op("act", lambda e, a=a: e.activation(out=Dm[a], in_=ps[:, a, 0:128], func=AF.Exp),
                         reads=[bps[a]], writes=[B[f"Dm{a}"]])
                    P.op("dve", lambda e, a=a, g=g: e.tensor_tensor(out=Mh[a], in0=CBm[:, g, :], in1=Dm[a], op=ALU.mult),
                         reads=[B["CBm"], B[f"Dm{a}"]], writes=[B[f"Mh{a}"]])
                    P.op("pe", lambda e, a=a, h=h, hh=hh, yb_=yb_: e.matmul(
                        ps[:, yb_, hh * 64:(hh + 1) * 64], lhsT=Mh[a], rhs=X[:, h * 64:(h + 1) * 64],
                        start=True, stop=True), reads=[B[f"Mh{a}"], B["X"]], writes=[bps[yb_]])
                P.op("pe", lambda e, g=g: e.matmul(ps[:, 5, 0:256], lhsT=CT[:, g, :], rhs=Sb[:, g, :], start=True, stop=True),
                     reads=[B["CT"], B["Sb"]], writes=[bps[5]])
                P.op("dve", lambda e, g=g, c=c: e.tensor_tensor(
                    out=h64(Yg), in0=h64(ps[:, 5, 0:256]),
                    in1=ea[:, c * 32 + g * 4:c * 32 + g * 4 + 4].unsqueeze(2).broadcast_to([128, 4, 64]), op=ALU.mult),
                    reads=[bps[5], bdt[2]], writes=[B["Yg"]])
                P.op("dve", lambda e, g=g, yb_=yb_: e.tensor_tensor(out=Yb[:, g * 256:(g + 1) * 256], in0=ps[:, yb_, 0:256],
                                                                    in1=Yg, op=ALU.add),
                     reads=[bps[yb_], B["Yg"]], writes=[B["Yb"]])
                P.op("pe", lambda e, g=g: e.matmul(ps[:, 2, 0:256], lhsT=Btok[:, g, :], rhs=Xd[:, g * 256:(g + 1) * 256],
                                                   start=True, stop=True),
                     reads=[B["Btok"], B["Xd"]], writes=[bps[2]])
                P.op("pool", lambda e, g=g, c=c: e.tensor_tensor(
                    out=h64(Sst[:, g, :]), in0=h64(Sst[:, g, :]),
                    in1=etot[:, c * 32 + g * 4:c * 32 + g * 4 + 4].unsqueeze(2).broadcast_to([128, 4, 64]), op=ALU.mult),
                    reads=[bdt[4], B["Sb"]], writes=[B["S"]])
                P.op("dve", lambda e, g=g: e.tensor_tensor(out=Sst[:, g, :], in0=ps[:, 2, 0:256], in1=Sst[:, g, :], op=ALU.add),
                     reads=[bps[2]], writes=[B["S"]])
                P.op("act", lambda e, g=g: e.activation(out=Sb[:, g, :], in_=Sst[:, g, :], func=AF.Copy),
                     reads=[B["S"], bps[5]], writes=[B["Sb"]])
            cl = c % 4
            for ch in range(16):
                P.op("pe", lambda e, ch=ch: e.transpose(psbf(ch // 8)[:, (ch % 8) * 128:(ch % 8 + 1) * 128],
                                                        Yb[:, ch * 128:(ch + 1) * 128], identb),
                     reads=[B["Yb"], self.b_const], writes=[bps[ch // 8]])
            for hf in range(2):
                P.op("act", lambda e, hf=hf, cl=cl: e.activation(
                    out=yT[:, hf * 8:(hf + 1) * 8, cl * 128:(cl + 1) * 128],
                    in_=psbf(hf)[:, :].rearrange("p (k n) -> p k n", k=8), func=AF.Copy),
                    reads=[bps[hf]], writes=[B["yT"]])
            if cl != 3 or os.environ.get("MB_STOP") == "2":
                continue
            t = c // 4
            tok = slice(t * 512, (t + 1) * 512)
            for g2 in range(8):
                for q in range(2):
                    ch = g2 * 2 + q
                    P.dma(gz[q][:], self.scr["MZ"][ch, :, tok], reads=[self.b_scr["MZ"][ch]], writes=[B[f"gz{q}"]])
                    P.dma(gx[q][:], self.scr["MX"][ch, :, tok], reads=[self.b_scr["MX"][ch]], writes=[B[f"gx{q}"]])
                    P.op("dve", lambda e, q=q, ch=ch: e.scalar_tensor_tensor(
                        out=gt[q], in0=gx[q][:], scalar=self.vcol(("mbD",), ch), op0=ALU.mult, in1=yT[:, ch, :], op1=ALU.add),
                        reads=[B[f"gx{q}"], B["yT"], self.b_vecs], writes=[B[f"gt{q}"]])
                    P.op("pool", lambda e, q=q: e.tensor_tensor(out=gt[q], in0=gt[q], in1=gz[q][:], op=ALU.mult),
                         reads=[B[f"gz{q}"]], writes=[B[f"gt{q}"]])
                    P.op("act", lambda e, q=q: e.activation(out=gsq, in_=gt[q], func=AF.Square),
                         reads=[B[f"gt{q}"]], writes=[B["gsq"]])
                    P.op("pe", lambda e, q=q: e.matmul(ps[:, 3, :], lhsT=self.ones_b[:], rhs=gsq, start=(q == 0), stop=(q == 1)),
                         reads=[B["gsq"], self.b_const], writes=[bps[3]])
                P.op("act", lambda e: e.activation(out=grs, in_=ps[:, 3, :], func=AF.Sqrt, bias=EPS, scale=4.0),
                     reads=[bps[3]], writes=[B["grs"]])
                P.op("dve", lambda e: e.reciprocal(out=grs, in_=grs), reads=[], writes=[B["grs"]])
                for q in range(2):
                    ch = g2 * 2 + q
                    P.op("dve", lambda e, q=q, ch=ch: e.scalar_tensor_tensor(
                        out=yT[:, ch, :], in0=gt[q], scalar=self.vcol(("mbng",), ch), op0=ALU.mult, in1=grs, op1=ALU.mult),
                        reads=[B[f"gt{q}"], B["grs"], self.b_vecs], writes=[B["yT"]])
            for fc in range(8):
                w, bw = self.load_w(self.dw["mb_wo"][fc], 2048)
                bank = 4 + (fc % 2)
                for ch in range(16):
                    P.op("pe", lambda e, ch=ch, w=w, bank=bank: e.matmul(
                        ps[:, bank, :], lhsT=w[:, ch * 128:(ch + 1) * 128], rhs=yT[:, ch, :],
                        start=(ch == 0), stop=(ch == 15)), reads=[bw, B["yT"]], writes=[bps[bank]])
                P.op("dve", lambda e, fc=fc, bank=bank: e.scalar_tensor_tensor(
                    out=self.xT[:, fc, tok], in0=ps[:, bank, :], scalar=self.gco[:, i0 + fc:i0 + fc + 1], op0=ALU.mult,
                    in1=self.xT[:, fc, tok], op1=ALU.add),
                    reads=[bps[bank], self.b_mod, self.b_xT[fc][t]], writes=[self.b_xT[fc][t]])
        self.handover(list(B.values()), allu + self.b_hb)


def make_in_maps(inp, ncores, nseq):
    H = build_host_arrays(inp)
    vecs = H["vecs"].pack()
    consts, coff = build_consts()
    x = np.asarray(inp["x"], np.float32)
    c = np.asarray(inp["c"], np.float32)
    maps = []
    for core in range(ncores):
        xs = x[core * nseq:(core + 1) * nseq]
        xT = np.ascontiguousarray(xs.transpose(0, 2, 1)).reshape(nseq, KC, 128, S)
        cs = c[core * nseq:(core + 1) * nseq]
        cT = np.ascontiguousarray(cs.reshape(nseq, KC, 128).transpose(2, 1, 0)).reshape(128, KC * nseq)
        m = {"xT": xT, "cT": cT, "vecs": vecs, "ada_w": H["ada_w"].reshape(DEPTH, 18, 128, 4096),
             "ffn_w_in": H["ffn_w_in"], "ffn_w_out": H["ffn_w_out"], "consts": consts}
        for k in W_SPECS:
            m[k] = H[k]
        maps.append(m)
    return maps, H["vecs"].off, vecs.shape[1], coff, consts.shape[1]


def run(inp, ncores=NCORES, nseq=NSEQ_CORE, n_layers=DEPTH, mixers=True, kinds=None, trace=False):
    maps, voff, nvec, coff, nconst = make_in_maps(inp, ncores, nseq)
    b = Builder(nseq, n_layers, voff, nvec, mixers=mixers, kinds=kinds, const_off=coff, nconst=nconst)
    nc = b.build()
    if trace:
        res = run_bass_kernel_spmd(nc, maps, core_ids=list(range(ncores)), trace=True)
        print('EXEC_TIME_NS', res.exec_time_ns)
    else:
        res = run_bass_kernel_spmd(nc, maps, core_ids=list(range(ncores)))
    outs = []
    for r in res.results:
        yT = r["yT"].reshape(nseq, D, S)
        outs.append(yT.transpose(0, 2, 1))
    return np.ascontiguousarray(np.concatenate(outs, axis=0))


def kernel(**inputs):
    return run(inputs)
```

```python
import math
from contextlib import ExitStack
import numpy as np
import concourse.bass as bass
import concourse.mybir as mybir
from concourse.bass_utils import run_bass_kernel_spmd

F32 = mybir.dt.float32
BF16 = mybir.dt.bfloat16
AF = mybir.ActivationFunctionType
ALU = mybir.AluOpType
AX = mybir.AxisListType

D = 1024
S = 2048
KC = 8
DFF = 2816
NJ = 22
DEPTH = 4
NCORES = 8
NSEQ_CORE = 4
EPS = 1e-6

SEM_BLOCK = 30000
ENGS = ("pe", "dve", "act", "pool", "sp")


class Buf:
    __slots__ = ("name", "last_w", "readers", "excl")

    def __init__(self, name="", excl=False):
        self.name = name
        self.last_w = None
        self.readers = []
        self.excl = excl


class Prog:
    def __init__(self, nc, stack, n_dma_sems=32):
        self.nc = nc
        self.stack = stack
        self.eng = {"pe": nc.tensor, "dve": nc.vector, "act": nc.scalar,
                    "pool": nc.gpsimd, "sp": nc.sync}
        self.count = {e: 0 for e in ENGS}
        self.sems = {e: [] for e in ENGS}
        self.waited = {e: {} for e in ENGS}
        self.waited_dma = {e: {} for e in ENGS}
        self.dma_sems = [stack.enter_context(nc.semaphore(f"dsem{i}")) for i in range(n_dma_sems)]
        self.dma_val = [0] * n_dma_sems
        self.dma_k = 0
        self.n_waits = 0
        self.n_dma = 0

    def _sem_for(self, e, idx):
        b = idx // SEM_BLOCK
        while len(self.sems[e]) <= b:
            self.sems[e].append(self.stack.enter_context(
                self.nc.semaphore(f"s_{e}_{len(self.sems[e])}")))
        return self.sems[e][b], (idx % SEM_BLOCK) + 1

    def _wait_tok(self, e, tok, same=False):
        if tok is None:
            return
        eng = self.eng[e]
        if tok[0] == "c":
            _, se, idx = tok
            if se == e and not same:
                return
            b = idx // SEM_BLOCK
            key = (se, b)
            if self.waited[e].get(key, -1) >= idx:
                return
            self.waited[e][key] = idx
            for bb in range(b):
                self.waited[e][(se, bb)] = 1 << 60
            sem, val = self._sem_for(se, idx)
            eng.wait_ge(sem, val)
            self.n_waits += 1
        else:
            _, si, val = tok
            if self.waited_dma[e].get(si, 0) >= val:
                return
            self.waited_dma[e][si] = val
            eng.wait_ge(self.dma_sems[si], val)
            self.n_waits += 1

    def _deps(self, e, reads, writes, same=False):
        for b in reads:
            self._wait_tok(e, b.last_w, same)
            if b.excl:
                for r in b.readers:
                    self._wait_tok(e, r, same)
        for b in writes:
            self._wait_tok(e, b.last_w, same)
            for r in b.readers:
                self._wait_tok(e, r, same)

    def _commit(self, tok, reads, writes):
        for b in reads:
            b.readers.append(tok)
            if len(b.readers) > 64:
                b.readers = b.readers[-64:] if False else b.readers
        for b in writes:
            b.last_w = tok
            b.readers = []

    def op(self, e, fn, reads=(), writes=()):
        self._deps(e, reads, writes)
        idx = self.count[e]
        inst = fn(self.eng[e])
        sem, _ = self._sem_for(e, idx)
        inst.then_inc(sem, 1)
        self.count[e] = idx + 1
        self._commit(("c", e, idx), reads, writes)

    def dma(self, out, in_, reads=(), writes=(), e="sp", **kw):
        self._deps(e, reads, writes, same=True)
        si = self.dma_k % len(self.dma_sems)
        self.dma_k += 1
        prev = self.dma_val[si]
        if prev:
            self._wait_tok(e, ("d", si, prev))
        val = prev + 16
        self.dma_val[si] = val
        self.eng[e].dma_start(out=out, in_=in_, **kw).then_inc(self.dma_sems[si], 16)
        self.n_dma += 1
        tok = ("d", si, val)
        self._commit(tok, reads, writes)
        return tok

    def wait_all_dma(self, e="sp"):
        for si, v in enumerate(self.dma_val):
            if v:
                self._wait_tok(e, ("d", si, v))


def _compact_readers(b):
    pass


def fm_cols(v):
    v = np.asarray(v, np.float32).reshape(-1)
    n = v.size // 128
    return np.ascontiguousarray(v.reshape(n, 128).T)


def tile_w(W, kpad=None):
    K, F = W.shape
    kch = (K + 127) // 128
    if K % 128:
        Wp = np.zeros((kch * 128, F), np.float32)
        Wp[:K] = W
        W = Wp
    nf = F // 128
    return np.ascontiguousarray(W.reshape(kch, 128, nf, 128).transpose(2, 1, 0, 3))


class VecPack:
    def __init__(self):
        self.cols = []
        self.off = {}
        self.n = 0

    def add(self, key, arr2d):
        self.off[key] = self.n
        self.cols.append(np.asarray(arr2d, np.float32))
        self.n += arr2d.shape[1]

    def pack(self):
        return np.ascontiguousarray(np.concatenate(self.cols, axis=1))


def vec_keys_sizes():
    ks = []
    for l in range(DEPTH):
        ks.append((("ng", l), 24))
        ks.append((("adab", l), 72))
    return ks


def build_host_arrays(inp):
    H = {}
    vp = VecPack()
    for l in range(DEPTH):
        vp.add(("ng", l), fm_cols(inp["norm_g"][l]))
        vp.add(("adab", l), fm_cols(inp["ada_b"][l]))
    H["vecs"] = vp
    build_host_mixers(inp, H, vp)
    aw = np.asarray(inp["ada_w"], np.float32)
    H["ada_w"] = np.ascontiguousarray(
        aw.reshape(DEPTH, 8, 128, 18, 512).transpose(0, 3, 2, 1, 4))
    wi = np.asarray(inp["ffn_w_in"], np.float32).reshape(DEPTH, 2, 8, 128, 2, NJ, 128)
    H["ffn_w_in"] = np.ascontiguousarray(wi.transpose(0, 1, 5, 3, 2, 4, 6)).reshape(
        DEPTH, 2, NJ, 128, 8 * 256)
    wo = np.asarray(inp["ffn_w_out"], np.float32).reshape(DEPTH, 2, NJ, 128, 8, 128)
    H["ffn_w_out"] = np.ascontiguousarray(wo.transpose(0, 1, 4, 3, 2, 5)).reshape(
        DEPTH, 2, 8, 128, NJ * 128)
    return H


def build_consts():
    C = {}
    p = np.arange(128)[:, None]
    q = np.arange(128)[None, :]
    C["ident"] = (p == q).astype(np.float32)
    C["blk64"] = ((p // 64) == (q // 64)).astype(np.float32) / 64.0
    C["mcur"] = (q >= p).astype(np.float32)
    C["mprev"] = (p > q).astype(np.float32)
    C["blk1"] = ((p // 64) == (q // 64)).astype(np.float32)
    i64, t64 = p % 64, q % 64
    C["mA"] = np.where(q < 64, i64 < t64, i64 <= t64).astype(np.float32)
    C["mT"] = (t64 < i64).astype(np.float32)[:, 0:64]
    C["rmask"] = np.broadcast_to((q % 64 != 0), (128, 128)).astype(np.float32)
    C["onesf"] = np.ones((128, 128), np.float32)
    off = {}
    cols = []
    n = 0
    for k, v in C.items():
        off[k] = n
        cols.append(v)
        n += v.shape[1]
    return np.ascontiguousarray(np.concatenate(cols, axis=1)), off


def build_host_mixers(inp, H, vp):
    wqkv = np.asarray(inp["sw_w_qkv"][0], np.float32)
    H["sw_wq"] = tile_w(wqkv[:, 0:1024]).reshape(8, 128, 1024)
    wk = wqkv[:, 1024:1280].reshape(1024, 4, 64)
    wkd = np.concatenate([wk, wk], axis=2).reshape(1024, 512)
    H["sw_wk"] = tile_w(wkd).reshape(4, 128, 1024)
    H["sw_wv"] = np.ascontiguousarray(
        wqkv[:, 1280:1536].reshape(8, 128, 256).transpose(1, 0, 2)).reshape(1, 128, 2048)
    H["sw_wo"] = tile_w(np.asarray(inp["sw_w_o"][0], np.float32)).reshape(8, 128, 1024)
    vp.add(("swq",), np.tile(np.asarray(inp["sw_q_norm"][0], np.float32), 2).reshape(128, 1))
    vp.add(("swk",), np.tile(np.asarray(inp["sw_k_norm"][0], np.float32), 2).reshape(128, 1))
    vp.add(("sink",), np.tile(np.asarray(inp["sw_sinks"][0], np.float32)[None, :], (128, 1)))
    build_host_rwkv(inp, H, vp)
    build_host_mamba(inp, H, vp)


def build_host_rwkv(inp, H, vp):
    NA = 2
    def st(fn):
        return np.ascontiguousarray(np.stack([fn(j) for j in range(NA)]))
    f32 = lambda a: np.asarray(a, np.float32)
    for i, nm in enumerate(("rw_wr", "rw_wk", "rw_wv")):
        H[nm] = st(lambda j: tile_w(f32(inp["rw_w_rkv"][j, i])).reshape(8, 128, 1024))
    H["rw_wo"] = st(lambda j: tile_w(f32(inp["rw_w_o"][j])).reshape(8, 128, 1024))
    def l1(key, R):
        return st(lambda j: f32(inp[key][j]).reshape(8, 128, R).transpose(1, 0, 2).reshape(128, 8 * R))
    H["rw_w1"] = l1("rw_w1", 64)
    H["rw_a1"] = l1("rw_a1", 64)
    H["rw_g1"] = l1("rw_g1", 160)
    H["rw_w2"] = st(lambda j: f32(inp["rw_w2"][j]))
    H["rw_a2"] = st(lambda j: f32(inp["rw_a2"][j]))
    H["rw_g2a"] = st(lambda j: f32(inp["rw_g2"][j][0:128]))
    H["rw_g2b"] = st(lambda j: f32(inp["rw_g2"][j][128:160]))
    v1 = f32(inp["rw_v1"][0]).reshape(8, 128, 32).transpose(1, 0, 2).reshape(128, 256)
    H["rw_v1"] = np.ascontiguousarray(np.stack([v1, v1]))
    v2 = f32(inp["rw_v2"][0])
    H["rw_v2"] = np.ascontiguousarray(np.stack([v2, v2]))
    for j in range(NA):
        vp.add(("mu", j), fm_cols(inp["rw_mu"][j]))
        for nm in ("w0", "a0", "k_k", "k_a", "ln_w", "ln_b", "r_k"):
            vp.add((nm, j), fm_cols(inp["rw_" + nm][j]))
    vp.add(("v0", 1), fm_cols(inp["rw_v0"][0]))


def build_host_mamba(inp, H, vp):
    f32 = lambda a: np.asarray(a, np.float32)
    win = f32(inp["mb_w_in"][0])
    H["mb_win"] = tile_w(win[:, 0:6144]).reshape(48, 128, 1024)
    H["mb_wdt"] = np.ascontiguousarray(win[:, 6144:6176].reshape(8, 128, 32).transpose(1, 0, 2)).reshape(1, 128, 256)
    H["mb_wo"] = tile_w(f32(inp["mb_w_out"][0])).reshape(8, 128, 2048)
    cw = f32(inp["mb_conv_w"][0])
    for tap in range(4):
        vp.add(("cw", tap), fm_cols(cw[tap]))
    vp.add(("cbias",), fm_cols(inp["mb_conv_b"][0]))
    vp.add(("mbD",), fm_cols(np.repeat(f32(inp["mb_D"][0]), 64)))
    vp.add(("mbng",), fm_cols(inp["mb_norm_g"][0]))
    vp.add(("dtb",), np.tile(f32(inp["mb_dt_bias"][0])[None, :], (128, 1)))
    vp.add(("alog",), np.tile(f32(inp["mb_A_log"][0])[None, :], (128, 1)))


W_SPECS = {"mb_win": [48, 128, 1024], "mb_wdt": [1, 128, 256], "mb_wo": [8, 128, 2048],
           "rw_wr": [2, 8, 128, 1024], "rw_wk": [2, 8, 128, 1024], "rw_wv": [2, 8, 128, 1024],
           "rw_wo": [2, 8, 128, 1024], "rw_w1": [2, 128, 512], "rw_a1": [2, 128, 512],
           "rw_g1": [2, 128, 1280], "rw_w2": [2, 64, 1024], "rw_a2": [2, 64, 1024],
           "rw_g2a": [2, 128, 1024], "rw_g2b": [2, 32, 1024], "rw_v1": [2, 128, 256],
           "rw_v2": [2, 32, 1024],
           "sw_wq": [8, 128, 1024], "sw_wk": [4, 128, 1024], "sw_wv": [1, 128, 2048],
           "sw_wo": [8, 128, 1024]}


class Builder:
    def __init__(self, nseq, n_layers, vec_off, nvec, mixers=True, dbg=None, kinds=None,
                 const_off=None, nconst=0):
        self.kinds = kinds if kinds is not None else [i % 3 for i in range(DEPTH)]
        self.const_off = const_off
        self.nconst = nconst
        self.nseq = nseq
        self.n_layers = n_layers
        self.vec_off = vec_off
        self.nvec = nvec
        self.mixers = mixers
        self.dbg = dbg

    def build(self):
        nc = bass.Bass("TRN2", target_bir_lowering=False)
        self.nc = nc
        ns = self.nseq
        dt = nc.dram_tensor
        self.d_xT = dt("xT", [ns, KC, 128, S], F32, kind="ExternalInput").ap()
        self.d_cT = dt("cT", [128, KC * ns], F32, kind="ExternalInput").ap()
        self.d_vecs = dt("vecs", [128, self.nvec], F32, kind="ExternalInput").ap()
        self.d_adaw = dt("ada_w", [DEPTH, 18, 128, 8 * 512], F32, kind="ExternalInput").ap()
        self.d_win = dt("ffn_w_in", [DEPTH, 2, NJ, 128, 8 * 256], F32, kind="ExternalInput").ap()
        self.d_wout = dt("ffn_w_out", [DEPTH, 2, 8, 128, NJ * 128], F32, kind="ExternalInput").ap()
        self.d_yT = dt("yT", [ns, KC, 128, S], F32, kind="ExternalOutput").ap()
        self.d_consts = dt("consts", [128, self.nconst], F32, kind="ExternalInput").ap()
        self.dw = {k: dt(k, shp, F32, kind="ExternalInput").ap() for k, shp in W_SPECS.items()}
        self.scr = {}
        self.b_scr = {}
        for nm in ("R", "K", "V", "A", "G", "VF", "LD"):
            self.scr[nm] = dt("scr_" + nm, [KC, 128, S], F32 if nm == "LD" else BF16, kind="Internal").ap()
            self.b_scr[nm] = [Buf(f"scr{nm}{t}") for t in range(4)]
        for nm, n in (("MZ", 16), ("MX", 16), ("MB", 8), ("MC", 8)):
            self.scr[nm] = dt("scr_" + nm, [n, 128, S], BF16, kind="Internal").ap()
            self.b_scr[nm] = [Buf(f"scr{nm}{i}") for i in range(n)]
        with ExitStack() as st:
            self.st = st
            self.P = Prog(nc, st)
            self.alloc()
            self.emit()
            self.P.wait_all_dma("sp")
        print("instr counts", self.P.count, "waits", self.P.n_waits, "dmas", self.P.n_dma)
        return nc

    def sb(self, name, shape, dtype):
        return self.st.enter_context(self.nc.sbuf_tensor(name, shape, dtype))

    def alloc(self):
        ns = self.nseq
        self.xT = self.sb("xT_sb", [128, KC, S], F32)
        self.b_xT = [[Buf(f"xT{kc}_{t}") for t in range(4)] for kc in range(KC)]
        self.hb = self.sb("hb", [128, KC, S], BF16)
        self.b_hb = [Buf(f"hb{t}") for t in range(4)]
        self.u = self.sb("u", [128, NJ, 1024], BF16)
        self.b_u = [[Buf(f"u{j}_{t}") for t in range(2)] for j in range(NJ)]
        self.st32 = [self.sb(f"st32_{i}", [128, NJ * 128], F32) for i in range(2)]
        self.stb = [self.sb(f"stb_{i}", [128, NJ * 128], BF16) for i in range(2)]
        self.b_st32 = [Buf(f"st32_{i}") for i in range(2)]
        self.b_stb = [Buf(f"stb_{i}") for i in range(2)]
        self.wslot = 0
        self.tmpf = [self.sb(f"tmpf{i}", [128, 512], F32) for i in range(4)]
        self.b_tmpf = [Buf(f"tmpf{i}") for i in range(4)]
        self.tmpb = [self.sb(f"tmpb{i}", [128, 512], BF16) for i in range(2)]
        self.b_tmpb = [Buf(f"tmpb{i}") for i in range(2)]
        self.vecs = self.sb("vecs_sb", [128, self.nvec], F32)
        self.b_vecs = Buf("vecs")
        self.modT = self.sb("modT", [128, DEPTH, 72, ns], F32)
        self.b_mod = Buf("mod")
        self.aco = self.sb("aco", [128, DEPTH * 3 * ns * 8], F32)
        self.gco = self.sb("gco", [128, DEPTH * 3 * ns * 8], F32)
        self.cact = self.sb("cact", [128, KC * ns], F32)
        self.b_cact = Buf("cact")
        self.ones_b = self.sb("ones_b", [128, 128], BF16)
        self.b_const = Buf("const")
        self.cf = self.sb("consts_f", [128, self.nconst], F32)
        self.cb = self.sb("consts_b", [128, self.nconst], BF16)
        self.ones1 = self.sb("ones1", [128, 128], BF16)
        self.esink = self.sb("esink", [128, 16], F32)
        self.omu = self.sb("omu", [128, 96], F32)
        self.mbdt = self.sb("mbdt", [128, 512], F32)
        self.b_mbdt = Buf("mbdt")
        self.bank_rr = 0
        self.psum = self.st.enter_context(self.nc.psum_tensor("ps", [128, 8, 512], F32))
        self.b_ps = [Buf(f"ps{i}", excl=True) for i in range(8)]

    def vcol(self, key, i=0, n=1):
        o = self.vec_off[key] + i
        return self.vecs[:, o:o + n]

    def co_idx(self, l, sub, s):
        return ((l * 3 + sub) * self.nseq + s) * 8

    def emit(self):
        P = self.P
        P.dma(self.vecs[:], self.d_vecs, writes=[self.b_vecs])
        P.dma(self.cact[:], self.d_cT, writes=[self.b_cact])
        P.op("pool", lambda e: e.memset(self.ones_b[:], 1.0 / 1024.0), writes=[self.b_const])
        P.op("pool", lambda e: e.memset(self.ones1[:], 1.0), writes=[self.b_const])
        P.dma(self.cf[:], self.d_consts, writes=[self.b_const])
        P.op("dve", lambda e: e.tensor_copy(out=self.cb[:], in_=self.cf[:]),
             reads=[self.b_const], writes=[self.b_const])
        for jj in range(2):
            P.op("dve", lambda e, jj=jj: e.tensor_scalar(out=self.omu[:, jj * 48:(jj + 1) * 48],
                                                         in0=self.vcol(("mu", jj), 0, 48), scalar1=-1.0, scalar2=1.0,
                                                         op0=ALU.mult, op1=ALU.add),
                 reads=[self.b_vecs], writes=[self.b_const])
        if 2 in self.kinds[:self.n_layers]:
            P.op("act", lambda e: e.activation(out=self.esink[:], in_=self.vcol(("sink",), 0, 16),
                                               func=AF.Exp),
                 reads=[self.b_vecs], writes=[self.b_const])
        P.op("act", lambda e: e.activation(out=self.cact[:], in_=self.cact[:], func=AF.Silu),
             reads=[self.b_cact], writes=[self.b_cact])
        self.emit_mod()
        for s in range(self.nseq):
            self.emit_seq(s)

    def emit_mod(self):
        P = self.P
        ns = self.nseq
        for l in range(self.n_layers):
            for blk in range(18):
                for half in range(2):
                    slot = self.wslot
                    self.wslot ^= 1
                    sl = self.st32[slot]
                    bsl = self.b_st32[slot]
                    P.dma(sl[:, 0:2048], self.d_adaw[l, blk, :, half * 2048:(half + 1) * 2048],
                          writes=[bsl])
                    for fcl in range(4):
                        for k4 in range(4):
                            kcg = half * 4 + k4
                            P.op("pe", lambda e, sl=sl, k4=k4, fcl=fcl, half=half, kcg=kcg: e.matmul(
                                self.psum[:, half, fcl * ns:(fcl + 1) * ns],
                                lhsT=sl[:, k4 * 512 + fcl * 128:k4 * 512 + (fcl + 1) * 128],
                                rhs=self.cact[:, kcg * ns:(kcg + 1) * ns],
                                start=(k4 == 0), stop=(k4 == 3)),
                                reads=[bsl, self.b_cact], writes=[self.b_ps[half]])
                P.op("act", lambda e: e.activation(out=self.tmpf[0][:, 0:4 * ns],
                                                   in_=self.psum[:, 1, 0:4 * ns], func=AF.Copy),
                     reads=[self.b_ps[1]], writes=[self.b_tmpf[0]])
                for fcl in range(4):
                    fch = blk * 4 + fcl
                    P.op("dve", lambda e, fcl=fcl, fch=fch, l=l: e.scalar_tensor_tensor(
                        out=self.modT[:, l, fch, :], in0=self.psum[:, 0, fcl * ns:(fcl + 1) * ns],
                        scalar=self.vcol(("adab", l), fch), op0=ALU.add,
                        in1=self.tmpf[0][:, fcl * ns:(fcl + 1) * ns], op1=ALU.add),
                        reads=[self.b_ps[0], self.b_vecs, self.b_tmpf[0]], writes=[self.b_mod])
            for sub in range(3):
                for s in range(ns):
                    i0 = self.co_idx(l, sub, s)
                    P.op("dve", lambda e, l=l, sub=sub, s=s, i0=i0: e.scalar_tensor_tensor(
                        out=self.aco[:, i0:i0 + 8], in0=self.modT[:, l, sub * 24 + 8:sub * 24 + 16, s],
                        scalar=1.0, op0=ALU.add, in1=self.vcol(("ng", l), sub * 8, 8), op1=ALU.mult),
                        reads=[self.b_mod, self.b_vecs], writes=[self.b_mod])
                    gsc = 1.0 if sub == 1 else 0.5
                    P.op("dve", lambda e, l=l, sub=sub, s=s, i0=i0, gsc=gsc: e.tensor_scalar(
                        out=self.gco[:, i0:i0 + 8], in0=self.modT[:, l, sub * 24 + 16:sub * 24 + 24, s],
                        scalar1=gsc, scalar2=None, op0=ALU.mult),
                        reads=[self.b_mod], writes=[self.b_mod])

    def emit_seq(self, s):
        P = self.P
        for kc in range(KC):
            P.dma(self.xT[:, kc, :], self.d_xT[s, kc], writes=self.b_xT[kc])
        for l in range(self.n_layers):
            self.emit_adaln(l, 0, s)
            self.emit_ffn(l, 0, s)
            if self.mixers:
                self.emit_adaln(l, 1, s)
                self.emit_mixer(l, s)
            self.emit_adaln(l, 2, s)
            self.emit_ffn(l, 1, s)
        for kc in range(KC):
            P.dma(self.d_yT[s, kc], self.xT[:, kc, :], reads=self.b_xT[kc])

    def emit_mixer(self, l, s):
        kind = self.kinds[l]
        if kind == 2:
            self.emit_swa(l, s)
        elif kind == 1:
            self.emit_mamba(l, s)
        else:
            self.emit_rwkv(l, s)

    def cst(self, key, n=128, bf=True):
        o = self.const_off[key]
        return (self.cb if bf else self.cf)[:, o:o + n]

    def proj(self, wd, nf, kch, rhs_fn, rhs_bufs_fn, tiles, consume, banks=(0, 1, 2, 3)):
        P = self.P
        for fc in range(nf):
            w, bw = self.load_w(wd[fc], kch * 128)
            for ti, (tok, ntok) in enumerate(tiles):
                bank = banks[self.bank_rr % len(banks)]
                self.bank_rr += 1
                for kc in range(kch):
                    P.op("pe", lambda e, kc=kc, bank=bank, w=w, tok=tok, ntok=ntok: e.matmul(
                        self.psum[:, bank, 0:ntok], lhsT=w[:, kc * 128:(kc + 1) * 128],
                        rhs=rhs_fn(kc, tok), start=(kc == 0), stop=(kc == kch - 1)),
                        reads=[bw] + list(rhs_bufs_fn(ti)), writes=[self.b_ps[bank]])
                consume(fc, ti, tok, bank)

    def out_proj(self, wd, l, s, kch=8):
        P = self.P
        i0 = self.co_idx(l, 1, s)
        tiles = [(slice(t * 512, (t + 1) * 512), 512) for t in range(4)]

        def consume(fc, ti, tok, bank):
            P.op("dve", lambda e: e.scalar_tensor_tensor(
                out=self.xT[:, fc, tok], in0=self.psum[:, bank, :],
                scalar=self.gco[:, i0 + fc:i0 + fc + 1], op0=ALU.mult,
                in1=self.xT[:, fc, tok], op1=ALU.add),
                reads=[self.b_ps[bank], self.b_mod, self.b_xT[fc][ti]], writes=[self.b_xT[fc][ti]])
        self.proj(wd, 8, kch, lambda kc, tok: self.hb[:, kc, tok], lambda ti: [self.b_hb[ti]],
                  tiles, consume)

    def emit_swa(self, l, s):
        P = self.P
        uflat = self.u[:].rearrange("p a b -> p (a b)")
        kn = uflat[:, 0:8192].rearrange("p (g t) -> p g t", g=4)
        vt = uflat[:, 8192:12288].rearrange("p (b n) -> p b n", b=16)
        qn = uflat[:, 12288:16384].rearrange("p (c t) -> p c t", c=8)
        pT = [uflat[:, 16384 + i * 256:16384 + (i + 1) * 256] for i in range(2)]
        b_kn = [Buf(f"kn{t}") for t in range(4)]
        b_vt = [Buf(f"vt{t}") for t in range(16)]
        b_qn = Buf("qn")
        b_pT = [Buf("pT0"), Buf("pT1")]
        b_rd = Buf("rd")
        allu = [b for row in self.b_u for b in row]
        tiles = [(slice(t * 512, (t + 1) * 512), 512) for t in range(4)]

        def norm_consume(dst_fn, dst_buf_fn, gkey):
            def consume(fc, ti, tok, bank):
                sq = self.tmpb[fc % 2]
                bsq = self.b_tmpb[fc % 2]
                P.op("act", lambda e: e.activation(out=sq[:], in_=self.psum[:, bank, :], func=AF.Square),
                     reads=[self.b_ps[bank]], writes=[bsq])
                P.op("pe", lambda e: e.matmul(self.psum[:, 6, :], lhsT=self.cst("blk64"), rhs=sq[:],
                                              start=True, stop=True),
                     reads=[bsq, self.b_const], writes=[self.b_ps[6]])
                rs = self.tmpf[2]
                brs = self.b_tmpf[2]
                P.op("act", lambda e: e.activation(out=rs[:], in_=self.psum[:, 6, :], func=AF.Sqrt,
                                                   bias=EPS, scale=1.0),
                     reads=[self.b_ps[6]], writes=[brs])
                P.op("dve", lambda e: e.reciprocal(out=rs[:], in_=rs[:]), reads=[brs], writes=[brs])
                P.op("dve", lambda e: e.scalar_tensor_tensor(
                    out=dst_fn(fc, ti, tok), in0=self.psum[:, bank, :], scalar=self.vcol(gkey),
                    op0=ALU.mult, in1=rs[:], op1=ALU.mult),
                    reads=[self.b_ps[bank], brs, self.b_vecs], writes=[dst_buf_fn(fc, ti)] + (allu if (fc == 0 and ti == 0) else []))
            return consume

        self.proj(self.dw["sw_wk"], 4, 8, lambda kc, tok: self.hb[:, kc, tok], lambda ti: [self.b_hb[ti]],
                  tiles, norm_consume(lambda fc, ti, tok: kn[:, fc, tok], lambda fc, ti: b_kn[ti], ("swk",)))
        wv, bwv = self.load_w(self.dw["sw_wv"][0], 2048)
        for blk in range(16):
            bank = (blk % 2)
            for kc in range(KC):
                P.op("pe", lambda e, kc=kc, blk=blk, bank=bank: e.matmul(
                    self.psum[:, bank, 0:256], lhsT=self.hb[:, kc, blk * 128:(blk + 1) * 128],
                    rhs=wv[:, kc * 256:(kc + 1) * 256], start=(kc == 0), stop=(kc == KC - 1)),
                    reads=[bwv, self.b_hb[blk // 4]], writes=[self.b_ps[bank]])
            P.op("act", lambda e, blk=blk, bank=bank: e.activation(
                out=vt[:, blk, :], in_=self.psum[:, bank, 0:256], func=AF.Copy),
                reads=[self.b_ps[bank]], writes=[b_vt[blk]])
        for t in range(4):
            self.proj(self.dw["sw_wq"], 8, 8, lambda kc, tok: self.hb[:, kc, tok],
                      lambda ti, t=t: [self.b_hb[t]], [tiles[t]],
                      norm_consume(lambda fc, ti, tok: qn[:, fc, :], lambda fc, ti: b_qn, ("swq",)))
            for bl in range(4):
                b = t * 4 + bl
                qtok = slice(bl * 128, (bl + 1) * 128)
                for h in range(16):
                    kc, pb, g = h // 2, (h % 2) * 64, h // 4
                    sbank = 4 + (h % 2)
                    nk = 256 if b > 0 else 128
                    P.op("pe", lambda e, kc=kc, pb=pb, g=g, sbank=sbank, b=b: e.matmul(
                        self.psum[:, sbank, 0:128], lhsT=kn[pb:pb + 64, g, b * 128:(b + 1) * 128],
                        rhs=qn[pb:pb + 64, kc, qtok], start=True, stop=True),
                        reads=[b_kn[b // 4], b_qn], writes=[self.b_ps[sbank]])
                    if b > 0:
                        P.op("pe", lambda e, kc=kc, pb=pb, g=g, sbank=sbank, b=b: e.matmul(
                            self.psum[:, sbank, 128:256], lhsT=kn[pb:pb + 64, g, (b - 1) * 128:b * 128],
                            rhs=qn[pb:pb + 64, kc, qtok], start=True, stop=True),
                            reads=[b_kn[(b - 1) // 4], b_qn], writes=[self.b_ps[sbank]])
                    pt = pT[h % 2]
                    bpt = b_pT[h % 2]
                    P.op("act", lambda e, sbank=sbank, pt=pt, nk=nk: e.activation(
                        out=pt[:, 0:nk], in_=self.psum[:, sbank, 0:nk], func=AF.Exp, scale=0.125),
                        reads=[self.b_ps[sbank]], writes=[bpt])
                    P.op("pool", lambda e, pt=pt: e.tensor_tensor(
                        out=pt[:, 0:128], in0=pt[:, 0:128], in1=self.cst("mcur"), op=ALU.mult),
                        reads=[bpt, self.b_const], writes=[bpt])
                    if b > 0:
                        P.op("pool", lambda e, pt=pt: e.tensor_tensor(
                            out=pt[:, 128:256], in0=pt[:, 128:256], in1=self.cst("mprev"), op=ALU.mult),
                            reads=[bpt, self.b_const], writes=[bpt])
                    P.op("pe", lambda e, pt=pt, b=b: e.matmul(
                        self.psum[:, 6, 0:128], lhsT=self.ones1[:], rhs=pt[:, 0:128],
                        start=True, stop=(b == 0)), reads=[bpt, self.b_const], writes=[self.b_ps[6]])
                    if b > 0:
                        P.op("pe", lambda e, pt=pt: e.matmul(
                            self.psum[:, 6, 0:128], lhsT=self.ones1[:], rhs=pt[:, 128:256],
                            start=False, stop=True), reads=[bpt, self.b_const], writes=[self.b_ps[6]])
                    P.op("pe", lambda e, pt=pt, b=b, g=g, pb=pb: e.matmul(
                        self.psum[pb:pb + 64, 7, 0:128], lhsT=vt[:, b, g * 64:(g + 1) * 64], rhs=pt[:, 0:128],
                        start=True, stop=(b == 0)), reads=[bpt, b_vt[b]], writes=[self.b_ps[7]])
                    if b > 0:
                        P.op("pe", lambda e, pt=pt, b=b, g=g, pb=pb: e.matmul(
                            self.psum[pb:pb + 64, 7, 0:128], lhsT=vt[:, b - 1, g * 64:(g + 1) * 64],
                            rhs=pt[:, 128:256], start=False, stop=True),
                            reads=[bpt, b_vt[b - 1]], writes=[self.b_ps[7]])
                    rd = self.tmpf[3]
                    brd = self.b_tmpf[3]
                    P.op("dve", lambda e, h=h: e.tensor_scalar(
                        out=rd[:, 0:128], in0=self.psum[:, 6, 0:128], scalar1=self.esink[:, h:h + 1],
                        scalar2=None, op0=ALU.add), reads=[self.b_ps[6], self.b_const], writes=[brd])
                    P.op("dve", lambda e: e.reciprocal(out=rd[:, 0:128], in_=rd[:, 0:128]),
                         reads=[brd], writes=[brd])
                    P.op("dve", lambda e, pb=pb, kc=kc, b=b: e.tensor_tensor(
                        out=self.hb[pb:pb + 64, kc, b * 128:(b + 1) * 128], in0=self.psum[pb:pb + 64, 7, 0:128],
                        in1=rd[pb:pb + 64, 0:128], op=ALU.mult),
                        reads=[self.b_ps[7], brd], writes=[self.b_hb[t]])
        self.out_proj(self.dw["sw_wo"], l, s)
        for row in self.b_u:
            for bb in row:
                for x in b_kn + b_vt + [b_qn] + b_pT:
                    if x.last_w is not None:
                        bb.readers.append(x.last_w)
                    bb.readers.extend(x.readers)

    def emit_adaln(self, l, sub, s):
        P = self.P
        i0 = self.co_idx(l, sub, s)
        for t in range(4):
            tok = slice(t * 512, (t + 1) * 512)
            pb = 6 + (t % 2)
            for kc in range(KC):
                sq = self.tmpb[kc % 2]
                bsq = self.b_tmpb[kc % 2]
                P.op("act", lambda e, sq=sq, kc=kc: e.activation(
                    out=sq[:], in_=self.xT[:, kc, tok], func=AF.Square),
                    reads=[self.b_xT[kc][t]], writes=[bsq])
                P.op("pe", lambda e, sq=sq, kc=kc, pb=pb: e.matmul(
                    self.psum[:, pb, :], lhsT=self.ones_b[:], rhs=sq[:],
                    start=(kc == 0), stop=(kc == KC - 1)),
                    reads=[bsq, self.b_const], writes=[self.b_ps[pb]])
            rs = self.tmpf[2]
            brs = self.b_tmpf[2]
            P.op("act", lambda e, pb=pb: e.activation(out=rs[:], in_=self.psum[:, pb, :],
                                                      func=AF.Sqrt, bias=EPS, scale=1.0),
                 reads=[self.b_ps[pb]], writes=[brs])
            P.op("dve", lambda e: e.reciprocal(out=rs[:], in_=rs[:]), reads=[brs], writes=[brs])
            for kc in range(KC):
                tf = self.tmpf[kc % 2]
                btf = self.b_tmpf[kc % 2]
                P.op("dve", lambda e, tf=tf, kc=kc: e.scalar_tensor_tensor(
                    out=tf[:], in0=self.xT[:, kc, tok], scalar=self.aco[:, i0 + kc:i0 + kc + 1],
                    op0=ALU.mult, in1=rs[:], op1=ALU.mult),
                    reads=[self.b_xT[kc][t], brs, self.b_mod], writes=[btf])
                P.op("act", lambda e, tf=tf, kc=kc: e.activation(
                    out=self.hb[:, kc, tok], in_=tf[:], func=AF.Identity,
                    bias=self.modT[:, l, sub * 24 + kc, s:s + 1], scale=1.0),
                    reads=[btf, self.b_mod], writes=[self.b_hb[t]])

    def load_w(self, dram_ap, ncols, npart=128):
        P = self.P
        slot = self.wslot
        self.wslot ^= 1
        P.dma(self.st32[slot][0:npart, 0:ncols], dram_ap, writes=[self.b_st32[slot]])
        P.op("pool", lambda e: e.tensor_copy(out=self.stb[slot][0:npart, 0:ncols],
                                             in_=self.st32[slot][0:npart, 0:ncols]),
             reads=[self.b_st32[slot]], writes=[self.b_stb[slot]])
        return self.stb[slot], self.b_stb[slot]

    @staticmethod
    def handover(olds, news):
        toks = []
        for b in olds:
            if b.last_w is not None:
                toks.append(b.last_w)
            toks.extend(b.readers)
        toks = list(dict.fromkeys(toks))
        for nb in news:
            nb.readers.extend(toks)

    def emit_ffn(self, l, f, s):
        P = self.P
        i0 = self.co_idx(l, 0 if f == 0 else 2, s)
        for half in range(2):
            for j in range(NJ):
                w, bw = self.load_w(self.d_win[l, f, j], 8 * 256)
                for tt in range(2):
                    t = half * 2 + tt
                    tok = slice(t * 512, (t + 1) * 512)
                    pa = 2 * (tt % 2)
                    pbk = pa + 1
                    for which, bank in ((0, pa), (1, pbk)):
                        for kc in range(KC):
                            P.op("pe", lambda e, kc=kc, which=which, bank=bank, w=w: e.matmul(
                                self.psum[:, bank, :],
                                lhsT=w[:, kc * 256 + which * 128:kc * 256 + which * 128 + 128],
                                rhs=self.hb[:, kc, tok], start=(kc == 0), stop=(kc == KC - 1)),
                                reads=[bw, self.b_hb[t]], writes=[self.b_ps[bank]])
                    sa = self.tmpf[2 + tt]
                    bsa = self.b_tmpf[2 + tt]
                    P.op("act", lambda e, sa=sa, pa=pa: e.activation(
                        out=sa[:], in_=self.psum[:, pa, :], func=AF.Silu),
                        reads=[self.b_ps[pa]], writes=[bsa])
                    P.op("dve", lambda e, sa=sa, pbk=pbk, j=j, tt=tt: e.tensor_tensor(
                        out=self.u[:, j, tt * 512:(tt + 1) * 512], in0=sa[:],
                        in1=self.psum[:, pbk, :], op=ALU.mult),
                        reads=[bsa, self.b_ps[pbk]], writes=[self.b_u[j][tt]])
            for c in range(KC):
                w, bw = self.load_w(self.d_wout[l, f, c], NJ * 128)
                for tt in range(2):
                    t = half * 2 + tt
                    tok = slice(t * 512, (t + 1) * 512)
                    bank = 4 + (tt % 2)
                    for j in range(NJ):
                        P.op("pe", lambda e, j=j, bank=bank, w=w, tt=tt: e.matmul(
                            self.psum[:, bank, :], lhsT=w[:, j * 128:(j + 1) * 128],
                            rhs=self.u[:, j, tt * 512:(tt + 1) * 512],
                            start=(j == 0), stop=(j == NJ - 1)),
                            reads=[bw, self.b_u[j][tt]], writes=[self.b_ps[bank]])
                    P.op("dve", lambda e, c=c, bank=bank: e.scalar_tensor_tensor(
                        out=self.xT[:, c, tok], in0=self.psum[:, bank, :],
                        scalar=self.gco[:, i0 + c:i0 + c + 1], op0=ALU.mult,
                        in1=self.xT[:, c, tok], op1=ALU.add),
                        reads=[self.b_ps[bank], self.b_mod, self.b_xT[c][t]],
                        writes=[self.b_xT[c][t]])


    def emit_rwkv(self, l, s):
        j = sum(1 for q in self.kinds[:l] if q == 0)
        import os
        stop = int(os.environ.get("RW_STOP", "9"))
        self.rwkv_phase1(l, s, j)
        if stop >= 2:
            self.rwkv_phase2(l, s, j)
        if stop >= 3:
            self.out_proj(self.dw["rw_wo"][j], l, s)

    def rwkv_phase1(self, l, s, j):
        P = self.P
        ps, bps = self.psum, self.b_ps
        vres = j > 0
        uflat = self.u[:].rearrange("p a b -> p (a b)")
        t1 = uflat[:, 0:4096].rearrange("p (a t) -> p a t", a=2)
        stg = [uflat[:, 4096 + i * 512:4608 + i * 512] for i in range(4)]
        stgf = [uflat[:, 6144 + i * 1024:7168 + i * 1024].bitcast(F32) for i in range(2)]
        v2b = uflat[:, 8192:9216]
        vgt = uflat[:, 9216:10240].bitcast(F32)
        b_t1, b_v2b, b_vgt = Buf("t1"), Buf("v2b"), Buf("vgt")
        b_stg = [Buf(f"stg{i}") for i in range(4)]
        b_stgf = [Buf(f"stgf{i}") for i in range(2)]
        news = [b_t1, b_v2b, b_vgt] + b_stg + b_stgf
        allu = [b for row in self.b_u for b in row]
        self.handover(allu, news)
        st = {"stg": 0, "stgf": 0}
        pbanks = (0, 1, 2, 3)
        tiles = [(slice(t * 512, (t + 1) * 512), 512) for t in range(4)]

        def nstg():
            i = st["stg"]
            st["stg"] = (i + 1) % 4
            return i

        def load_mix(dram_ap, C, which):
            slot = self.wslot
            self.wslot ^= 1
            n = 8 * C
            P.dma(self.st32[slot][:, 0:n], dram_ap, writes=[self.b_st32[slot]])
            src3 = self.st32[slot][:, 0:n].rearrange("p (k c) -> p k c", k=8)
            for half, vec in ((0, self.omu[:, j * 48 + which * 8:j * 48 + which * 8 + 8]),
                              (1, self.vcol(("mu", j), which * 8, 8))):
                P.op("pool", lambda e, half=half, vec=vec: e.tensor_tensor(
                    out=self.stb[slot][:, half * n:(half + 1) * n].rearrange("p (k c) -> p k c", k=8),
                    in0=src3, in1=vec.unsqueeze(2).broadcast_to([128, 8, C]), op=ALU.mult),
                    reads=[self.b_st32[slot], self.b_vecs, self.b_const], writes=[self.b_stb[slot]])
            return self.stb[slot], self.b_stb[slot]

        def mixproj(w, bw, C, col0, ncol, ti, out_ap_fn):
            tok, _ = tiles[ti]
            t0 = ti * 512
            bank = pbanks[self.bank_rr % 4]
            self.bank_rr += 1
            rd = [bw, self.b_hb[ti]] + ([self.b_hb[ti - 1]] if ti else [])
            for kc in range(KC):
                P.op("pe", lambda e, kc=kc: e.matmul(
                    ps[0:ncol, bank, :], lhsT=w[:, kc * C + col0:kc * C + col0 + ncol], rhs=self.hb[:, kc, tok],
                    start=(kc == 0), stop=False), reads=rd, writes=[bps[bank]])
            c0 = 1 if ti == 0 else 0
            for kc in range(KC):
                P.op("pe", lambda e, kc=kc: e.matmul(
                    ps[0:ncol, bank, c0:512], lhsT=w[:, 8 * C + kc * C + col0:8 * C + kc * C + col0 + ncol],
                    rhs=self.hb[:, kc, t0 - 1 + c0:t0 + 511], start=False, stop=(kc == KC - 1)),
                    reads=rd, writes=[bps[bank]])
            return bank

        def bigproj(wkey, which, consume):
            for fc in range(8):
                w, bw = load_mix(self.dw[wkey][j][fc], 128, which)
                for ti in range(4):
                    bank = mixproj(w, bw, 128, 0, 128, ti, None)
                    consume(fc, ti, tiles[ti][0], bank)

        def to_scratch(names, func=AF.Copy, bias_key=None):
            def consume(fc, ti, tok, bank):
                i = nstg()
                kw = {}
                if bias_key is not None:
                    kw = dict(bias=self.vcol(bias_key, fc), scale=1.0)
                P.op("act", lambda e: e.activation(out=stg[i][:], in_=ps[:, bank, :], func=func, **kw),
                     reads=[bps[bank], self.b_vecs], writes=[b_stg[i]])
                for nm in names:
                    P.dma(self.scr[nm][fc, :, tok], stg[i][:], reads=[b_stg[i]], writes=[self.b_scr[nm][ti]], e="act")
            return consume

        def lora1(wkey, R, which, func, dst):
            w, bw = load_mix(self.dw[wkey][j], R, which)
            for (c0, cn, slot) in dst:
                for ti in range(4):
                    bank = mixproj(w, bw, R, c0, cn, ti, None)
                    P.op("act", lambda e, cn=cn, slot=slot, ti=ti, bank=bank: e.activation(
                        out=t1[0:cn, slot, tiles[ti][0]], in_=ps[0:cn, bank, :], func=func),
                        reads=[bps[bank]], writes=[b_t1])

        def lora2(parts, consume):
            ws = [self.load_w(self.dw[wk][j], 1024, npart=npart) + (npart, slot) for (wk, npart, slot) in parts]
            for fc in range(8):
                for ti in range(4):
                    tok = tiles[ti][0]
                    bank = pbanks[self.bank_rr % 4]
                    self.bank_rr += 1
                    for pi, (w2, bw2, npart, slot) in enumerate(ws):
                        P.op("pe", lambda e, w2=w2, npart=npart, slot=slot, pi=pi: e.matmul(
                            ps[:, bank, :], lhsT=w2[0:npart, fc * 128:(fc + 1) * 128],
                            rhs=t1[0:npart, slot, tok], start=(pi == 0), stop=(pi == len(ws) - 1)),
                            reads=[bw2, b_t1], writes=[bps[bank]])
                    consume(fc, ti, tok, bank)

        bigproj("rw_wr", 0, to_scratch(["R"]))
        bigproj("rw_wk", 2, to_scratch(["K"]))
        if vres:
            lora1("rw_v1", 32, 3, AF.Copy, [(0, 32, 0)])
            wv2, bwv2 = self.load_w(self.dw["rw_v2"][j], 1024, npart=32)
            P.op("pool", lambda e: e.tensor_copy(out=v2b[0:32, :], in_=wv2[0:32, 0:1024]), reads=[bwv2], writes=[b_v2b])

            def v_consume(fc, ti, tok, bank):
                ia, ib = nstg(), nstg()
                fi = st["stgf"]
                st["stgf"] ^= 1
                P.dma(stg[ia][:], self.scr["VF"][fc, :, tok], reads=[self.b_scr["VF"][ti]], writes=[b_stg[ia]])
                P.op("pe", lambda e: e.matmul(ps[:, 4, :], lhsT=v2b[0:32, fc * 128:(fc + 1) * 128], rhs=t1[0:32, 0, tok],
                                              start=True, stop=True), reads=[b_v2b, b_t1], writes=[bps[4]])
                P.op("act", lambda e: e.activation(out=vgt, in_=ps[:, 4, :], func=AF.Sigmoid,
                                                   bias=self.vcol(("v0", 1), fc), scale=1.0),
                     reads=[bps[4], self.b_vecs], writes=[b_vgt])
                P.op("dve", lambda e: e.tensor_tensor(out=stgf[fi][:], in0=stg[ia][:], in1=ps[:, bank, :], op=ALU.subtract),
                     reads=[b_stg[ia], bps[bank]], writes=[b_stgf[fi]])
                P.op("dve", lambda e: e.tensor_tensor(out=stgf[fi][:], in0=stgf[fi][:], in1=vgt, op=ALU.mult),
                     reads=[b_vgt], writes=[b_stgf[fi]])
                P.op("dve", lambda e: e.tensor_tensor(out=stg[ib][:], in0=stgf[fi][:], in1=ps[:, bank, :], op=ALU.add),
                     reads=[b_stgf[fi], bps[bank]], writes=[b_stg[ib]])
                P.dma(self.scr["V"][fc, :, tok], stg[ib][:], reads=[b_stg[ib]], writes=[self.b_scr["V"][ti]], e="act")
            bigproj("rw_wv", 3, v_consume)
        else:
            bigproj("rw_wv", 3, to_scratch(["V", "VF"]))
        lora1("rw_w1", 64, 1, AF.Tanh, [(0, 64, 0)])

        def ld_consume(fc, ti, tok, bank):
            fi = st["stgf"]
            st["stgf"] ^= 1
            P.op("act", lambda e: e.activation(out=stgf[fi][:], in_=ps[:, bank, :], func=AF.Sigmoid,
                                               bias=self.vcol(("w0", j), fc), scale=1.0),
                 reads=[bps[bank], self.b_vecs], writes=[b_stgf[fi]])
            P.op("pool", lambda e: e.tensor_scalar(out=stgf[fi][:], in0=stgf[fi][:], scalar1=-math.exp(-0.5),
                                                   scalar2=None, op0=ALU.mult), reads=[], writes=[b_stgf[fi]])
            P.dma(self.scr["LD"][fc, :, tok], stgf[fi][:], reads=[b_stgf[fi]], writes=[self.b_scr["LD"][ti]], e="act")
        lora2([("rw_w2", 64, 0)], ld_consume)
        lora1("rw_a1", 64, 4, AF.Copy, [(0, 64, 1)])
        lora2([("rw_a2", 64, 1)], to_scratch(["A"], AF.Sigmoid, ("a0", j)))
        lora1("rw_g1", 160, 5, AF.Sigmoid, [(0, 128, 0), (128, 32, 1)])
        lora2([("rw_g2a", 128, 0), ("rw_g2b", 32, 1)], to_scratch(["G"]))
        self.handover(news, allu)

    def rwkv_phase2(self, l, s, j):
        P = self.P
        uflat = self.u[:].rearrange("p a b -> p (a b)")
        v3 = lambda a: a.rearrange("p (c t) -> p c t", c=8)
        Rl, Kl, Al, Vl = [v3(uflat[:, i * 1024:(i + 1) * 1024]) for i in range(4)]
        LDl = v3(uflat[:, 4096:6144].bitcast(F32))
        AR = uflat[:, 6144:8192].rearrange("p (k c n) -> p k c n", k=8, c=2)
        BK = uflat[:, 8192:10240].rearrange("p (k c n) -> p k c n", k=8, c=2)
        bonus = v3(uflat[:, 10240:11264])
        BKT = uflat[:, 11264:13312].rearrange("p (c h n) -> p c h n", c=2, h=16)
        ZT = uflat[:, 13312:15360].rearrange("p (c h n) -> p c h n", c=2, h=16)
        AMs = uflat[:, 15360:17408].rearrange("p (h n) -> p h n", h=16)
        ATs = uflat[:, 17408:18432].rearrange("p (h n) -> p h n", h=16)
        Tm = uflat[:, 18432:19456].rearrange("p (h n) -> p h n", h=16)
        Gl = v3(uflat[:, 19456:20480])
        PLt = uflat[:, 20480:20512].bitcast(F32).rearrange("p (k c) -> p k c", k=8)
        WTs = uflat[:, 20544:21568].rearrange("p (h n) -> p h n", h=16)
        s0 = self.st32[0]
        SS = s0[:, 0:512].rearrange("p (k n) -> p k n", k=8)
        Yf = s0[:, 512:1536].rearrange("p (k n) -> p k n", k=8)
        SSb = s0[:, 1536:1792].bitcast(BF16).rearrange("p (k n) -> p k n", k=8)
        NSETS = 1
        s1f = self.st32[1]
        sb1f = self.stb[1][:, :].bitcast(F32)
        sb0f = self.stb[0][:, 768:2816].bitcast(F32)
        pool_f = [s0[:, 1792 + i * 128:1920 + i * 128] for i in range(8)] + \
                 [s1f[:, 2048 + i * 128:2176 + i * 128] for i in range(6)] + \
                 [sb1f[:, i * 128:(i + 1) * 128] for i in range(11)] + \
                 [sb0f[:, i * 128:(i + 1) * 128] for i in range(8)]
        Tsets = [pool_f[i * 8:(i + 1) * 8] for i in range(NSETS)]
        Tbsets = [[self.stb[0][:, (2 * i + q) * 128:(2 * i + q + 1) * 128] for q in range(2)] for i in range(NSETS)]
        s1 = self.st32[1][:, :].bitcast(BF16)
        Apow = [[s1[:, (2 * a + b) * 1024:(2 * a + b + 1) * 1024].rearrange("p (h n) -> p h n", h=16)
                 for b in range(2)] for a in range(2)]
        G1 = v3(self.stb[1][:, 0:2048].bitcast(F32))
        G2 = v3(self.stb[0][:, 768:2816].bitcast(F32))
        Yb = v3(self.st32[1][:, 2048:2560].bitcast(BF16))
        names = ["in", "ld", "AR", "BK", "bonus", "PL", "BKT", "ZTu", "ZTv", "AMs", "ATs", "Tm", "WTs",
                 "SS", "SSb", "Yf", "Gl", "G1", "G2", "Yb", "vin"] + [f"Tb{i}_{q}" for i in range(3) for q in range(2)] + \
                [f"T{i}_{q}" for i in range(3) for q in range(8)] + ["Ap00", "Ap01", "Ap10", "Ap11"]
        B = {n: Buf(n) for n in names}
        allold = [b for row in self.b_u for b in row] + self.b_st32 + self.b_stb
        self.handover(allold, list(B.values()))
        bAp = [[B["Ap00"], B["Ap01"]], [B["Ap10"], B["Ap11"]]]
        ps = self.psum
        bps = self.b_ps
        identb = self.cst("ident")
        psbf = lambda bank: ps[:, bank, :].bitcast(BF16)
        PAv = ps[:, 0:4, :].rearrange("p b (h n) -> p (b h) n", n=128)
        v64 = lambda b0: ps[0:64, b0:b0 + 2, :].rearrange("p b (h n) -> p (b h) n", n=64)
        mAb = self.cst("mA").unsqueeze(1).broadcast_to([128, 16, 128])
        mTb = self.cst("mT", 64)[0:64, :].unsqueeze(1).broadcast_to([64, 16, 64])
        idb = identb[0:64, 0:64].unsqueeze(1).broadcast_to([64, 16, 64])
        vc = lambda key, kc: self.vcol((key, j), kc)

        P.op("pool", lambda e: e.memset(SS[:], 0.0), writes=[B["SS"]])
        P.op("pool", lambda e: e.memset(SSb[:], 0.0), writes=[B["SSb"]])

        for ti in range(16):
            t0 = ti * 128
            tk = slice(t0, t0 + 128)
            tq = ti // 4
            for nm, dst in (("R", Rl), ("K", Kl), ("A", Al), ("V", Vl)):
                P.dma(dst[:, :, :], self.scr[nm][:, :, tk].rearrange("k p t -> p k t"),
                      reads=[self.b_scr[nm][tq]], writes=[B["vin"] if nm == "V" else B["in"]])
            P.dma(LDl[:, :, :], self.scr["LD"][:, :, tk].rearrange("k p t -> p k t"),
                  reads=[self.b_scr["LD"][tq]], writes=[B["ld"]])
            for kc in range(KC):
                K_, R_, A_, V_, LD_ = Kl[:, kc, :], Rl[:, kc, :], Al[:, kc, :], Vl[:, kc, :], LDl[:, kc, :]
                c2 = lambda a: a.rearrange("p (c n) -> p c n", c=2)
                si = (ti * KC + kc) % NSETS
                T = Tsets[si]
                Tb = Tbsets[si]
                bT = [B[f"T{si}_{q}"] for q in range(8)]
                bTb = [B[f"Tb{si}_{q}"] for q in range(2)]
                P.op("dve", lambda e: e.tensor_scalar(out=T[0], in0=K_, scalar1=vc("k_k", kc), scalar2=None,
                                                      op0=ALU.mult), reads=[B["in"], self.b_vecs], writes=[bT[0]])
                P.op("act", lambda e: e.activation(out=Tb[0], in_=T[0], func=AF.Square),
                     reads=[bT[0]], writes=[bTb[0]])
                P.op("pe", lambda e: e.matmul(ps[:, 6, 0:128], lhsT=self.cst("blk1"), rhs=Tb[0], start=True, stop=True),
                     reads=[bTb[0], self.b_const], writes=[bps[6]])
                P.op("act", lambda e: e.activation(out=T[1], in_=ps[:, 6, 0:128], func=AF.Sqrt),
                     reads=[bps[6]], writes=[bT[1]])
                P.op("dve", lambda e: e.tensor_scalar(out=T[1], in0=T[1], scalar1=1e-12, scalar2=None, op0=ALU.max),
                     reads=[], writes=[bT[1]])
                P.op("dve", lambda e: e.reciprocal(out=T[1], in_=T[1]), reads=[], writes=[bT[1]])
                P.op("dve", lambda e: e.tensor_tensor(out=T[2], in0=T[0], in1=T[1], op=ALU.mult),
                     reads=[bT[0], bT[1]], writes=[bT[2]])
                P.op("pool", lambda e: e.tensor_scalar(out=T[3], in0=A_, scalar1=-1.0, scalar2=vc("k_a", kc),
                                                       op0=ALU.add, op1=ALU.mult),
                     reads=[B["in"], self.b_vecs], writes=[bT[3]])
                P.op("dve", lambda e: e.scalar_tensor_tensor(out=T[3], in0=T[3], scalar=1.0, op0=ALU.add,
                                                             in1=K_, op1=ALU.mult),
                     reads=[bT[3], B["in"]], writes=[bT[3]])
                P.op("dve", lambda e: e.scalar_tensor_tensor(out=Tb[1], in0=R_, scalar=vc("r_k", kc), op0=ALU.mult,
                                                             in1=T[3], op1=ALU.mult),
                     reads=[bT[3], B["in"], self.b_vecs], writes=[bTb[1]])
                P.op("pe", lambda e: e.matmul(ps[:, 7, 0:128], lhsT=self.cst("blk1"), rhs=Tb[1], start=True, stop=True),
                     reads=[bTb[1], self.b_const], writes=[bps[7]])
                P.op("dve", lambda e: e.tensor_tensor(out=bonus[:, kc, :], in0=ps[:, 7, 0:128], in1=V_, op=ALU.mult),
                     reads=[bps[7], B["vin"]], writes=[B["bonus"]])
                P.op("dve", lambda e: e.tensor_tensor_scan(out=T[4], data0=self.cst("rmask", bf=False), data1=LD_,
                                                           initial=0.0, op0=ALU.mult, op1=ALU.add),
                     reads=[B["ld"], self.b_const], writes=[bT[4]])
                P.op("act", lambda e: e.activation(out=T[5], in_=T[4], func=AF.Exp), reads=[bT[4]], writes=[bT[5]])
                P.op("act", lambda e: e.activation(out=PLt[:, kc, :], in_=T[5][:, 63:128:64], func=AF.Copy),
                     reads=[bT[5]], writes=[B["PL"]])
                P.op("dve", lambda e: e.tensor_tensor(out=AR[:, kc, :, 64:128], in0=c2(R_), in1=c2(T[5]), op=ALU.mult),
                     reads=[bT[5], B["in"]], writes=[B["AR"]])
                P.op("act", lambda e: e.activation(out=T[6], in_=T[4], func=AF.Exp, scale=-1.0),
                     reads=[bT[4]], writes=[bT[6]])
                P.op("pool", lambda e: e.tensor_tensor(out=BK[:, kc, :, 64:128], in0=c2(T[3]), in1=c2(T[6]), op=ALU.mult),
                     reads=[bT[3], bT[6]], writes=[B["BK"]])
                P.op("pool", lambda e: e.tensor_tensor(out=T[7], in0=T[2], in1=A_, op=ALU.mult),
                     reads=[bT[2], B["in"]], writes=[bT[7]])
                P.op("pool", lambda e: e.tensor_tensor(out=BK[:, kc, :, 0:64], in0=c2(T[7]), in1=c2(T[6]), op=ALU.mult),
                     reads=[bT[7], bT[6]], writes=[B["BK"]])
                P.op("pool", lambda e: e.tensor_tensor(out=T[4], in0=T[4], in1=LD_, op=ALU.subtract),
                     reads=[B["ld"], bT[5], bT[6]], writes=[bT[4]])
                P.op("act", lambda e: e.activation(out=T[5], in_=T[4], func=AF.Exp),
                     reads=[bT[4], B["AR"], B["PL"]], writes=[bT[5]])
                P.op("dve", lambda e: e.scalar_tensor_tensor(out=AR[:, kc, :, 0:64], in0=c2(T[2]), scalar=-1.0,
                                                             op0=ALU.mult, in1=c2(T[5]), op1=ALU.mult),
                     reads=[bT[2], bT[5]], writes=[B["AR"]])
            import os
            if int(os.environ.get("RW_STOP", "9")) == 2 and os.environ.get("RW_PREP_ONLY"):
                continue
            for c in range(2):
                for hidx in range(16):
                    par, kc = hidx // 8, hidx % 8
                    pb = par * 64
                    P.op("pe", lambda e, par=par, kc=kc, pb=pb, c=c: e.transpose(
                        psbf(6 + par)[:, kc * 64:(kc + 1) * 64], BK[pb:pb + 64, kc, c, :],
                        identb[pb:pb + 64, pb:pb + 64]),
                        reads=[B["BK"], self.b_const], writes=[bps[6 + par]])
                for par in range(2):
                    P.op("act", lambda e, par=par, c=c: e.activation(
                        out=BKT[:, c, par * 8:(par + 1) * 8, :],
                        in_=psbf(6 + par)[:, 0:512].rearrange("p (k n) -> p k n", k=8), func=AF.Copy),
                        reads=[bps[6 + par]], writes=[B["BKT"]])
                for kc in range(KC):
                    P.op("pe", lambda e, kc=kc, c=c: e.transpose(
                        psbf(4)[64:128, kc * 128:(kc + 1) * 128], Vl[:, kc, c * 64:(c + 1) * 64], identb),
                        reads=[B["vin"], self.b_const], writes=[bps[4]])
                P.op("dve", lambda e, c=c: e.tensor_copy(
                    out=ZT[64:128, c, :, :].rearrange("p (par k) n -> p k par n", par=2),
                    in_=psbf(4)[64:128, :].rearrange("p (k par n) -> p k par n", k=8, par=2)),
                    reads=[bps[4]], writes=[B["ZTv"]])
            for c in range(2):
                hd = lambda hidx: (hidx // 8, hidx % 8, (hidx // 8) * 64)
                for hidx in range(16):
                    par, kc, pb = hd(hidx)
                    P.op("pe", lambda e, hidx=hidx, kc=kc, pb=pb: e.matmul(
                        PAv[:, hidx, :], lhsT=BK[pb:pb + 64, kc, c, :], rhs=AR[pb:pb + 64, kc, c, :],
                        start=True, stop=True), reads=[B["BK"], B["AR"]], writes=[bps[hidx // 4]])
                for hidx in range(16):
                    par, kc, pb = hd(hidx)
                    P.op("pe", lambda e, par=par, kc=kc, pb=pb: e.matmul(
                        ps[0:64, 4 + par, kc * 64:(kc + 1) * 64], lhsT=AR[pb:pb + 64, kc, c, 0:64],
                        rhs=BK[pb:pb + 64, kc, c, 0:64], start=True, stop=True),
                        reads=[B["BK"], B["AR"]], writes=[bps[4 + par]])
                P.op("dve", lambda e: e.tensor_tensor(out=AMs[:, :, :], in0=PAv, in1=mAb, op=ALU.mult),
                     reads=[bps[0], bps[1], bps[2], bps[3], self.b_const], writes=[B["AMs"]])
                P.op("dve", lambda e: e.tensor_tensor(out=ATs[0:64, :, :], in0=v64(4), in1=mTb, op=ALU.mult),
                     reads=[bps[4], bps[5], self.b_const], writes=[B["ATs"]])
                P.op("pool", lambda e: e.tensor_tensor(out=Tm[0:64, :, :], in0=AMs[0:64, :, 0:64], in1=idb, op=ALU.add),
                     reads=[B["AMs"], self.b_const], writes=[B["Tm"]])
                pA = lambda hidx: AMs[0:64, hidx, 0:64]
                pAT = lambda hidx: ATs[0:64, hidx, :]
                bA, bAT = B["AMs"], B["ATs"]
                for jj in range(1, 6):
                    a = jj % 2
                    if jj < 5:
                        for hidx in range(16):
                            P.op("pe", lambda e, hidx=hidx, pA=pA, pAT=pAT: e.matmul(
                                v64(0)[:, hidx, :], lhsT=pAT(hidx), rhs=pA(hidx), start=True, stop=True),
                                reads=[bA, bAT], writes=[bps[hidx // 8]])
                    for hidx in range(16):
                        P.op("pe", lambda e, hidx=hidx, pA=pA, pAT=pAT: e.matmul(
                            v64(2)[:, hidx, :], lhsT=pA(hidx), rhs=pAT(hidx), start=True, stop=True),
                            reads=[bA, bAT], writes=[bps[2 + hidx // 8]])
                    if jj < 5:
                        P.op("act", lambda e, a=a: e.activation(out=Apow[a][0][0:64, :, :], in_=v64(0), func=AF.Copy),
                             reads=[bps[0], bps[1]], writes=[bAp[a][0]])
                    P.op("dve", lambda e, a=a: e.tensor_copy(out=Apow[a][1][0:64, :, :], in_=v64(2)),
                         reads=[bps[2], bps[3]], writes=[bAp[a][1]])
                    for hidx in range(16):
                        P.op("pe", lambda e, hidx=hidx, a=a: e.matmul(
                            v64(4)[:, hidx, :], lhsT=Apow[a][1][0:64, hidx, :], rhs=Tm[0:64, hidx, :],
                            start=True, stop=True), reads=[bAp[a][1], B["Tm"]], writes=[bps[4 + hidx // 8]])
                    P.op("dve", lambda e: e.tensor_tensor(out=Tm[0:64, :, :], in0=v64(4), in1=Tm[0:64, :, :], op=ALU.add),
                         reads=[bps[4], bps[5]], writes=[B["Tm"]])
                    pA = (lambda a: (lambda hidx: Apow[a][0][0:64, hidx, :]))(a)
                    pAT = (lambda a: (lambda hidx: Apow[a][1][0:64, hidx, :]))(a)
                    bA, bAT = bAp[a][0], bAp[a][1]
                for hidx in range(16):
                    par, kc, pb = hd(hidx)
                    P.op("pe", lambda e, hidx=hidx, kc=kc, pb=pb: e.matmul(
                        v64(6)[:, hidx, :], lhsT=AR[pb:pb + 64, kc, c, 0:64], rhs=SSb[pb:pb + 64, kc, :],
                        start=True, stop=True), reads=[B["AR"], B["SSb"]], writes=[bps[6 + hidx // 8]])
                for hidx in range(16):
                    P.op("pe", lambda e, hidx=hidx: e.matmul(
                        v64(0)[:, hidx, :], lhsT=AMs[64:128, hidx, 0:64], rhs=ZT[64:128, c, hidx, :],
                        start=True, stop=True), reads=[B["AMs"], B["ZTv"]], writes=[bps[hidx // 8]])
                P.op("act", lambda e: e.activation(out=WTs[0:64, :, :], in_=v64(6), func=AF.Copy),
                     reads=[bps[6], bps[7]], writes=[B["WTs"]])
                P.op("dve", lambda e: e.tensor_tensor(out=WTs[0:64, :, :], in0=v64(0), in1=WTs[0:64, :, :], op=ALU.add),
                     reads=[bps[0], bps[1]], writes=[B["WTs"]])
                for hidx in range(16):
                    P.op("pe", lambda e, hidx=hidx: e.matmul(
                        v64(2)[:, hidx, :], lhsT=Tm[0:64, hidx, :], rhs=WTs[0:64, hidx, :], start=True, stop=True),
                        reads=[B["Tm"], B["WTs"]], writes=[bps[2 + hidx // 8]])
                P.op("act", lambda e: e.activation(out=ZT[0:64, c, :, :], in_=v64(2), func=AF.Copy),
                     reads=[bps[2], bps[3]], writes=[B["ZTu"]])
                for hidx in range(16):
                    par, kc, pb = hd(hidx)
                    P.op("pe", lambda e, par=par, kc=kc, pb=pb: e.matmul(
                        ps[pb:pb + 64, 6 + par, kc * 64:(kc + 1) * 64], lhsT=SSb[pb:pb + 64, kc, :],
                        rhs=AR[pb:pb + 64, kc, c, 64:128], start=True, stop=True),
                        reads=[B["AR"], B["SSb"]], writes=[bps[6 + par]])
                for hidx in range(16):
                    par, kc, pb = hd(hidx)
                    P.op("pe", lambda e, hidx=hidx, kc=kc, pb=pb: e.matmul(
                        ps[pb:pb + 64, 4, kc * 64:(kc + 1) * 64], lhsT=ZT[:, c, hidx, :], rhs=AMs[:, hidx, 64:128],
                        start=True, stop=True), reads=[B["ZTu"], B["ZTv"], B["AMs"]], writes=[bps[4]])
                ysl = slice(c * 64, (c + 1) * 64)
                k8 = lambda a: a.rearrange("p (k n) -> p k n", k=8)
                P.op("act", lambda e: e.activation(out=Yf[:, :, ysl], in_=k8(ps[:, 4, :]), func=AF.Copy),
                     reads=[bps[4]], writes=[B["Yf"]])
                P.op("dve", lambda e: e.tensor_tensor(out=Yf[0:64, :, ysl], in0=k8(ps[0:64, 6, :]), in1=Yf[0:64, :, ysl],
                                                      op=ALU.add), reads=[bps[6]], writes=[B["Yf"]])
                P.op("dve", lambda e: e.tensor_tensor(out=Yf[64:128, :, ysl], in0=k8(ps[64:128, 7, :]),
                                                      in1=Yf[64:128, :, ysl], op=ALU.add),
                     reads=[bps[7]], writes=[B["Yf"]])
                for hidx in range(16):
                    par, kc, pb = hd(hidx)
                    P.op("pe", lambda e, hidx=hidx, kc=kc, pb=pb: e.matmul(
                        ps[pb:pb + 64, 5, kc * 64:(kc + 1) * 64], lhsT=BKT[:, c, hidx, :], rhs=ZT[:, c, hidx, :],
                        start=True, stop=True), reads=[B["BKT"], B["ZTu"], B["ZTv"]], writes=[bps[5]])
                P.op("dve", lambda e: e.tensor_tensor(out=SS[:, :, :], in0=k8(ps[:, 5, :]), in1=SS[:, :, :], op=ALU.add),
                     reads=[bps[5]], writes=[B["SS"]])
                P.op("dve", lambda e, c=c: e.tensor_tensor(out=SS[:, :, :], in0=SS[:, :, :],
                                                           in1=PLt[:, :, c:c + 1].broadcast_to([128, 8, 64]), op=ALU.mult),
                     reads=[B["PL"]], writes=[B["SS"]])
                P.op("act", lambda e: e.activation(out=SSb[:, :, :], in_=SS[:, :, :], func=AF.Copy),
                     reads=[B["SS"]], writes=[B["SSb"]])
            P.dma(Gl[:, :, :], self.scr["G"][:, :, tk].rearrange("k p t -> p k t"),
                  reads=[self.b_scr["G"][tq]], writes=[B["Gl"]])
            f2 = lambda a, h: a[:, 4 * h:4 * h + 4, :]
            P.op("act", lambda e: e.activation(out=Yb[:, :, :], in_=Yf[:, :, :], func=AF.Copy),
                 reads=[B["Yf"]], writes=[B["Yb"]])
            for h2 in range(2):
                P.op("pe", lambda e, h2=h2: e.matmul(ps[:, 6 + h2, :], lhsT=self.cst("blk64"), rhs=f2(Yb, h2),
                                                     start=True, stop=True),
                     reads=[B["Yb"], self.b_const], writes=[bps[6 + h2]])
            P.op("dve", lambda e: e.tensor_tensor(out=G1[:, :, :], in0=Yf[:, :, :],
                                                  in1=ps[:, 6:8, :].rearrange("p b (k n) -> p (b k) n", n=128),
                                                  op=ALU.subtract),
                 reads=[B["Yf"], bps[6], bps[7]], writes=[B["G1"]])
            P.op("act", lambda e: e.activation(out=Yb[:, :, :], in_=G1[:, :, :], func=AF.Square),
                 reads=[B["G1"]], writes=[B["Yb"]])
            for h2 in range(2):
                P.op("pe", lambda e, h2=h2: e.matmul(ps[:, 6 + h2, :], lhsT=self.cst("blk64"), rhs=f2(Yb, h2),
                                                     start=True, stop=True),
                     reads=[B["Yb"], self.b_const], writes=[bps[6 + h2]])
            P.op("act", lambda e: e.activation(out=G2[:, :, :],
                                               in_=ps[:, 6:8, :].rearrange("p b (k n) -> p (b k) n", n=128),
                                               func=AF.Sqrt, bias=64e-5, scale=1.0),
                 reads=[bps[6], bps[7]], writes=[B["G2"]])
            P.op("dve", lambda e: e.reciprocal(out=G2[:, :, :], in_=G2[:, :, :]), reads=[], writes=[B["G2"]])
            P.op("dve", lambda e: e.tensor_tensor(out=G1[:, :, :], in0=G1[:, :, :], in1=G2[:, :, :], op=ALU.mult),
                 reads=[B["G2"]], writes=[B["G1"]])
            for kc in range(KC):
                P.op("pool", lambda e, kc=kc: e.tensor_scalar(out=G1[:, kc, :], in0=G1[:, kc, :],
                                                              scalar1=vc("ln_w", kc), scalar2=vc("ln_b", kc),
                                                              op0=ALU.mult, op1=ALU.add),
                     reads=[self.b_vecs], writes=[B["G1"]])
            P.op("pool", lambda e: e.tensor_tensor(out=G1[:, :, :], in0=G1[:, :, :], in1=bonus[:, :, :], op=ALU.add),
                 reads=[B["bonus"]], writes=[B["G1"]])
            P.op("dve", lambda e: e.tensor_tensor(out=self.hb[:, :, tk], in0=G1[:, :, :], in1=Gl[:, :, :], op=ALU.mult),
                 reads=[B["G1"], B["Gl"]], writes=[self.b_hb[tq]])
        self.handover(list(B.values()), allold)


    def emit_mamba(self, l, s):
        P = self.P
        ps, bps = self.psum, self.b_ps
        uflat = self.u[:].rearrange("p a b -> p (a b)")
        allu = [b for row in self.b_u for b in row]
        preb = [uflat[:, i * 2052:(i + 1) * 2052] for i in range(2)]
        dg = [uflat[:, 4104 + i * 512:4616 + i * 512].rearrange("p (t n) -> p t n", t=4) for i in range(2)]
        ob = [uflat[:, 8200 + i * 2048:10248 + i * 2048] for i in range(2)]
        wdt_sb = uflat[:, 12296:12552]
        b_pre, b_dg, b_wdt = [Buf("pre0"), Buf("pre1")], [Buf("dg0"), Buf("dg1")], Buf("wdt")
        b_ob = [Buf("ob0"), Buf("ob1")]
        news = b_pre + b_dg + [b_wdt] + b_ob
        self.handover(allu, news)
        for i in range(2):
            P.op("pool", lambda e, i=i: e.memset(preb[i][:, 0:3], 0.0), writes=[b_pre[i]])
        tiles = [(slice(t * 512, (t + 1) * 512), 512) for t in range(4)]
        obi = [0]

        def cons_z(fc, ti, tok, bank):
            if ti == 0:
                obi[0] ^= 1
            i = obi[0]
            P.op("act", lambda e: e.activation(out=ob[i][:, tok], in_=ps[:, bank, :], func=AF.Silu),
                 reads=[bps[bank]], writes=[b_ob[i]])
            if ti == 3:
                P.dma(self.scr["MZ"][fc], ob[i][:], reads=[b_ob[i]], writes=[self.b_scr["MZ"][fc]], e="act")

        def cons_conv(fc, ti, tok, bank):
            ch = fc
            pi = fc % 2
            P.op("act", lambda e: e.activation(out=preb[pi][:, 3 + ti * 512:3 + (ti + 1) * 512], in_=ps[:, bank, :],
                                               func=AF.Copy), reads=[bps[bank]], writes=[b_pre[pi]])
            if ti == 0:
                for tap in range(4):
                    P.op("pool", lambda e, tap=tap: e.tensor_scalar(
                        out=dg[pi][:, tap, :], in0=self.cst("ident"), scalar1=self.vcol(("cw", tap), ch), scalar2=None,
                        op0=ALU.mult), reads=[self.b_const, self.b_vecs], writes=[b_dg[pi]])
            if ti == 3:
                obi[0] ^= 1
                i = obi[0]
                for t2 in range(4):
                    cb_ = 4 + (t2 % 2)
                    for tap in range(4):
                        P.op("pe", lambda e, tap=tap, t2=t2, cb_=cb_: e.matmul(
                            ps[:, cb_, :], lhsT=dg[pi][:, tap, :], rhs=preb[pi][:, tap + t2 * 512:tap + t2 * 512 + 512],
                            start=(tap == 0), stop=(tap == 3)), reads=[b_dg[pi], b_pre[pi]], writes=[bps[cb_]])
                    P.op("act", lambda e, t2=t2, cb_=cb_: e.activation(
                        out=ob[i][:, t2 * 512:(t2 + 1) * 512], in_=ps[:, cb_, :], func=AF.Silu,
                        bias=self.vcol(("cbias",), ch), scale=1.0),
                        reads=[bps[cb_], self.b_vecs], writes=[b_ob[i]])
                nm, idx = ("MX", ch) if ch < 16 else (("MB", ch - 16) if ch < 24 else ("MC", ch - 24))
                P.dma(self.scr[nm][idx], ob[i][:], reads=[b_ob[i]], writes=[self.b_scr[nm][idx]], e="act")

        rhs = lambda kc, tok: self.hb[:, kc, tok]
        rb = lambda ti: [self.b_hb[ti]]
        self.proj(self.dw["mb_win"][0:16], 16, 8, rhs, rb, tiles, cons_z)
        self.proj(self.dw["mb_win"][16:48], 32, 8, rhs, rb, tiles, cons_conv)
        P.dma(self.st32[0][:, 0:256], self.dw["mb_wdt"][0], writes=[self.b_st32[0]])
        P.op("pool", lambda e: e.tensor_copy(out=wdt_sb, in_=self.st32[0][:, 0:256]),
             reads=[self.b_st32[0]], writes=[b_wdt])
        for c in range(16):
            for kc in range(KC):
                P.op("pe", lambda e, c=c, kc=kc: e.matmul(
                    ps[:, 4, c * 32:(c + 1) * 32], lhsT=self.hb[:, kc, c * 128:(c + 1) * 128],
                    rhs=wdt_sb[:, kc * 32:(kc + 1) * 32], start=(kc == 0), stop=(kc == KC - 1)),
                    reads=[b_wdt, self.b_hb[c // 4]], writes=[bps[4]])
        dtk, dA, ea, dend, etot = self.mbdt, self.tmpf[0], self.tmpf[1], self.tmpf[2], self.tmpf[3]
        bdt = [self.b_mbdt] + self.b_tmpf
        c3 = lambda a: a[:, :].rearrange("p (c h) -> p c h", c=16)
        bc16 = lambda key: self.vcol((key,), 0, 32).unsqueeze(1).broadcast_to([128, 16, 32])
        P.op("dve", lambda e: e.tensor_tensor(out=c3(dtk), in0=c3(ps[:, 4, :]), in1=bc16("dtb"), op=ALU.add),
             reads=[bps[4], self.b_vecs], writes=[bdt[0]])
        P.op("act", lambda e: e.activation(out=dtk[:, :], in_=dtk[:, :], func=AF.Exp), reads=[], writes=[bdt[0]])
        P.op("act", lambda e: e.activation(out=dtk[:, :], in_=dtk[:, :], func=AF.Ln, bias=1.0, scale=1.0),
             reads=[], writes=[bdt[0]])
        P.op("act", lambda e: e.activation(out=ea[:, 0:32], in_=self.vcol(("alog",), 0, 32), func=AF.Exp),
             reads=[self.b_vecs], writes=[bdt[2]])
        P.op("dve", lambda e: e.scalar_tensor_tensor(
            out=c3(dA), in0=c3(dtk), scalar=-1.0, op0=ALU.mult,
            in1=ea[:, 0:32].unsqueeze(1).broadcast_to([128, 16, 32]), op1=ALU.mult),
            reads=[bdt[0], bdt[2]], writes=[bdt[1]])
        P.op("pe", lambda e: e.matmul(ps[:, 5, :], lhsT=self.cst("mcur", bf=False), rhs=dA[:, :], start=True, stop=True),
             reads=[bdt[1], self.b_const], writes=[bps[5]])
        P.op("pe", lambda e: e.matmul(ps[:, 6, :], lhsT=self.cst("onesf", bf=False), rhs=dA[:, :], start=True, stop=True),
             reads=[bdt[1], self.b_const], writes=[bps[6]])
        P.op("act", lambda e: e.activation(out=ea[:, :], in_=ps[:, 5, :], func=AF.Exp), reads=[bps[5]], writes=[bdt[2]])
        P.op("act", lambda e: e.activation(out=etot[:, :], in_=ps[:, 6, :], func=AF.Exp), reads=[bps[6]], writes=[bdt[4]])
        P.op("act", lambda e: e.activation(out=dend[:, :], in_=ps[:, 5, :], func=AF.Copy), reads=[bps[5]], writes=[bdt[3]])
        P.op("dve", lambda e: e.tensor_tensor(out=dend[:, :], in0=ps[:, 6, :], in1=dend[:, :], op=ALU.subtract),
             reads=[bps[6]], writes=[bdt[3]])
        P.op("act", lambda e: e.activation(out=dend[:, :], in_=dend[:, :], func=AF.Exp), reads=[], writes=[bdt[3]])

        import os
        if os.environ.get("MB_STOP") == "1":
            return
        hflat = self.hb[:].rearrange("p a b -> p (a b)")
        Sst = hflat[:, 0:4096].bitcast(F32).rearrange("p (g n) -> p g n", g=8)
        Sb = hflat[:, 4096:6144].rearrange("p (g n) -> p g n", g=8)
        yT = hflat[:, 6144:14336].rearrange("p (c t) -> p c t", c=16)
        xsT = uflat[:, 0:2048].rearrange("p (c t) -> p c t", c=16)
        CT = uflat[:, 2048:3072].rearrange("p (g t) -> p g t", g=8)
        BT = uflat[:, 3072:4096].rearrange("p (g t) -> p g t", g=8)
        xtok = uflat[:, 4096:6144]
        Btok = uflat[:, 6144:7168].rearrange("p (g n) -> p g n", g=8)
        X = uflat[:, 7168:9216]
        Xd = uflat[:, 9216:11264]
        Yb = uflat[:, 11264:13312]
        CBm = uflat[:, 13312:14336].rearrange("p (g t) -> p g t", g=8)
        lh = [uflat[:, 14336 + i * 256:14592 + i * 256].bitcast(F32) for i in range(4)]
        Mh = [uflat[:, 15360:15488], uflat[:, 15488:15616], uflat[:, 21760:21888], uflat[:, 21888:22016]]
        Dm4 = uflat[:, 22016:22528]
        Yg = uflat[:, 15616:16128].bitcast(F32)
        gz = [uflat[:, 16128 + i * 512:16640 + i * 512] for i in range(2)]
        gx = [uflat[:, 17152 + i * 512:17664 + i * 512] for i in range(2)]
        gt = [uflat[:, 18176 + i * 1024:19200 + i * 1024].bitcast(F32) for i in range(2)]
        gsq = uflat[:, 20224:20736]
        grs = uflat[:, 20736:21760].bitcast(F32)
        nm2 = ["S", "Sb", "yT", "xsT", "CT", "BT", "xtok", "Btok", "X", "Xd", "Yb", "CBm", "Mh0", "Mh1", "Mh2", "Mh3", "Dm4",
               "lh0", "lh1", "lh2", "lh3", "Yg", "gz0", "gz1", "gx0", "gx1", "gt0", "gt1", "gsq", "grs"]
        B = {n: Buf(n) for n in nm2}
        self.handover(news + self.b_hb, list(B.values()))
        identb = self.cst("ident")
        psbf = lambda bank: ps[:, bank, :].bitcast(BF16)
        P.op("pool", lambda e: e.memset(Sst[:, :, :], 0.0), writes=[B["S"]])
        P.op("pool", lambda e: e.memset(Sb[:, :, :], 0.0), writes=[B["Sb"]])
        i0 = self.co_idx(l, 1, s)
        h64 = lambda a: a.rearrange("p (h n) -> p h n", n=64)

        for c in range(16):
            tk = slice(c * 128, (c + 1) * 128)
            P.dma(xsT[:, :, :], self.scr["MX"][:, :, tk].rearrange("k p t -> p k t"),
                  reads=self.b_scr["MX"], writes=[B["xsT"]])
            P.dma(CT[:, :, :], self.scr["MC"][:, :, tk].rearrange("k p t -> p k t"),
                  reads=self.b_scr["MC"], writes=[B["CT"]])
            P.dma(BT[:, :, :], self.scr["MB"][:, :, tk].rearrange("k p t -> p k t"),
                  reads=self.b_scr["MB"], writes=[B["BT"]])
            for ch in range(16):
                P.op("pe", lambda e, ch=ch: e.transpose(psbf(ch // 8)[:, (ch % 8) * 128:(ch % 8 + 1) * 128],
                                                        xsT[:, ch, :], identb),
                     reads=[B["xsT"], self.b_const], writes=[bps[ch // 8]])
            for hf in range(2):
                P.op("act" if hf else "dve", (lambda e, hf=hf: e.activation(out=xtok[:, hf * 1024:(hf + 1) * 1024],
                     in_=psbf(hf)[:, :], func=AF.Copy)) if hf else
                     (lambda e, hf=hf: e.tensor_copy(out=xtok[:, hf * 1024:(hf + 1) * 1024], in_=psbf(hf)[:, :])),
                     reads=[bps[hf]], writes=[B["xtok"]])
            for g in range(8):
                P.op("pe", lambda e, g=g: e.transpose(psbf(2)[:, g * 128:(g + 1) * 128], BT[:, g, :], identb),
                     reads=[B["BT"], self.b_const], writes=[bps[2]])
            P.op("act", lambda e: e.activation(out=Btok[:, :, :], in_=psbf(2)[:, :].rearrange("p (g n) -> p g n", g=8),
                                               func=AF.Copy), reads=[bps[2]], writes=[B["Btok"]])
            P.op("dve", lambda e, c=c: e.tensor_tensor(
                out=h64(X), in0=h64(xtok), in1=dtk[:, c * 32:(c + 1) * 32].unsqueeze(2).broadcast_to([128, 32, 64]),
                op=ALU.mult), reads=[B["xtok"], bdt[0]], writes=[B["X"]])
            P.op("pool", lambda e, c=c: e.tensor_tensor(
                out=h64(Xd), in0=h64(X), in1=dend[:, c * 32:(c + 1) * 32].unsqueeze(2).broadcast_to([128, 32, 64]),
                op=ALU.mult), reads=[B["X"], bdt[3]], writes=[B["Xd"]])
            for g in range(8):
                P.op("pe", lambda e, g=g: e.matmul(ps[:, 6 + g // 4, (g % 4) * 128:(g % 4 + 1) * 128], lhsT=BT[:, g, :],
                                                   rhs=CT[:, g, :], start=True, stop=True),
                     reads=[B["BT"], B["CT"]], writes=[bps[6 + g // 4]])
            P.op("dve", lambda e: e.tensor_tensor(
                out=CBm[:, :, :], in0=ps[:, 6:8, :].rearrange("p b (g t) -> p (b g) t", t=128),
                in1=self.cst("mcur").unsqueeze(1).broadcast_to([128, 8, 128]), op=ALU.mult),
                reads=[bps[6], bps[7], self.b_const], writes=[B["CBm"]])
            for g in range(8):
                yb_ = 3 + (g % 2)
                for hh in range(4):
                    h = g * 4 + hh
                    P.op("act", lambda e, hh=hh, h=h, c=c: e.activation(
                        out=lh[hh], in_=self.cst("mprev", bf=False), func=AF.Copy,
                        scale=dA[:, c * 32 + h:c * 32 + h + 1]),
                        reads=[bdt[1], self.b_const], writes=[B[f"lh{hh}"]])
                for hh in range(4):
                    P.op("pe", lambda e, hh=hh: e.matmul(ps[:, 0, hh * 128:(hh + 1) * 128], lhsT=lh[hh],
                                                         rhs=self.cst("mcur", bf=False), start=True, stop=True),
                         reads=[B[f"lh{hh}"], self.b_const], writes=[bps[0]])
                P.op("act", lambda e: e.activation(out=Dm4, in_=ps[:, 0, :], func=AF.Exp),
                     reads=[bps[0]], writes=[B["Dm4"]])
                for hh in range(4):
                    P.op("dve", lambda e, hh=hh, g=g: e.tensor_tensor(out=Mh[hh], in0=CBm[:, g, :],
                                                                      in1=Dm4[:, hh * 128:(hh + 1) * 128], op=ALU.mult),
                         reads=[B["CBm"], B["Dm4"]], writes=[B[f"Mh{hh}"]])
                for hh in range(4):
                    h = g * 4 + hh
                    P.op("pe", lambda e, hh=hh, h=h, yb_=yb_: e.matmul(
                        ps[:, yb_, hh * 64:(hh + 1) * 64], lhsT=Mh[hh], rhs=X[:, h * 64:(h + 1) * 64],
                        start=True, stop=True), reads=[B[f"Mh{hh}"], B["X"]], writes=[bps[yb_]])
                P.op("pe", lambda e, g=g: e.matmul(ps[:, 5, 0:256], lhsT=CT[:, g, :], rhs=Sb[:, g, :], start=True, stop=True),
                     reads=[B["CT"], B["Sb"]], writes=[bps[5]])
                P.op("dve", lambda e, g=g, c=c: e.tensor_tensor(
                    out=h64(Yg), in0=h64(ps[:, 5, 0:256]),
                    in1=ea[:, c * 32 + g * 4:c * 32 + g * 4 + 4].unsqueeze(2).broadcast_to([128, 4, 64]), op=ALU.mult),
                    reads=[bps[5], bdt[2]], writes=[B["Yg"]])
                P.op("dve", lambda e, g=g, yb_=yb_: e.tensor_tensor(out=Yb[:, g * 256:(g + 1) * 256], in0=ps[:, yb_, 0:256],
                                                                    in1=Yg, op=ALU.add),
                     reads=[bps[yb_], B["Yg"]], writes=[B["Yb"]])
                P.op("pe", lambda e, g=g: e.matmul(ps[:, 2, 0:256], lhsT=Btok[:, g, :], rhs=Xd[:, g * 256:(g + 1) * 256],
                                                   start=True, stop=True),
                     reads=[B["Btok"], B["Xd"]], writes=[bps[2]])
                P.op("pool", lambda e, g=g, c=c: e.tensor_tensor(
                    out=h64(Sst[:, g, :]), in0=h64(Sst[:, g, :]),
                    in1=etot[:, c * 32 + g * 4:c * 32 + g * 4 + 4].unsqueeze(2).broadcast_to([128, 4, 64]), op=ALU.mult),
                    reads=[bdt[4], B["Sb"]], writes=[B["S"]])
                P.op("dve", lambda e, g=g: e.tensor_tensor(out=Sst[:, g, :], in0=ps[:, 2, 0:256], in1=Sst[:, g, :], op=ALU.add),
                     reads=[bps[2]], writes=[B["S"]])
                P.op("act", lambda e, g=g: e.activation(out=Sb[:, g, :], in_=Sst[:, g, :], func=AF.Copy),
                     reads=[B["S"], bps[5]], writes=[B["Sb"]])
            cl = c % 4
            for ch in range(16):
                P.op("pe", lambda e, ch=ch: e.transpose(psbf(ch // 8)[:, (ch % 8) * 128:(ch % 8 + 1) * 128],
                                                        Yb[:, ch * 128:(ch + 1) * 128], identb),
                     reads=[B["Yb"], self.b_const], writes=[bps[ch // 8]])
            for hf in range(2):
                P.op("act", lambda e, hf=hf, cl=cl: e.activation(
                    out=yT[:, hf * 8:(hf + 1) * 8, cl * 128:(cl + 1) * 128],
                    in_=psbf(hf)[:, :].rearrange("p (k n) -> p k n", k=8), func=AF.Copy),
                    reads=[bps[hf]], writes=[B["yT"]])
            if cl != 3 or os.environ.get("MB_STOP") == "2":
                continue
            t = c // 4
            tok = slice(t * 512, (t + 1) * 512)
            for g2 in range(8):
                for q in range(2):
                    ch = g2 * 2 + q
                    P.dma(gz[q][:], self.scr["MZ"][ch, :, tok], reads=[self.b_scr["MZ"][ch]], writes=[B[f"gz{q}"]])
                    P.dma(gx[q][:], self.scr["MX"][ch, :, tok], reads=[self.b_scr["MX"][ch]], writes=[B[f"gx{q}"]])
                    P.op("dve", lambda e, q=q, ch=ch: e.scalar_tensor_tensor(
                        out=gt[q], in0=gx[q][:], scalar=self.vcol(("mbD",), ch), op0=ALU.mult, in1=yT[:, ch, :], op1=ALU.add),
                        reads=[B[f"gx{q}"], B["yT"], self.b_vecs], writes=[B[f"gt{q}"]])
                    P.op("pool", lambda e, q=q: e.tensor_tensor(out=gt[q], in0=gt[q], in1=gz[q][:], op=ALU.mult),
                         reads=[B[f"gz{q}"]], writes=[B[f"gt{q}"]])
                    P.op("act", lambda e, q=q: e.activation(out=gsq, in_=gt[q], func=AF.Square),
                         reads=[B[f"gt{q}"]], writes=[B["gsq"]])
                    P.op("pe", lambda e, q=q: e.matmul(ps[:, 3, :], lhsT=self.ones_b[:], rhs=gsq, start=(q == 0), stop=(q == 1)),
                         reads=[B["gsq"], self.b_const], writes=[bps[3]])
                P.op("act", lambda e: e.activation(out=grs, in_=ps[:, 3, :], func=AF.Sqrt, bias=EPS, scale=4.0),
                     reads=[bps[3]], writes=[B["grs"]])
                P.op("dve", lambda e: e.reciprocal(out=grs, in_=grs), reads=[], writes=[B["grs"]])
                for q in range(2):
                    ch = g2 * 2 + q
                    P.op("dve", lambda e, q=q, ch=ch: e.scalar_tensor_tensor(
                        out=yT[:, ch, :], in0=gt[q], scalar=self.vcol(("mbng",), ch), op0=ALU.mult, in1=grs, op1=ALU.mult),
                        reads=[B[f"gt{q}"], B["grs"], self.b_vecs], writes=[B["yT"]])
            for fc in range(8):
                w, bw = self.load_w(self.dw["mb_wo"][fc], 2048)
                bank = 4 + (fc % 2)
                for ch in range(16):
                    P.op("pe", lambda e, ch=ch, w=w, bank=bank: e.matmul(
                        ps[:, bank, :], lhsT=w[:, ch * 128:(ch + 1) * 128], rhs=yT[:, ch, :],
                        start=(ch == 0), stop=(ch == 15)), reads=[bw, B["yT"]], writes=[bps[bank]])
                P.op("dve", lambda e, fc=fc, bank=bank: e.scalar_tensor_tensor(
                    out=self.xT[:, fc, tok], in0=ps[:, bank, :], scalar=self.gco[:, i0 + fc:i0 + fc + 1], op0=ALU.mult,
                    in1=self.xT[:, fc, tok], op1=ALU.add),
                    reads=[bps[bank], self.b_mod, self.b_xT[fc][t]], writes=[self.b_xT[fc][t]])
        self.handover(list(B.values()), allu + self.b_hb)


def make_in_maps(inp, ncores, nseq):
    H = build_host_arrays(inp)
    vecs = H["vecs"].pack()
    consts, coff = build_consts()
    x = np.asarray(inp["x"], np.float32)
    c = np.asarray(inp["c"], np.float32)
    maps = []
    for core in range(ncores):
        xs = x[core * nseq:(core + 1) * nseq]
        xT = np.ascontiguousarray(xs.transpose(0, 2, 1)).reshape(nseq, KC, 128, S)
        cs = c[core * nseq:(core + 1) * nseq]
        cT = np.ascontiguousarray(cs.reshape(nseq, KC, 128).transpose(2, 1, 0)).reshape(128, KC * nseq)
        m = {"xT": xT, "cT": cT, "vecs": vecs, "ada_w": H["ada_w"].reshape(DEPTH, 18, 128, 4096),
             "ffn_w_in": H["ffn_w_in"], "ffn_w_out": H["ffn_w_out"], "consts": consts}
        for k in W_SPECS:
            m[k] = H[k]
        maps.append(m)
    return maps, H["vecs"].off, vecs.shape[1], coff, consts.shape[1]


def run(inp, ncores=NCORES, nseq=NSEQ_CORE, n_layers=DEPTH, mixers=True, kinds=None, trace=False):
    maps, voff, nvec, coff, nconst = make_in_maps(inp, ncores, nseq)
    b = Builder(nseq, n_layers, voff, nvec, mixers=mixers, kinds=kinds, const_off=coff, nconst=nconst)
    nc = b.build()
    if trace:
        res = run_bass_kernel_spmd(nc, maps, core_ids=list(range(ncores)), trace=True)
        print('EXEC_TIME_NS', res.exec_time_ns)
    else:
        res = run_bass_kernel_spmd(nc, maps, core_ids=list(range(ncores)))
    outs = []
    for r in res.results:
        yT = r["yT"].reshape(nseq, D, S)
        outs.append(yT.transpose(0, 2, 1))
    return np.ascontiguousarray(np.concatenate(outs, axis=0))


def kernel(**inputs):
    return run(inputs)
```
